# Optimizing a Trainium2 kernel written in Bass

```python
import jax, jax.numpy as jnp
from jax import lax
import numpy as np


D_MODEL = 2048
BATCH = 4
SEQ = 4096
DEPTH = 2

GRID_W = 64
CTX_LEN = 256
N_HEADS = 8
N_KV_HEADS = 2
HEAD_DIM = 128
ATTN_W = N_HEADS * HEAD_DIM
KV_W = N_KV_HEADS * HEAD_DIM
Q_BLOCK = 128
ROPE_THETA = 10000.0
ROPE_PAIRS = HEAD_DIM // 4
D_RNN = 1024
RNN_BLOCKS = 8
RNN_BLOCK_W = D_RNN // RNN_BLOCKS
CONV_W = 4
RG_C = 8.0
AR_IN = ATTN_W + 2 * KV_W + 2 * D_RNN
AR_OUT = ATTN_W + D_RNN
D_GM = 2048
GM_GROUPS = 16
GM_GROUP_W = D_GM // GM_GROUPS
CHUNK = 128
D_FF = 4 * D_MODEL
EPS = 1e-6
N_EVEN = (DEPTH + 1) // 2
N_ODD = DEPTH // 2

kernel_name = 'hybrid_attn_rglru_chunkgmlp_diffusion'


def rms_norm(x, g):
    xf = x.astype(jnp.float32)
    y = xf * lax.rsqrt(jnp.mean(xf * xf, axis=-1, keepdims=True) + EPS)
    return (y * g.astype(jnp.float32)).astype(x.dtype)


def layer_norm(x, g, b):
    xf = x.astype(jnp.float32)
    mu = jnp.mean(xf, axis=-1, keepdims=True)
    xc = xf - mu
    y = xc * lax.rsqrt(jnp.mean(xc * xc, axis=-1, keepdims=True) + EPS)
    return (y * g.astype(jnp.float32) + b.astype(jnp.float32)).astype(x.dtype)


def modulate(h, shift, scale):
    return h * (1 + scale) + shift


def axial_angles(n):
    rows = n // GRID_W
    r_idx, c_idx = jnp.meshgrid(jnp.arange(rows), jnp.arange(GRID_W), indexing='ij')
    r_idx = r_idx.reshape(-1).astype(jnp.float32)
    c_idx = c_idx.reshape(-1).astype(jnp.float32)
    freqs = ROPE_THETA ** (-jnp.arange(ROPE_PAIRS, dtype=jnp.float32) / ROPE_PAIRS)
    return r_idx[:, None] * freqs, c_idx[:, None] * freqs


def rope_1d(x, ang):
    x1, x2 = jnp.split(x.astype(jnp.float32), 2, axis=-1)
    cos = jnp.cos(ang)[None, :, None, :]
    sin = jnp.sin(ang)[None, :, None, :]
    return jnp.concatenate([x1 * cos - x2 * sin, x2 * cos + x1 * sin], axis=-1)


def rope_2d(x, ang_row, ang_col):
    half = HEAD_DIM // 2
    out = jnp.concatenate([rope_1d(x[..., :half], ang_row), rope_1d(x[..., half:], ang_col)], axis=-1)
    return out.astype(x.dtype)


def gqa_attend(q, k, v):
    bsz, n = q.shape[0], q.shape[1]
    nb = n // Q_BLOCK
    groups = N_HEADS // N_KV_HEADS
    scale = HEAD_DIM ** -0.5
    qb = q.reshape(bsz, nb, Q_BLOCK, N_KV_HEADS, groups, HEAD_DIM).transpose(1, 0, 2, 3, 4, 5)

    def one_block(qi):
        s = jnp.einsum('bqkgd,btkd->bkgqt', qi, k).astype(jnp.float32) * scale
        p = jax.nn.softmax(s, axis=-1).astype(v.dtype)
        return jnp.einsum('bkgqt,btkd->bqkgd', p, v)

    o = lax.map(one_block, qb)
    return o.transpose(1, 0, 2, 3, 4, 5).reshape(bsz, n, N_HEADS * HEAD_DIM)


def centred_dwconv(x, w, b):
    n = x.shape[1]
    left = CONV_W // 2
    xp = jnp.pad(x, ((0, 0), (left, CONV_W - 1 - left), (0, 0)))
    y = b
    for j in range(CONV_W):
        y = y + xp[:, j:j + n] * w[j]
    return y


def block_diag(x, w, b):
    xb = x.reshape(x.shape[0], x.shape[1], RNN_BLOCKS, RNN_BLOCK_W)
    return jnp.einsum('bsnc,ncd->bsnd', xb, w).reshape(x.shape) + b


def rglru_direction(x, wa, ba, wx, bx, lam, h0, reverse):
    r = jax.nn.sigmoid(block_diag(x, wa, ba).astype(jnp.float32))
    i = jax.nn.sigmoid(block_diag(x, wx, bx).astype(jnp.float32))
    log_a = -RG_C * r * jax.nn.softplus(-lam.astype(jnp.float32))
    a = jnp.exp(log_a)
    b = jnp.sqrt(-jnp.expm1(2.0 * log_a)) * (i * x.astype(jnp.float32))

    def combine(lhs, rhs):
        return (lhs[0] * rhs[0], rhs[0] * lhs[1] + rhs[1])

    a_cum, h = lax.associative_scan(combine, (a, b), reverse=reverse, axis=1)
    h = h + a_cum * h0[:, None, :]
    final = h[:, 0] if reverse else h[:, -1]
    return h, final


def mix_attn_rglru(hl, hc, w_in, q_g, k_g, conv_w, conv_b, wa, ba, wx, bx, lam, w_out, need_ctx):
    s1 = ATTN_W
    s2 = s1 + KV_W
    s3 = s2 + KV_W
    s4 = s3 + D_RNN

    def project(h):
        bsz, n = h.shape[0], h.shape[1]
        q, k, v, xr, gr = jnp.split(h @ w_in, [s1, s2, s3, s4], axis=-1)
        q = rms_norm(q.reshape(bsz, n, N_HEADS, HEAD_DIM), q_g)
        k = rms_norm(k.reshape(bsz, n, N_KV_HEADS, HEAD_DIM), k_g)
        v = v.reshape(bsz, n, N_KV_HEADS, HEAD_DIM)
        xr = centred_dwconv(xr, conv_w, conv_b)
        return q, k, v, xr, gr

    ql, kl, vl, xl, gl = project(hl)
    qc, kc, vc, xc, gc = project(hc)

    ang_r, ang_c = axial_angles(hl.shape[1])
    ql = rope_2d(ql, ang_r, ang_c)
    kl = rope_2d(kl, ang_r, ang_c)
    k_all = jnp.concatenate([kc, kl], axis=1)
    v_all = jnp.concatenate([vc, vl], axis=1)
    attn_l = gqa_attend(ql, k_all, v_all)

    zeros = jnp.zeros((hc.shape[0], D_RNN), jnp.float32)
    hcf, fin_f = rglru_direction(xc, wa[0], ba[0], wx[0], bx[0], lam[0], zeros, False)
    hcb, fin_b = rglru_direction(xc, wa[1], ba[1], wx[1], bx[1], lam[1], zeros, True)
    hlf, _ = rglru_direction(xl, wa[0], ba[0], wx[0], bx[0], lam[0], fin_f, False)
    hlb, _ = rglru_direction(xl, wa[1], ba[1], wx[1], bx[1], lam[1], fin_b, True)
    rnn_l = ((hlf + hlb) * jax.nn.gelu(gl.astype(jnp.float32))).astype(hl.dtype)
    out_l = jnp.concatenate([attn_l, rnn_l], axis=-1) @ w_out

    out_c = None
    if need_ctx:
        attn_c = gqa_attend(qc, kc, vc)
        rnn_c = ((hcf + hcb) * jax.nn.gelu(gc.astype(jnp.float32))).astype(hc.dtype)
        out_c = jnp.concatenate([attn_c, rnn_c], axis=-1) @ w_out
    return out_l, out_c


def chunk_gmlp(h, w_in, b_in, v_g, v_b, w_sp, b_sp, w_out):
    bsz, n = h.shape[0], h.shape[1]
    z = jax.nn.gelu(h @ w_in + b_in)
    u, v = jnp.split(z, 2, axis=-1)
    v = layer_norm(v, v_g, v_b)
    v = v.reshape(bsz, n // CHUNK, CHUNK, GM_GROUPS, GM_GROUP_W)
    sv = jnp.einsum('gpq,bcqgd->bcpgd', w_sp, v) + b_sp.T[None, None, :, :, None]
    return (u * sv.reshape(bsz, n, D_GM)) @ w_out


def sq_relu_mlp(h, w1, w2):
    return jnp.square(jax.nn.relu(h @ w1)) @ w2


def setup_inputs(seed: int = 0) -> dict:
    key = jax.random.key(seed)
    ks = jax.random.split(key, 32)
    f32 = jnp.float32
    D = D_MODEL

    def nrm(k, shape, scale):
        return jax.random.normal(k, shape, f32) * scale

    lam_u = jax.random.uniform(ks[20], (N_EVEN, 2, D_RNN), f32, 0.9, 0.999)
    a0 = lam_u ** (1.0 / RG_C)
    return {
        'x': nrm(ks[0], (BATCH, SEQ, D), 1.0),
        'c': nrm(ks[1], (BATCH, D), 1.0),
        'ctx': nrm(ks[2], (BATCH, CTX_LEN, D), 1.0),
        'c_ctx': nrm(ks[3], (D,), 1.0),
        'w_mod': nrm(ks[4], (DEPTH, D, 6 * D), 0.5 * D ** -0.5),
        'b_mod': nrm(ks[5], (DEPTH, 6 * D), 0.02),
        'norm_g': 1.0 + nrm(ks[6], (DEPTH, 4, D), 0.02),
        'w_ff_in': nrm(ks[7], (DEPTH, D, D_FF), D ** -0.5),
        'w_ff_out': nrm(ks[8], (DEPTH, D_FF, D), D_FF ** -0.5),
        'ar_w_in': nrm(ks[9], (N_EVEN, D, AR_IN), D ** -0.5),
        'ar_q_g': 1.0 + nrm(ks[10], (N_EVEN, HEAD_DIM), 0.02),
        'ar_k_g': 1.0 + nrm(ks[11], (N_EVEN, HEAD_DIM), 0.02),
        'ar_conv_w': nrm(ks[12], (N_EVEN, CONV_W, D_RNN), CONV_W ** -0.5),
        'ar_conv_b': nrm(ks[13], (N_EVEN, D_RNN), 0.02),
        'ar_wa': nrm(ks[14], (N_EVEN, 2, RNN_BLOCKS, RNN_BLOCK_W, RNN_BLOCK_W), RNN_BLOCK_W ** -0.5),
        'ar_ba': nrm(ks[15], (N_EVEN, 2, D_RNN), 0.02),
        'ar_wx': nrm(ks[16], (N_EVEN, 2, RNN_BLOCKS, RNN_BLOCK_W, RNN_BLOCK_W), RNN_BLOCK_W ** -0.5),
        'ar_bx': nrm(ks[17], (N_EVEN, 2, D_RNN), 0.02),
        'ar_lambda': jnp.log(a0) - jnp.log1p(-a0),
        'ar_w_out': nrm(ks[18], (N_EVEN, AR_OUT, D), AR_OUT ** -0.5),
        'gm_w_in': nrm(ks[19], (N_ODD, D, 2 * D_GM), D ** -0.5),
        'gm_b_in': nrm(ks[21], (N_ODD, 2 * D_GM), 0.02),
        'gm_v_g': 1.0 + nrm(ks[22], (N_ODD, D_GM), 0.02),
        'gm_v_b': nrm(ks[23], (N_ODD, D_GM), 0.02),
        'gm_w_sp': nrm(ks[24], (N_ODD, GM_GROUPS, CHUNK, CHUNK), CHUNK ** -0.5),
        'gm_b_sp': 1.0 + nrm(ks[25], (N_ODD, GM_GROUPS, CHUNK), 0.02),
        'gm_w_out': nrm(ks[26], (N_ODD, D_GM, D), D_GM ** -0.5),
    }


def reference(x, c, ctx, c_ctx, w_mod, b_mod, norm_g, w_ff_in, w_ff_out, ar_w_in, ar_q_g, ar_k_g, ar_conv_w, ar_conv_b, ar_wa, ar_ba, ar_wx, ar_bx, ar_lambda, ar_w_out, gm_w_in, gm_b_in, gm_v_g, gm_v_b, gm_w_sp, gm_b_sp, gm_w_out):
    xl, xc = x, ctx
    s_c = jax.nn.silu(c)
    s_ctx = jax.nn.silu(c_ctx)
    for i in range(DEPTH):
        j = i // 2
        need_ctx = any(l % 2 == 0 for l in range(i + 1, DEPTH))
        g = norm_g[i]
        ml = jnp.split((s_c @ w_mod[i] + b_mod[i])[:, None, :], 6, axis=-1)
        mc = jnp.split((s_ctx @ w_mod[i] + b_mod[i])[None, None, :], 6, axis=-1)
        hl = modulate(rms_norm(xl, g[0]), ml[0], ml[1])
        oc = None
        if i % 2 == 0:
            hc = modulate(rms_norm(xc, g[0]), mc[0], mc[1])
            ol, oc = mix_attn_rglru(hl, hc, ar_w_in[j], ar_q_g[j], ar_k_g[j], ar_conv_w[j], ar_conv_b[j],
                                    ar_wa[j], ar_ba[j], ar_wx[j], ar_bx[j], ar_lambda[j], ar_w_out[j], need_ctx)
        else:
            ol = chunk_gmlp(hl, gm_w_in[j], gm_b_in[j], gm_v_g[j], gm_v_b[j], gm_w_sp[j], gm_b_sp[j], gm_w_out[j])
            if need_ctx:
                hc = modulate(rms_norm(xc, g[0]), mc[0], mc[1])
                oc = chunk_gmlp(hc, gm_w_in[j], gm_b_in[j], gm_v_g[j], gm_v_b[j], gm_w_sp[j], gm_b_sp[j], gm_w_out[j])
        xl = xl + ml[2] * rms_norm(ol, g[1])
        hl = modulate(rms_norm(xl, g[2]), ml[3], ml[4])
        xl = xl + ml[5] * rms_norm(sq_relu_mlp(hl, w_ff_in[i], w_ff_out[i]), g[3])
        if need_ctx:
            xc = xc + mc[2] * rms_norm(oc, g[1])
            hc = modulate(rms_norm(xc, g[2]), mc[3], mc[4])
            xc = xc + mc[5] * rms_norm(sq_relu_mlp(hc, w_ff_in[i], w_ff_out[i]), g[3])
    return xl
```

```python
import contextlib
import numpy as np
import concourse.bass as bass
import concourse.mybir as mybir
from concourse.bass_utils import run_bass_kernel_spmd

F32 = mybir.dt.float32
BF16 = mybir.dt.bfloat16
AF = mybir.ActivationFunctionType
ALU = mybir.AluOpType
AX = mybir.AxisListType

D = 2048
SEQ = 4096
HALF = 2048
CTX = 256
NTOK = SEQ + CTX
DFF = 8192
ARIN = 3584
EPS = 1e-6
XL = 4 + SEQ
XC = 4 + CTX


class Tile:
    __slots__ = ("name", "w", "r", "dsem", "dcnt")

    def __init__(self, name):
        self.name = name
        self.w = None
        self.r = []
        self.dsem = None
        self.dcnt = 0


class Buf:
    def __init__(self, t, name, dram=False):
        self.t = t
        self.T = Tile(name)
        self.dram = dram

    def __getitem__(self, k):
        return self.t[k]


class Sched:
    ENGS = ("pe", "act", "dve", "pool", "sp")

    def __init__(self, nc, stack):
        self.nc = nc
        self.stack = stack
        self.ops = {e: [] for e in self.ENGS}
        self.sem = {e: stack.enter_context(nc.semaphore("s_" + e)) for e in self.ENGS}
        self.cnt = {e: 0 for e in self.ENGS}
        self.waited = {e: {} for e in self.ENGS}
        self.dtiles = []
        self.free_dsems = []
        self.nsem = 0

    def _dsem(self, t):
        if t.dsem is None:
            if self.free_dsems:
                t.dsem, t.dcnt = self.free_dsems.pop()
            else:
                self.nsem += 1
                t.dsem = self.stack.enter_context(self.nc.semaphore("d%d" % self.nsem))
                t.dcnt = 0
            self.dtiles.append(t)
        return t.dsem

    def _waits(self, eng, deps):
        need = {}
        for (sem, val) in deps:
            if eng == "pe" and sem is self.sem["pe"]:
                continue
            if need.get(sem, 0) < val:
                need[sem] = val
        out = []
        wd = self.waited[eng]
        for sem, val in need.items():
            if wd.get(sem, 0) < val:
                wd[sem] = val
                out.append((sem, val))
        return out

    def op(self, eng, fn, reads=(), writes=(), inc=True):
        deps = []
        for b in reads:
            t = b.T
            if t.w is not None:
                deps.append(t.w)
        for b in writes:
            t = b.T
            if t.w is not None:
                deps.append(t.w)
            deps.extend(t.r)
        waits = self._waits(eng, deps)
        val = self.cnt[eng] + 1
        if inc:
            self.cnt[eng] = val
        ev = (self.sem[eng], val)
        for b in reads:
            b.T.r.append(ev)
        for b in writes:
            b.T.w = ev
            b.T.r = []
        self.ops[eng].append((waits, fn, (self.sem[eng], 1) if inc else None))

    def dma(self, eng, out, in_, dst, src=None):
        deps = []
        if dst.dram:
            own = src.T
            if own.w is not None:
                deps.append(own.w)
        else:
            own = dst.T
            if src is not None and not src.dram and src.T.w is not None:
                deps.append(src.T.w)
            if own.w is not None:
                deps.append(own.w)
            deps.extend(own.r)
        waits = self._waits(eng, deps)
        sem = self._dsem(own)
        own.dcnt += 16
        ev = (sem, own.dcnt)
        if dst.dram:
            own.r.append(ev)
        else:
            if src is not None and not src.dram:
                src.T.r.append(ev)
            own.w = ev
            own.r = []
        self.ops[eng].append((waits, lambda e: e.dma_start(out=out, in_=in_), (sem, 16)))

    def barrier(self):
        evs = [(self.sem[e], self.cnt[e]) for e in self.ENGS if self.cnt[e] > 0]
        evs += [(t.dsem, t.dcnt) for t in self.dtiles if t.dcnt > 0]
        for e in self.ENGS:
            deps = [ev for ev in evs if ev[0] is not self.sem[e]]
            waits = self._waits(e, deps)
            if waits:
                self.ops[e].append((waits, None, None))
        for t in self.dtiles:
            self.free_dsems.append((t.dsem, t.dcnt))
            t.dsem = None
            t.w = None
            t.r = []
        self.dtiles = []

    def emit(self):
        nc = self.nc
        handles = {"pe": "tensor", "act": "scalar", "dve": "vector", "pool": "gpsimd", "sp": "sync"}
        with nc.Block() as block:
            for e in self.ENGS:
                ops = self.ops[e]

                def body(eng, ops=ops):
                    for waits, fn, inc in ops:
                        for sem, val in waits:
                            eng.wait_ge(sem, val)
                        if fn is not None:
                            ins = fn(eng)
                            if inc is not None:
                                ins.then_inc(inc[0], inc[1])
                getattr(block, handles[e])(body)


class KB:
    def __init__(self, nc, stack):
        self.nc = nc
        self.S = Sched(nc, stack)
        self.off = 18688
        self.uid = 0
        self.base = 18688

    def reset(self):
        self.off = self.base

    def sb(self, name, shape, dtype):
        nb = int(np.prod(shape[1:])) * (4 if dtype == F32 else 2)
        nb = (nb + 63) // 64 * 64
        self.uid += 1
        t = self.nc.alloc_sbuf_tensor_at("%s_%d" % (name, self.uid), list(shape), dtype, offset=self.off)
        self.off += nb
        assert self.off <= 229376, ("SBUF overflow", name, self.off)
        return Buf(t, name)

    def sbs(self, name, shape, dtype, n):
        return [self.sb(name + str(i), shape, dtype) for i in range(n)]

    def act(self, out, in_, func, r, w, scale=1.0, bias=0.0, accum=None):
        if accum is None:
            fn = lambda e: e.activation(out=out, in_=in_, func=func, bias=bias, scale=scale)
        else:
            fn = lambda e: e.activation(out=out, in_=in_, func=func, bias=bias, scale=scale, accum_out=accum)
        self.S.op("act", fn, r, w)

    def ts(self, eng, out, in0, s1, s2, op0, op1, r, w):
        if op1 is None:
            fn = lambda e: e.tensor_scalar(out=out, in0=in0, scalar1=s1, scalar2=None, op0=op0)
        else:
            fn = lambda e: e.tensor_scalar(out=out, in0=in0, scalar1=s1, scalar2=s2, op0=op0, op1=op1)
        self.S.op(eng, fn, r, w)

    def tt(self, eng, out, in0, in1, op, r, w):
        self.S.op(eng, lambda e: e.tensor_tensor(out=out, in0=in0, in1=in1, op=op), r, w)

    def stt(self, eng, out, in0, scalar, in1, op0, op1, r, w):
        self.S.op(eng, lambda e: e.scalar_tensor_tensor(out=out, in0=in0, scalar=scalar, in1=in1, op0=op0, op1=op1), r, w)

    def copy(self, eng, out, in_, r, w):
        if eng == "act":
            self.S.op("act", lambda e: e.activation(out=out, in_=in_, func=AF.Copy), r, w)
        else:
            self.S.op(eng, lambda e: e.tensor_copy(out=out, in_=in_), r, w)

    def memset(self, eng, ap, val, w):
        self.S.op(eng, lambda e: e.memset(ap, val), (), w)

    def mm(self, out, lhsT, rhs, start, stop, r, w, inc=None):
        if inc is None:
            inc = stop
        self.S.op("pe", lambda e: e.matmul(out, lhsT=lhsT, rhs=rhs, start=start, stop=stop), r, w, inc=inc)

    def tr(self, out, in_, ident, r, w, inc):
        self.S.op("pe", lambda e: e.transpose(out, in_, ident), r, w, inc=inc)

    def dma(self, eng, out, in_, dst, src=None):
        self.S.dma(eng, out, in_, dst, src)


def build(only=None, dbg=(), ext_in=()):
    nc = bass.Bass("TRN2", target_bir_lowering=False)
    st = contextlib.ExitStack()
    with st:
        return _build(nc, st, only, dbg, ext_in)


class _LazyIn:
    def __init__(self, nc, name, shape, used):
        self.nc, self.name, self.shape, self.used = nc, name, list(shape), used
        self._ap = None

    def ap(self):
        if self._ap is None:
            self._ap = self.nc.dram_tensor(self.name, self.shape, F32, kind="ExternalInput").ap()
            self.used.append(self.name)
        return self._ap

    def __getitem__(self, k):
        return self.ap()[k]

    def rearrange(self, *a, **kw):
        return self.ap().rearrange(*a, **kw)


def _build(nc, st, only, dbg, ext_in):
    used_inputs = []
    nc._used_inputs = used_inputs

    def din(name, shape, dt=F32):
        return _LazyIn(nc, name, shape, used_inputs)

    def dscr(name, shape, dt):
        kind = "ExternalOutput" if name in dbg else ("ExternalInput" if name in ext_in else "Internal")
        if kind == "ExternalInput":
            used_inputs.append(name)
        return Buf(nc.dram_tensor(name, list(shape), dt, kind=kind).ap(), name, dram=True)

    xin = din("xin", [NTOK, D])
    cvec = din("cvec", [128, 16, 2])
    w_mod = din("w_mod", [2, D, 6 * D])
    b_mod = din("b_mod", [2, 6 * D])
    norm_g = din("norm_g", [2, 4 * D])
    w_ff_in = din("w_ff_in", [2, D, DFF])
    w_ff_out = din("w_ff_out", [2, DFF, D])
    ar_w_in = din("ar_w_in", [D, ARIN])
    ar_w_out = din("ar_w_out", [D, D])
    qkg = din("qkg", [128, 2])
    ropeP = din("ropeP", [128, 128])
    cosT = din("cosT", [128, SEQ])
    sinT = din("sinT", [128, SEQ])
    taps = din("taps", [128, 8, 5])
    convb = din("convb", [128, 8])
    ar_wa = din("ar_wa", [2, 8, 128, 128])
    ar_wx = din("ar_wx", [2, 8, 128, 128])
    rnnp = din("rnnp", [128, 3, 2, 8])
    gm_w_in = din("gm_w_in", [D, 2 * D])
    gm_bu = din("gm_bu", [128, 16])
    gm_rows = din("gm_rows", [3, D])
    gm_w_sp = din("gm_w_sp", [16, 128, 128])
    gm_b_sp = din("gm_b_sp", [1, D])
    gm_w_out = din("gm_w_out", [D, D])
    out = Buf(nc.dram_tensor("out", [HALF, D], F32, kind="ExternalOutput").ap(), "out", dram=True)
    xin_b = Buf(xin, "xin", dram=True)

    modv = dscr("modv", [14, D], F32)
    hT0 = dscr("hT0", [16, 128, HALF], BF16)
    qT = dscr("qT", [8, 128, HALF], BF16)
    kT = dscr("kT", [2, 128, NTOK], BF16)
    Vs = dscr("Vs", [NTOK, 256], BF16)
    xrT = dscr("xrT", [8, 128, NTOK], F32)
    ggT = dscr("ggT", [8, 128, HALF], F32)
    catT = dscr("catT", [16, 128, HALF], BF16)
    x1s = dscr("x1s", [HALF, D], F32)
    x2s = dscr("x2s", [HALF, D], F32)
    hTs = dscr("hTs", [16, 128, HALF], BF16)
    uT = dscr("uT", [64, 128, HALF], BF16)
    ffo = dscr("ffo", [HALF, D], F32)

    K = KB(nc, st)
    S = K.S

    PS = [Buf(nc.alloc_psum_tensor("ps%d" % i, [128, 512], F32), "ps%d" % i) for i in range(8)]

    ident = K.sb("ident", [128, 128], BF16)
    ones = K.sb("ones", [128, 128], BF16)
    idf = K.sb("idf", [128, 128], F32)
    K.memset("pool", idf[:], 1.0, [idf])
    S.op("pool", lambda e: e.affine_select(out=idf[:], in_=idf[:], pattern=[[-1, 128]], compare_op=ALU.is_equal,
                                            fill=0.0, base=0, channel_multiplier=1), [idf], [idf])
    K.copy("dve", ident[:], idf[:], [idf], [ident])
    K.memset("pool", ones[:], 1.0, [ones])
    K.base = K.off

    def rstd_from_ssq(ssq_ap, out_ap, n, r, w):
        K.act(out_ap, ssq_ap, AF.Sqrt, r, w, scale=1.0 / n, bias=EPS)
        S.op("dve", lambda e: e.reciprocal(out=out_ap, in_=out_ap), w, w)

    def norm_mod_transpose(xt, Abc, Bbc, junk, tmp, hb, ssq, rstd, pbanks, dst_ap, dst_buf):
        K.act(junk[:], xt[:], AF.Square, [xt], [junk, ssq], accum=ssq[:, 0:1])
        rstd_from_ssq(ssq[:, 0:1], rstd[:, 0:1], D, [ssq], [rstd])
        K.stt("dve", tmp[:], xt[:], rstd[:, 0:1], Abc[:], ALU.mult, ALU.mult, [xt, rstd, Abc], [tmp])
        K.tt("pool", hb[:], tmp[:], Bbc[:], ALU.add, [tmp, Bbc], [hb])
        for half in range(2):
            pb = pbanks[half]
            pv = pb.t[:].bitcast(BF16)
            for kk in range(8):
                k = half * 8 + kk
                K.tr(pv[:, kk * 128:(kk + 1) * 128], hb[:, k * 128:(k + 1) * 128], ident[:], [hb, ident], [pb], inc=(kk == 7))
            K.copy("act", dst_ap[:, half * 8:(half + 1) * 8, :], pv.rearrange("p (k t) -> p k t", k=8), [pb], [dst_buf])

    def load_bcast(buf, row_ap, src):
        K.dma("sp", buf[:], row_ap.partition_broadcast(128), buf, src)

    def phase_mod():
        K.reset()
        cv = K.sb("cv", [128, 16, 2], F32)
        sv = K.sb("sv", [128, 16, 2], F32)
        wt = K.sbs("wmod", [128, 16, 512], F32, 2)
        raw = K.sb("raw", [2, 6 * D], F32)
        gg = K.sb("gg", [2, 4 * D], F32)
        K.dma("sp", cv[:], cvec.ap(), cv)
        K.act(sv[:], cv[:], AF.Silu, [cv], [sv])
        it = 0
        for l in range(2):
            K.dma("sp", raw[:], b_mod[l:l + 1, :].partition_broadcast(2), raw)
            K.dma("sp", gg[:], norm_g[l:l + 1, :].partition_broadcast(2), gg)
            for n in range(24):
                w = wt[it % 2]
                it += 1
                K.dma("sp", w[:], w_mod[l, :, n * 512:(n + 1) * 512].rearrange("(k p) n -> p k n", p=128), w)
                pb = PS[it % 2]
                for k in range(16):
                    K.mm(pb[0:2, :], sv[:, k, :], w[:, k, :], k == 0, k == 15, [sv, w], [pb])
                K.tt("dve", raw[:, n * 512:(n + 1) * 512], pb[0:2, :], raw[:, n * 512:(n + 1) * 512], ALU.add, [pb, raw], [raw])
            K.stt("dve", raw[:, D:2 * D], raw[:, D:2 * D], 1.0, gg[:, 0:D], ALU.add, ALU.mult, [raw, gg], [raw])
            K.tt("dve", raw[:, 2 * D:3 * D], raw[:, 2 * D:3 * D], gg[:, D:2 * D], ALU.mult, [raw, gg], [raw])
            K.stt("dve", raw[:, 4 * D:5 * D], raw[:, 4 * D:5 * D], 1.0, gg[:, 2 * D:3 * D], ALU.add, ALU.mult, [raw, gg], [raw])
            K.tt("dve", raw[:, 5 * D:6 * D], raw[:, 5 * D:6 * D], gg[:, 3 * D:4 * D], ALU.mult, [raw, gg], [raw])
            for r, src in enumerate((1, 0, 2, 4, 3, 5)):
                K.dma("sp", modv[6 * l + r:6 * l + r + 1, :], raw[0:1, src * D:(src + 1) * D], modv, raw)
            if l == 0:
                K.dma("sp", modv[12:13, :], raw[1:2, D:2 * D], modv, raw)
                K.dma("sp", modv[13:14, :], raw[1:2, 0:D], modv, raw)
        S.barrier()

    def phase_inproj(mode):
        K.reset()
        if mode == "qg":
            col0 = [0, 512, 2560, 3072]
        else:
            col0 = [1024, 1536, 2048]
        npan = len(col0)
        W = K.sb("Win", [128, 16, npan * 512], BF16)
        Wp = [Buf(W.t, "Winp%d" % i) for i in range(npan)]
        for p in range(npan):
            K.dma("pool", W[:, :, p * 512:(p + 1) * 512],
                  ar_w_in[:, col0[p]:col0[p] + 512].rearrange("(k p) n -> p k n", p=128), Wp[p])

        def wloc(ocol):
            for p in range(npan):
                if col0[p] <= ocol < col0[p] + 512:
                    return p * 512 + ocol - col0[p], p
            raise KeyError(ocol)
        Abc = K.sb("Abc", [128, D], F32)
        Bbc = K.sb("Bbc", [128, D], F32)
        xt = K.sbs("xt", [128, D], F32, 2)
        tmp = K.sb("tmp", [128, D], F32)
        junk = K.sb("junk", [128, D], BF16)
        hb = K.sb("hb", [128, D], BF16)
        hT = K.sbs("hT", [128, 16, 512], BF16, 2)
        ssq = K.sb("ssq", [128, 4], F32)
        rstd = K.sb("rstd", [128, 4], F32)
        qk = K.sb("qkg", [128, 2], F32)
        Pm = K.sb("Pm", [128, 128], BF16)
        Pf = K.sb("Pf", [128, 128], F32)
        cs = K.sbs("cs", [128, 512], F32, 2)
        sn = K.sbs("sn", [128, 512], F32, 2)
        sq = K.sbs("sq", [128, 512], BF16, 2)
        qg = K.sbs("qg", [128, 512], BF16, 2)
        rs = K.sbs("rs", [128, 512], F32, 2)
        t1 = K.sbs("t1", [128, 512], F32, 2)
        t2 = K.sbs("t2", [128, 512], F32, 2)
        ob = K.sbs("ob", [128, 512], BF16, 3)
        of = K.sbs("of", [128, 512], F32, 3)
        vb = K.sbs("vb", [128, 256], BF16, 2)
        K.dma("sp", qk[:], qkg.ap(), qk)
        K.dma("sp", Pf[:], ropeP.ap(), Pf)
        K.copy("dve", Pm[:], Pf[:], [Pf], [Pm])
        load_bcast(Abc, modv[0:1, :], modv)
        load_bcast(Bbc, modv[1:2, :], modv)

        cnt = {"x": 0, "ps": 0, "o": 0, "f": 0, "v": 0, "qk": 0}

        def proj_cols(h, NT, n):
            lc, p = wloc(n * 128)
            pb = PS[2 + cnt["ps"] % 3]
            cnt["ps"] += 1
            for k in range(16):
                K.mm(pb[:, 0:NT], W[:, k, lc:lc + 128], h[:, k, 0:NT], k == 0, k == 15, [Wp[p], h], [pb])
            return pb

        def do_qk(pb, NT, gcol, rope_t0, dst_ap, dst_buf):
            i = cnt["qk"] % 2
            cnt["qk"] += 1
            K.act(sq[i][:, 0:NT], pb[:, 0:NT], AF.Square, [pb], [sq[i]])
            K.act(qg[i][:, 0:NT], pb[:, 0:NT], AF.Identity, [pb, qk], [qg[i]], scale=qk[:, gcol:gcol + 1])
            K.mm(PS[5][:, 0:NT], ones[:], sq[i][:, 0:NT], True, True, [ones, sq[i]], [PS[5]])
            rstd_from_ssq(PS[5][:, 0:NT], rs[i][:, 0:NT], 128, [PS[5]], [rs[i]])
            o = ob[cnt["o"] % 3]
            cnt["o"] += 1
            if rope_t0 is None:
                K.tt("pool", o[:, 0:NT], qg[i][:, 0:NT], rs[i][:, 0:NT], ALU.mult, [qg[i], rs[i]], [o])
            else:
                K.mm(PS[6][:, 0:NT], Pm[:], qg[i][:, 0:NT], True, True, [Pm, qg[i]], [PS[6]])
                K.tt("pool", t1[i][:, 0:NT], qg[i][:, 0:NT], cs[rope_t0][:, 0:NT], ALU.mult, [qg[i], cs[rope_t0]], [t1[i]])
                K.tt("dve", t2[i][:, 0:NT], PS[6][:, 0:NT], sn[rope_t0][:, 0:NT], ALU.mult, [PS[6], sn[rope_t0]], [t2[i]])
                K.tt("pool", t1[i][:, 0:NT], t1[i][:, 0:NT], t2[i][:, 0:NT], ALU.add, [t1[i], t2[i]], [t1[i]])
                K.tt("pool", o[:, 0:NT], t1[i][:, 0:NT], rs[i][:, 0:NT], ALU.mult, [t1[i], rs[i]], [o])
            K.dma("sp", dst_ap, o[:, 0:NT], dst_buf, o)

        for c in range(4 if mode == "qg" else 9):
            own = c < 4
            isctx = c == 8
            NT = 256 if isctx else 512
            t0 = c * 512
            h = hT[c % 2]
            if isctx:
                load_bcast(Abc, modv[12:13, :], modv)
                load_bcast(Bbc, modv[13:14, :], modv)
            if own and mode == "kvx":
                K.dma("sp", h[:], hT0[:, :, t0:t0 + 512].rearrange("k p t -> p k t"), h, hT0)
            else:
                for tt in range(NT // 128):
                    x = xt[cnt["x"] % 2]
                    cnt["x"] += 1
                    K.dma("sp", x[:], xin[t0 + tt * 128:t0 + (tt + 1) * 128, :], x)
                    norm_mod_transpose(x, Abc, Bbc, junk, tmp, hb, ssq, rstd, (PS[0], PS[1]),
                                       h[:, :, tt * 128:(tt + 1) * 128], h)
            if not isctx:
                ci = c % 2
                K.dma("sp", cs[ci][:], cosT[:, t0:t0 + 512], cs[ci])
                K.dma("sp", sn[ci][:], sinT[:, t0:t0 + 512], sn[ci])
            ri = None if isctx else c % 2
            if mode == "qg":
                K.dma("sp", hT0[:, :, t0:t0 + 512].rearrange("k p t -> p k t"), h[:], hT0, h)
                for hd in range(8):
                    pb = proj_cols(h, NT, hd)
                    do_qk(pb, NT, 0, ri, qT[hd, :, t0:t0 + NT], qT)
                for n in range(8):
                    pb = proj_cols(h, NT, 20 + n)
                    o = of[cnt["f"] % 3]
                    cnt["f"] += 1
                    K.act(o[:, 0:NT], pb[:, 0:NT], AF.Gelu_apprx_tanh, [pb], [o])
                    K.dma("sp", ggT[n, :, t0:t0 + NT], o[:, 0:NT], ggT, o)
                continue
            for kv in range(2):
                pb = proj_cols(h, NT, 8 + kv)
                do_qk(pb, NT, 1, ri, kT[kv, :, t0:t0 + NT], kT)
            vl, vp = wloc(1280)
            for tt in range(NT // 128):
                pb = PS[2 + cnt["ps"] % 3]
                cnt["ps"] += 1
                for k in range(16):
                    K.mm(pb[:, 0:256], h[:, k, tt * 128:(tt + 1) * 128], W[:, k, vl:vl + 256], k == 0, k == 15, [h, Wp[vp]], [pb])
                v = vb[cnt["v"] % 2]
                cnt["v"] += 1
                K.copy("dve", v[:], pb[:, 0:256], [pb], [v])
                K.dma("sp", Vs[t0 + tt * 128:t0 + (tt + 1) * 128, :], v[:], Vs, v)
            for n in range(8):
                pb = proj_cols(h, NT, 12 + n)
                o = of[cnt["f"] % 3]
                cnt["f"] += 1
                K.copy("act", o[:, 0:NT], pb[:, 0:NT], [pb], [o])
                K.dma("sp", xrT[n, :, t0:t0 + NT], o[:, 0:NT], xrT, o)
        S.barrier()

    def phase_attn():
        K.reset()
        NKT = NTOK // 128
        kt_sb = K.sb("kTs", [128, 2, NTOK], BF16)
        v_sb = K.sb("Vsb", [128, NKT, 256], BF16)
        qs = K.sbs("qs", [128, 512], BF16, 2)
        pbuf = K.sbs("pexp", [128, 512], BF16, 4)
        rl = K.sbs("rl", [128, 512], F32, 2)
        ob = K.sbs("aob", [128, 512], BF16, 2)
        K.dma("sp", kt_sb[:], kT[:].rearrange("k p t -> p k t"), kt_sb, kT)
        v_h = [Buf(v_sb.t, "v_h%d" % i) for i in range(2)]
        for i in range(2):
            K.dma("sp", v_sb[:, i * 17:(i + 1) * 17, :],
                  Vs[i * 17 * 128:(i + 1) * 17 * 128, :].rearrange("(kt p) d -> p kt d", p=128), v_h[i], Vs)
        scale = 128.0 ** -0.5
        it = 0
        sc = 0
        for hd in range(8):
            kv = hd // 4
            for qt in range(4):
                q = qs[it % 2]
                K.dma("sp", q[:], qT[hd, :, qt * 512:(qt + 1) * 512], q, qT)
                Ob = PS[4 + 2 * (it % 2)]
                Lb = PS[5 + 2 * (it % 2)]
                sbanks = {}

                def issue_s(kt, q=q, kv=kv):
                    nonlocal sc
                    pb = PS[sc % 4]
                    sc += 1
                    K.mm(pb[:], kt_sb[:, kv, kt * 128:(kt + 1) * 128], q[:], True, True, [kt_sb, q], [pb])
                    sbanks[kt] = pb
                issue_s(0)
                issue_s(1)
                for kt in range(NKT):
                    if kt + 2 < NKT:
                        issue_s(kt + 2)
                    pb = sbanks.pop(kt)
                    p = pbuf[(it * NKT + kt) % 4]
                    K.act(p[:], pb[:], AF.Exp, [pb], [p], scale=scale)
                    K.mm(Ob[:], v_sb[:, kt, kv * 128:(kv + 1) * 128], p[:], kt == 0, kt == NKT - 1, [v_h[kt // 17], p], [Ob], inc=(kt == NKT - 1))
                    K.mm(Lb[:], ones[:], p[:], kt == 0, kt == NKT - 1, [ones, p], [Lb], inc=True)
                r = rl[it % 2]
                o = ob[it % 2]
                S.op("dve", lambda e, r=r, Lb=Lb: e.reciprocal(out=r[:], in_=Lb[:]), [Lb], [r])
                K.tt("dve", o[:], Ob[:], r[:], ALU.mult, [Ob, r], [o])
                K.dma("sp", catT[hd, :, qt * 512:(qt + 1) * 512], o[:], catT, o)
                it += 1
        S.barrier()

    def phase_rnn():
        K.reset()
        tp = K.sb("taps", [128, 8, 5], F32)
        cb = K.sb("convb", [128, 8], F32)
        rp = K.sb("rnnp", [128, 3, 2, 8], F32)
        c1 = K.sb("c1", [128, 2, 8], F32)
        e1 = K.sb("e1", [128, 2, 8], F32)
        waf = K.sbs("waf", [128, 128], F32, 2)
        wab = K.sbs("wab", [128, 128], BF16, 4)
        wxb = K.sbs("wxb", [128, 128], BF16, 4)
        xl = K.sbs("xl", [128, XL], F32, 2)
        xc = K.sbs("xc", [128, XC], F32, 2)
        acc = K.sb("acc", [128, NTOK], F32)
        accb = K.sb("accb", [128, NTOK], BF16)
        av = K.sb("av", [128, NTOK], F32)
        bv = K.sb("bv", [128, NTOK], F32)
        avt = [Buf(av.t, "avt%d" % i) for i in range(9)]
        bvt = [Buf(bv.t, "bvt%d" % i) for i in range(9)]
        rr = K.sbs("rr", [128, 512], F32, 2)
        ii = K.sbs("ii", [128, 512], F32, 2)
        a2 = K.sbs("a2", [128, 512], F32, 2)
        sq = K.sbs("sq", [128, 512], F32, 2)
        ix = K.sbs("ix", [128, 512], F32, 2)
        hA = K.sb("hA", [128, HALF], F32)
        hB = K.sb("hB", [128, HALF], F32)
        hj = K.sb("hj", [128, HALF], F32)
        hc = K.sb("hc", [128, CTX], F32)
        gg = K.sbs("gg", [128, HALF], F32, 2)
        ro = K.sbs("ro", [128, HALF], BF16, 2)
        K.dma("sp", tp[:], taps.ap(), tp)
        K.dma("sp", cb[:], convb.ap(), cb)
        K.dma("sp", rp[:], rnnp.ap(), rp)
        K.act(e1[:], rp[:, 2, :, :], AF.Exp, [rp], [e1], scale=-1.0)
        K.act(e1[:], e1[:], AF.Ln, [e1], [e1], bias=1.0)
        K.ts("dve", c1[:], e1[:], -8.0, None, ALU.mult, None, [e1], [c1])
        for b in xl:
            K.memset("pool", b[:, 0:2], 0.0, [b])
            K.memset("pool", b[:, XL - 2:XL], 0.0, [b])
        for b in xc:
            K.memset("pool", b[:, 0:2], 0.0, [b])
            K.memset("pool", b[:, XC - 2:XC], 0.0, [b])
        wi = 0
        ti = 0
        for n in range(8):
            X = xl[n % 2]
            C = xc[n % 2]
            K.dma("sp", X[:, 2:2 + SEQ], xrT[n, :, 0:SEQ], X, xrT)
            K.dma("sp", C[:, 2:2 + CTX], xrT[n, :, SEQ:NTOK], C, xrT)
            G = gg[n % 2]
            K.dma("sp", G[:], ggT[n, :, :], G, ggT)
            for (src, L, o0) in ((X, SEQ, 0), (C, CTX, SEQ)):
                K.act(acc[:, o0:o0 + L], src[:, 0:L], AF.Identity, [src, tp, cb], [acc], scale=tp[:, n, 0:1], bias=cb[:, n:n + 1])
                for j in range(1, 5):
                    K.stt("dve", acc[:, o0:o0 + L], src[:, j:j + L], tp[:, n, j:j + 1], acc[:, o0:o0 + L], ALU.mult, ALU.add, [src, tp, acc], [acc])
            K.copy("act", accb[:], acc[:], [acc], [accb])
            for d in range(2):
                wA = wab[wi % 4]
                wX = wxb[wi % 4]
                wi += 1
                for (srcw, dstw) in ((ar_wa, wA), (ar_wx, wX)):
                    f = waf[ti % 2]
                    ti += 1
                    K.dma("sp", f[:], srcw[d, n, :, :], f)
                    K.copy("dve", dstw[:], f[:], [f], [dstw])
                ranges = [(SEQ, CTX)] + ([(0, HALF)] if d == 0 else [(0, SEQ)])
                for (r0, rl_) in ranges:
                    for c0 in range(r0, r0 + rl_, 512):
                        NT = min(512, r0 + rl_ - c0)
                        i = (c0 // 512) % 2
                        K.mm(PS[0 + i][:, 0:NT], wA[:], accb[:, c0:c0 + NT], True, True, [wA, accb], [PS[0 + i]])
                        K.mm(PS[2 + i][:, 0:NT], wX[:], accb[:, c0:c0 + NT], True, True, [wX, accb], [PS[2 + i]])
                        K.act(rr[i][:, 0:NT], PS[0 + i][:, 0:NT], AF.Sigmoid, [PS[0 + i], rp], [rr[i]], bias=rp[:, 0, d, n:n + 1])
                        K.act(ii[i][:, 0:NT], PS[2 + i][:, 0:NT], AF.Sigmoid, [PS[2 + i], rp], [ii[i]], bias=rp[:, 1, d, n:n + 1])
                        ct = c0 // 512
                        K.act(av[:, c0:c0 + NT], rr[i][:, 0:NT], AF.Exp, [rr[i], c1], [avt[ct]], scale=c1[:, d, n:n + 1])
                        K.tt("pool", a2[i][:, 0:NT], av[:, c0:c0 + NT], av[:, c0:c0 + NT], ALU.mult, [avt[ct]], [a2[i]])
                        K.act(sq[i][:, 0:NT], a2[i][:, 0:NT], AF.Sqrt, [a2[i]], [sq[i]], scale=-1.0, bias=1.0)
                        K.tt("pool", ix[i][:, 0:NT], ii[i][:, 0:NT], acc[:, c0:c0 + NT], ALU.mult, [ii[i], acc], [ix[i]])
                        K.tt("pool", bv[:, c0:c0 + NT], sq[i][:, 0:NT], ix[i][:, 0:NT], ALU.mult, [sq[i], ix[i]], [bvt[ct]])
                if d == 0:
                    S.op("dve", lambda e: e.tensor_tensor_scan(out=hc[:, :], data0=av[:, SEQ:NTOK], data1=bv[:, SEQ:NTOK],
                                                               initial=0.0, op0=ALU.mult, op1=ALU.add), [avt[8], bvt[8]], [hc])
                    S.op("dve", lambda e: e.tensor_tensor_scan(out=hA[:, :], data0=av[:, 0:HALF], data1=bv[:, 0:HALF],
                                                               initial=hc[:, CTX - 1:CTX], op0=ALU.mult, op1=ALU.add), avt[0:4] + bvt[0:4] + [hc], [hA])
                else:
                    S.op("dve", lambda e: e.tensor_tensor_scan(out=hc[:, ::-1], data0=av[:, SEQ:NTOK][:, ::-1], data1=bv[:, SEQ:NTOK][:, ::-1],
                                                               initial=0.0, op0=ALU.mult, op1=ALU.add), [avt[8], bvt[8]], [hc])
                    S.op("dve", lambda e: e.tensor_tensor_scan(out=hj[:, ::-1], data0=av[:, HALF:SEQ][:, ::-1], data1=bv[:, HALF:SEQ][:, ::-1],
                                                               initial=hc[:, 0:1], op0=ALU.mult, op1=ALU.add), avt[4:8] + bvt[4:8] + [hc], [hj])
                    S.op("dve", lambda e: e.tensor_tensor_scan(out=hB[:, ::-1], data0=av[:, 0:HALF][:, ::-1], data1=bv[:, 0:HALF][:, ::-1],
                                                               initial=hj[:, 0:1], op0=ALU.mult, op1=ALU.add), avt[0:4] + bvt[0:4] + [hj], [hB])
            K.tt("pool", hA[:], hA[:], hB[:], ALU.add, [hA, hB], [hA])
            R = ro[n % 2]
            K.tt("pool", R[:], hA[:], G[:], ALU.mult, [hA, G], [R])
            K.dma("sp", catT[8 + n, :, :], R[:], catT, R)
        S.barrier()

    def make_epi_bufs():
        B = {}
        B["Gbc"] = K.sb("Gbc", [128, D], F32)
        B["Abc"] = K.sb("Abc", [128, D], F32)
        B["Bbc"] = K.sb("Bbc", [128, D], F32)
        B["xt"] = K.sbs("ext", [128, D], F32, 2)
        B["x1"] = K.sbs("ex1", [128, D], F32, 2)
        B["tmp"] = K.sb("etmp", [128, D], F32)
        B["junk"] = K.sb("ejunk", [128, D], BF16)
        B["hb"] = K.sb("ehb", [128, D], BF16)
        B["ssq4"] = K.sb("essq4", [128, 4], F32)
        B["ssq"] = K.sb("essq", [128, 4], F32)
        B["rstd"] = K.sb("erstd", [128, 4], F32)
        B["rstd2"] = K.sb("erstd2", [128, 4], F32)
        B["hst"] = K.sbs("ehst", [128, 16, 512], BF16, 2)
        B["n"] = 0
        return B

    def epilogue(B, raw_aps, raw_bufs, xsrc, tok0, xdst, do_norm, hdst, pbanks):
        i = B["n"] % 2
        tt = B["n"] % 4
        B["n"] += 1
        ssq4, ssq, rstd, tmp, junk = B["ssq4"], B["ssq"], B["rstd"], B["tmp"], B["junk"]
        x = B["xt"][i]
        x1 = B["x1"][i]
        K.dma("sp", x[:], xsrc[tok0:tok0 + 128, :], x, xsrc)
        for fb in range(4):
            K.act(junk[:, fb * 512:(fb + 1) * 512], raw_aps[fb], AF.Square, [raw_bufs[fb]], [junk, ssq4], accum=ssq4[:, fb:fb + 1])
        S.op("dve", lambda e: e.reduce_sum(out=ssq[:, 0:1], in_=ssq4[:, 0:4], axis=AX.X), [ssq4], [ssq])
        rstd_from_ssq(ssq[:, 0:1], rstd[:, 0:1], D, [ssq], [rstd])
        for fb in range(4):
            K.stt("dve", tmp[:, fb * 512:(fb + 1) * 512], raw_aps[fb], rstd[:, 0:1], B["Gbc"][:, fb * 512:(fb + 1) * 512],
                  ALU.mult, ALU.mult, [raw_bufs[fb], rstd, B["Gbc"]], [tmp])
        K.tt("pool", x1[:], tmp[:], x[:], ALU.add, [tmp, x], [x1])
        K.dma("sp", xdst[tok0:tok0 + 128, :], x1[:], xdst, x1)
        if do_norm:
            hst = B["hst"][(B["n"] - 1) // 4 % 2]
            norm_mod_transpose(x1, B["Abc"], B["Bbc"], junk, tmp, B["hb"], B["ssq"], B["rstd2"], pbanks,
                               hst[:, :, tt * 128:(tt + 1) * 128], hst)
            if tt == 3:
                c0 = tok0 - 384
                K.dma("sp", hdst[:, :, c0:c0 + 512].rearrange("k p t -> p k t"), hst[:], hdst, hst)

    def phase_outproj(Wd, mrow, xsrc, xdst, hdst):
        K.reset()
        W = K.sb("Wout", [128, 16, D], BF16)
        Wp = [Buf(W.t, "Woutp%d" % i) for i in range(4)]
        for p in range(4):
            K.dma("pool", W[:, :, p * 512:(p + 1) * 512], Wd[:, p * 512:(p + 1) * 512].rearrange("(k p) n -> p k n", p=128), Wp[p])
        B = make_epi_bufs()
        load_bcast(B["Gbc"], modv[mrow + 2:mrow + 3, :], modv)
        load_bcast(B["Abc"], modv[mrow + 3:mrow + 4, :], modv)
        load_bcast(B["Bbc"], modv[mrow + 4:mrow + 5, :], modv)
        cat = K.sbs("cat", [128, 16, 512], BF16, 2)
        for c in range(4):
            cc = cat[c % 2]
            K.dma("sp", cc[:], catT[:, :, c * 512:(c + 1) * 512].rearrange("k p t -> p k t"), cc, catT)
            for tt in range(4):
                for k in range(16):
                    for fb in range(4):
                        K.mm(PS[fb][:], cc[:, k, tt * 128:(tt + 1) * 128], W[:, k, fb * 512:(fb + 1) * 512],
                             k == 0, k == 15, [cc, Wp[fb]], [PS[fb]])
                epilogue(B, [PS[fb][:] for fb in range(4)], [PS[fb] for fb in range(4)], xsrc, c * 512 + tt * 128,
                         xdst, True, hdst, (PS[4], PS[5]))
        S.barrier()

    def phase_up(Wd, ncols, hsrc, dst, kind, bias_d=None):
        K.reset()
        h = K.sb("hres", [128, 16, HALF], BF16)
        hp = [Buf(h.t, "hresp%d" % i) for i in range(4)]
        for c in range(4):
            K.dma("sp", h[:, :, c * 512:(c + 1) * 512], hsrc[:, :, c * 512:(c + 1) * 512].rearrange("k p t -> p k t"), hp[c], hsrc)
        Wt = K.sbs("Wup", [128, 16, 512], BF16, 3)
        rl = K.sbs("rl", [128, 512], F32, 3)
        ust = K.sbs("ust", [128, HALF], BF16, 3)
        bias = None
        if bias_d is not None:
            bias = K.sb("bias", [128, ncols // 128], F32)
            K.dma("sp", bias[:], bias_d.ap(), bias)
        pi = 0
        for pn in range(ncols // 512):
            Wb = Wt[pn % 3]
            K.dma("pool", Wb[:], Wd[:, pn * 512:(pn + 1) * 512].rearrange("(k p) n -> p k n", p=128), Wb)
            for jc in range(4):
                j = pn * 4 + jc
                u = ust[j % 3]
                for c in range(4):
                    pb = PS[pi % 6]
                    pi += 1
                    for k in range(16):
                        K.mm(pb[:], Wb[:, k, jc * 128:(jc + 1) * 128], h[:, k, c * 512:(c + 1) * 512], k == 0, k == 15, [Wb, hp[c]], [pb])
                    if kind == "relu2":
                        r = rl[pi % 3]
                        K.act(r[:], pb[:], AF.Relu, [pb], [r])
                        K.tt("pool", u[:, c * 512:(c + 1) * 512], r[:], r[:], ALU.mult, [r], [u])
                    else:
                        K.act(u[:, c * 512:(c + 1) * 512], pb[:], AF.Gelu_apprx_tanh, [pb, bias], [u], bias=bias[:, j:j + 1])
                K.dma("sp", dst[j, :, :], u[:], dst, u)
        S.barrier()

    def phase_down(Wd):
        K.reset()
        Wq = K.sbs("W2q", [128, 64, 512], BF16, 2)
        Wqp = [[Buf(Wq[i].t, "W2q%dp%d" % (i, p)) for p in range(4)] for i in range(2)]
        ut = K.sbs("ut", [128, 64, 256], BF16, 2)
        utp = [[Buf(ut[i].t, "ut%dp%d" % (i, p)) for p in range(4)] for i in range(2)]
        of = K.sbs("dof", [128, 512], F32, 3)
        ui = 0
        oi = 0
        for q in range(4):
            Wb = Wq[q % 2]
            for p in range(4):
                K.dma("pool", Wb[:, p * 16:(p + 1) * 16, :],
                      Wd[p * 2048:(p + 1) * 2048, q * 512:(q + 1) * 512].rearrange("(j p) n -> p j n", p=128), Wqp[q % 2][p])
            for tc in range(8):
                U = ut[ui % 2]
                Up = utp[ui % 2]
                ui += 1
                for p in range(4):
                    K.dma("sp", U[:, p * 16:(p + 1) * 16, :], uT[p * 16:(p + 1) * 16, :, tc * 256:(tc + 1) * 256].rearrange("j p t -> p j t"), Up[p], uT)
                for tt in range(2):
                    pb = PS[oi % 4]
                    for j in range(64):
                        K.mm(pb[:], U[:, j, tt * 128:(tt + 1) * 128], Wb[:, j, :], j == 0, j == 63, [Up[j // 16], Wqp[q % 2][j // 16]], [pb])
                    o = of[oi % 3]
                    oi += 1
                    K.copy("act", o[:], pb[:], [pb], [o])
                    t0 = tc * 256 + tt * 128
                    K.dma("sp", ffo[t0:t0 + 128, q * 512:(q + 1) * 512], o[:], ffo, o)
        S.barrier()

    def phase_ffn_epi(mrow, xsrc, xdst, do_norm, nrow, hdst):
        K.reset()
        B = make_epi_bufs()
        load_bcast(B["Gbc"], modv[mrow + 5:mrow + 6, :], modv)
        if do_norm:
            load_bcast(B["Abc"], modv[nrow:nrow + 1, :], modv)
            load_bcast(B["Bbc"], modv[nrow + 1:nrow + 2, :], modv)
        ft = K.sbs("ft", [128, D], F32, 2)
        for t in range(16):
            f = ft[t % 2]
            K.dma("sp", f[:], ffo[t * 128:(t + 1) * 128, :], f, ffo)
            epilogue(B, [f[:, fb * 512:(fb + 1) * 512] for fb in range(4)], [f] * 4, xsrc, t * 128, xdst, do_norm, hdst, (PS[4], PS[5]))
        S.barrier()

    def phase_gm_v():
        K.reset()
        W = K.sb("Wv", [128, 16, D], BF16)
        Wp = [Buf(W.t, "Wvp%d" % i) for i in range(4)]
        for p in range(4):
            K.dma("pool", W[:, :, p * 512:(p + 1) * 512], gm_w_in[:, D + p * 512:D + (p + 1) * 512].rearrange("(k p) n -> p k n", p=128), Wp[p])
        bvb = K.sb("bvb", [128, D], F32)
        vgb = K.sb("vgb", [128, D], F32)
        vbb = K.sb("vbb", [128, D], F32)
        gr = None
        load_bcast(bvb, gm_rows[0:1, :], gr)
        load_bcast(vgb, gm_rows[1:2, :], gr)
        load_bcast(vbb, gm_rows[2:3, :], gr)
        spf = K.sb("spf", [128, 16, 128], F32)
        spb = K.sb("spb", [128, 16, 128], BF16)
        spT = K.sb("spT", [128, 16, 128], BF16)
        bsf = K.sb("bsf", [1, D], F32)
        bsb = K.sb("bsb", [1, D], BF16)
        K.dma("sp", spf[:], gm_w_sp.rearrange("g p q -> p g q"), spf)
        K.copy("dve", spb[:], spf[:], [spf], [spb])
        for half in range(2):
            pb = PS[half]
            pv = pb.t[:].bitcast(BF16)
            for gg_ in range(8):
                g = half * 8 + gg_
                K.tr(pv[:, gg_ * 128:(gg_ + 1) * 128], spb[:, g, :], ident[:], [spb, ident], [pb], inc=(gg_ == 7))
            K.copy("act", spT[:, half * 8:(half + 1) * 8, :], pv.rearrange("p (g t) -> p g t", g=8), [pb], [spT])
        K.dma("sp", bsf[:], gm_b_sp.ap(), bsf)
        K.copy("dve", bsb[:], bsf[:], [bsf], [bsb])
        hT = K.sbs("hTg", [128, 16, 512], BF16, 2)
        gu = K.sbs("gu", [128, 16, 512], BF16, 1)
        pst = K.sbs("pst", [128, 16, 512], BF16, 1)
        v = K.sb("v", [128, D], F32)
        vg = K.sb("vg", [128, D], F32)
        junk = K.sb("junk", [128, D], BF16)
        vln = K.sb("vln", [128, D], BF16)
        st4 = K.sb("st4", [128, 8], F32)
        for c in range(4):
            h = hT[c % 2]
            G = gu[0]
            P = pst[0]
            K.dma("sp", h[:], hTs[:, :, c * 512:(c + 1) * 512].rearrange("k p t -> p k t"), h, hTs)
            K.dma("sp", G[:], ggTu[:, :, c * 512:(c + 1) * 512].rearrange("k p t -> p k t"), G, ggTu)
            for tt in range(4):
                ts_ = slice(tt * 128, (tt + 1) * 128)
                for k in range(16):
                    for fb in range(4):
                        K.mm(PS[fb][:], h[:, k, ts_], W[:, k, fb * 512:(fb + 1) * 512], k == 0, k == 15, [h, Wp[fb]], [PS[fb]])
                for fb in range(4):
                    fs = slice(fb * 512, (fb + 1) * 512)
                    K.tt("dve", v[:, fs], PS[fb][:], bvb[:, fs], ALU.add, [PS[fb], bvb], [v])
                K.act(vg[:], v[:], AF.Gelu_apprx_tanh, [v], [vg, st4], accum=st4[:, 0:1])
                K.act(junk[:], vg[:], AF.Square, [vg], [junk, st4], accum=st4[:, 1:2])
                K.ts("dve", st4[:, 2:3], st4[:, 0:1], 1.0 / D, None, ALU.mult, None, [st4], [st4])
                K.tt("dve", st4[:, 3:4], st4[:, 2:3], st4[:, 2:3], ALU.mult, [st4], [st4])
                K.stt("dve", st4[:, 4:5], st4[:, 1:2], 1.0 / D, st4[:, 3:4], ALU.mult, ALU.subtract, [st4], [st4])
                rstd_from_ssq(st4[:, 4:5], st4[:, 5:6], 1.0, [st4], [st4])
                K.stt("dve", st4[:, 6:7], st4[:, 2:3], -1.0, st4[:, 5:6], ALU.mult, ALU.mult, [st4], [st4])
                K.act(v[:], vg[:], AF.Identity, [vg, st4], [v], scale=st4[:, 5:6], bias=st4[:, 6:7])
                K.tt("dve", vg[:], v[:], vgb[:], ALU.mult, [v, vgb], [vg])
                K.tt("pool", vln[:], vg[:], vbb[:], ALU.add, [vg, vbb], [vln])
                for g in range(16):
                    pb = PS[4 + g // 4]
                    oc = slice((g % 4) * 128, (g % 4 + 1) * 128)
                    K.mm(pb[:, oc], vln[:, g * 128:(g + 1) * 128], spT[:, g, :], True, False, [vln, spT], [pb], inc=False)
                    K.mm(pb[:, oc], ones[0:1, :], bsb[0:1, g * 128:(g + 1) * 128], False, True, [ones, bsb], [pb], inc=(g % 4 == 3))
                for gq in range(4):
                    K.tt("dve", P[:, gq * 4:(gq + 1) * 4, ts_], PS[4 + gq][:].rearrange("p (g t) -> p g t", g=4),
                         G[:, gq * 4:(gq + 1) * 4, ts_], ALU.mult, [PS[4 + gq], G], [P])
            K.dma("sp", catT[:, :, c * 512:(c + 1) * 512].rearrange("k p t -> p k t"), P[:], catT, P)
        S.barrier()

    ggTu = dscr("ggTu", [16, 128, HALF], BF16)

    phases = [
        ("mod", phase_mod),
        ("inproj", lambda: (phase_inproj("qg"), phase_inproj("kvx"))),
        ("attn", phase_attn),
        ("rnn", phase_rnn),
        ("outproj0", lambda: phase_outproj(ar_w_out, 0, xin_b, x1s, hTs)),
        ("up0", lambda: phase_up(w_ff_in[0], DFF, hTs, uT, "relu2")),
        ("down0", lambda: phase_down(w_ff_out[0])),
        ("epi0", lambda: phase_ffn_epi(0, x1s, x2s, True, 6, hTs)),
        ("gmu", lambda: phase_up(gm_w_in, D, hTs, ggTu, "gelu", gm_bu)),
        ("gmv", phase_gm_v),
        ("outproj1", lambda: phase_outproj(gm_w_out, 6, x2s, x1s, hTs)),
        ("up1", lambda: phase_up(w_ff_in[1], DFF, hTs, uT, "relu2")),
        ("down1", lambda: phase_down(w_ff_out[1])),
        ("epi1", lambda: phase_ffn_epi(6, x1s, out, False, 0, None)),
    ]
    for i, (name, fn) in enumerate(phases):
        if only is not None and name not in only:
            continue
        fn()
    S.barrier()
    S.emit()
    return nc


def _rope_tables():
    rows = SEQ // 64
    r_idx, c_idx = np.meshgrid(np.arange(rows), np.arange(64), indexing="ij")
    r_idx = r_idx.reshape(-1).astype(np.float32)
    c_idx = c_idx.reshape(-1).astype(np.float32)
    freqs = (np.float32(10000.0) ** (-np.arange(32, dtype=np.float32) / np.float32(32))).astype(np.float32)
    ang_r = r_idx[:, None] * freqs
    ang_c = c_idx[:, None] * freqs
    ang = np.concatenate([ang_r, ang_r, ang_c, ang_c], axis=1)
    cosT = np.ascontiguousarray(np.cos(ang).T.astype(np.float32))
    sinT = np.ascontiguousarray(np.sin(ang).T.astype(np.float32))
    Pm = np.zeros((128, 128), np.float32)
    for d in range(128):
        blk = d // 32
        if blk % 2 == 0:
            Pm[d, d + 32] = -1.0
        else:
            Pm[d, d - 32] = 1.0
    ropeP = np.ascontiguousarray(Pm.T)
    return cosT, sinT, ropeP


def make_in_maps(inp, cores=range(8)):
    f = lambda a: np.ascontiguousarray(a, dtype=np.float32)
    cosT, sinT, ropeP = _rope_tables()
    shared = {
        "w_mod": f(inp["w_mod"]), "b_mod": f(inp["b_mod"]), "norm_g": f(inp["norm_g"].reshape(2, 4 * D)),
        "w_ff_in": f(inp["w_ff_in"]), "w_ff_out": f(inp["w_ff_out"]),
        "ar_w_in": f(inp["ar_w_in"][0]), "ar_w_out": f(inp["ar_w_out"][0]),
        "qkg": f(np.stack([inp["ar_q_g"][0], inp["ar_k_g"][0]], axis=1)),
        "ropeP": ropeP,
        "convb": f(inp["ar_conv_b"][0].reshape(8, 128).T),
        "gm_w_in": f(inp["gm_w_in"][0]),
        "gm_bu": f(inp["gm_b_in"][0][:D].reshape(16, 128).T),
        "gm_rows": f(np.stack([inp["gm_b_in"][0][D:], inp["gm_v_g"][0], inp["gm_v_b"][0]], axis=0)),
        "gm_w_out": f(inp["gm_w_out"][0]),
    }
    cw = inp["ar_conv_w"][0]
    maps = []
    for core in cores:
        b, half = core // 2, core % 2
        m = dict(shared)
        x = inp["x"][b]
        cx = inp["ctx"][b]
        if half == 0:
            xin = np.concatenate([x[:HALF], x[HALF:], cx], axis=0)
            cs, sn = cosT, sinT
            tp = np.stack([cw[0], cw[1], cw[2], cw[3], np.zeros_like(cw[0])], axis=1)
            dsel = [0, 1]
            wsp = inp["gm_w_sp"][0]
            bsp = inp["gm_b_sp"][0]
        else:
            xr = x[::-1]
            xin = np.concatenate([xr[:HALF], xr[HALF:], cx[::-1]], axis=0)
            cs, sn = cosT[:, ::-1], sinT[:, ::-1]
            tp = np.stack([np.zeros_like(cw[0]), cw[3], cw[2], cw[1], cw[0]], axis=1)
            dsel = [1, 0]
            wsp = inp["gm_w_sp"][0][:, ::-1, ::-1]
            bsp = inp["gm_b_sp"][0][:, ::-1]
        m["xin"] = f(xin)
        m["cosT"] = f(cs)
        m["sinT"] = f(sn)
        m["taps"] = f(tp.reshape(8, 128, 5).transpose(1, 0, 2))
        m["cvec"] = f(np.stack([inp["c"][b].reshape(16, 128).T, inp["c_ctx"].reshape(16, 128).T], axis=2))
        m["ar_wa"] = f(inp["ar_wa"][0][dsel])
        m["ar_wx"] = f(inp["ar_wx"][0][dsel])
        pr = np.stack([inp["ar_ba"][0][dsel], inp["ar_bx"][0][dsel], inp["ar_lambda"][0][dsel]], axis=0)
        m["rnnp"] = f(pr.reshape(3, 2, 8, 128).transpose(3, 0, 1, 2))
        m["gm_w_sp"] = f(wsp)
        m["gm_b_sp"] = f(bsp.reshape(1, D))
        maps.append(m)
    return maps


_NC_CACHE = {}


def kernel(**inputs):
    inp = {k: np.asarray(v) for k, v in inputs.items()}
    if "nc" not in _NC_CACHE:
        _NC_CACHE["nc"] = build()
    nc = _NC_CACHE["nc"]
    maps = make_in_maps(inp)
    res = run_bass_kernel_spmd(nc, maps, core_ids=list(range(8)))
    out = np.empty((4, SEQ, D), np.float32)
    for core in range(8):
        b, half = core // 2, core % 2
        o = np.asarray(res.results[core]["out"], dtype=np.float32)
        if half == 0:
            out[b, :HALF] = o
        else:
            out[b, HALF:] = o[::-1]
    return out
```

```python
import contextlib
import numpy as np
import concourse.bass as bass
import concourse.mybir as mybir
from concourse.bass_utils import run_bass_kernel_spmd

F32 = mybir.dt.float32
BF16 = mybir.dt.bfloat16
AF = mybir.ActivationFunctionType
ALU = mybir.AluOpType
AX = mybir.AxisListType

D = 2048
SEQ = 4096
HALF = 2048
CTX = 256
NTOK = SEQ + CTX
DFF = 8192
ARIN = 3584
EPS = 1e-6
XL = 4 + SEQ
XC = 4 + CTX


class Tile:
    __slots__ = ("name", "w", "r", "dsem", "dcnt")

    def __init__(self, name):
        self.name = name
        self.w = None
        self.r = []
        self.dsem = None
        self.dcnt = 0


class Buf:
    def __init__(self, t, name, dram=False):
        self.t = t
        self.T = Tile(name)
        self.dram = dram

    def __getitem__(self, k):
        return self.t[k]


class Sched:
    ENGS = ("pe", "act", "dve", "pool", "sp")

    def __init__(self, nc, stack):
        self.nc = nc
        self.stack = stack
        self.ops = {e: [] for e in self.ENGS}
        self.sem = {e: stack.enter_context(nc.semaphore("s_" + e)) for e in self.ENGS}
        self.cnt = {e: 0 for e in self.ENGS}
        self.waited = {e: {} for e in self.ENGS}
        self.dtiles = []
        self.free_dsems = []
        self.nsem = 0

    def _dsem(self, t):
        if t.dsem is None:
            if self.free_dsems:
                t.dsem, t.dcnt = self.free_dsems.pop()
            else:
                self.nsem += 1
                t.dsem = self.stack.enter_context(self.nc.semaphore("d%d" % self.nsem))
                t.dcnt = 0
            self.dtiles.append(t)
        return t.dsem

    def _waits(self, eng, deps):
        need = {}
        for (sem, val) in deps:
            if eng == "pe" and sem is self.sem["pe"]:
                continue
            if need.get(sem, 0) < val:
                need[sem] = val
        out = []
        wd = self.waited[eng]
        for sem, val in need.items():
            if wd.get(sem, 0) < val:
                wd[sem] = val
                out.append((sem, val))
        return out

    def op(self, eng, fn, reads=(), writes=(), inc=True):
        deps = []
        for b in reads:
            t = b.T
            if t.w is not None:
                deps.append(t.w)
        for b in writes:
            t = b.T
            if t.w is not None:
                deps.append(t.w)
            deps.extend(t.r)
        waits = self._waits(eng, deps)
        val = self.cnt[eng] + 1
        if inc:
            self.cnt[eng] = val
        ev = (self.sem[eng], val)
        for b in reads:
            b.T.r.append(ev)
        for b in writes:
            b.T.w = ev
            b.T.r = []
        self.ops[eng].append((waits, fn, (self.sem[eng], 1) if inc else None))

    def dma(self, eng, out, in_, dst, src=None):
        deps = []
        if dst.dram:
            own = src.T
            if own.w is not None:
                deps.append(own.w)
        else:
            own = dst.T
            if src is not None and not src.dram and src.T.w is not None:
                deps.append(src.T.w)
            if own.w is not None:
                deps.append(own.w)
            deps.extend(own.r)
        waits = self._waits(eng, deps)
        sem = self._dsem(own)
        own.dcnt += 16
        ev = (sem, own.dcnt)
        if dst.dram:
            own.r.append(ev)
        else:
            if src is not None and not src.dram:
                src.T.r.append(ev)
            own.w = ev
            own.r = []
        self.ops[eng].append((waits, lambda e: e.dma_start(out=out, in_=in_), (sem, 16)))

    def barrier(self):
        evs = [(self.sem[e], self.cnt[e]) for e in self.ENGS if self.cnt[e] > 0]
        evs += [(t.dsem, t.dcnt) for t in self.dtiles if t.dcnt > 0]
        for e in self.ENGS:
            deps = [ev for ev in evs if ev[0] is not self.sem[e]]
            waits = self._waits(e, deps)
            if waits:
                self.ops[e].append((waits, None, None))
        for t in self.dtiles:
            self.free_dsems.append((t.dsem, t.dcnt))
            t.dsem = None
            t.w = None
            t.r = []
        self.dtiles = []

    def emit(self):
        nc = self.nc
        handles = {"pe": "tensor", "act": "scalar", "dve": "vector", "pool": "gpsimd", "sp": "sync"}
        with nc.Block() as block:
            for e in self.ENGS:
                ops = self.ops[e]

                def body(eng, ops=ops):
                    for waits, fn, inc in ops:
                        for sem, val in waits:
                            eng.wait_ge(sem, val)
                        if fn is not None:
                            ins = fn(eng)
                            if inc is not None:
                                ins.then_inc(inc[0], inc[1])
                getattr(block, handles[e])(body)


class KB:
    def __init__(self, nc, stack):
        self.nc = nc
        self.S = Sched(nc, stack)
        self.off = 18688
        self.uid = 0
        self.base = 18688

    def reset(self):
        self.off = self.base

    def sb(self, name, shape, dtype):
        nb = int(np.prod(shape[1:])) * (4 if dtype == F32 else 2)
        nb = (nb + 63) // 64 * 64
        self.uid += 1
        t = self.nc.alloc_sbuf_tensor_at("%s_%d" % (name, self.uid), list(shape), dtype, offset=self.off)
        self.off += nb
        assert self.off <= 229376, ("SBUF overflow", name, self.off)
        return Buf(t, name)

    def sbs(self, name, shape, dtype, n):
        return [self.sb(name + str(i), shape, dtype) for i in range(n)]

    def act(self, out, in_, func, r, w, scale=1.0, bias=0.0, accum=None):
        if accum is None:
            fn = lambda e: e.activation(out=out, in_=in_, func=func, bias=bias, scale=scale)
        else:
            fn = lambda e: e.activation(out=out, in_=in_, func=func, bias=bias, scale=scale, accum_out=accum)
        self.S.op("act", fn, r, w)

    def ts(self, eng, out, in0, s1, s2, op0, op1, r, w):
        if op1 is None:
            fn = lambda e: e.tensor_scalar(out=out, in0=in0, scalar1=s1, scalar2=None, op0=op0)
        else:
            fn = lambda e: e.tensor_scalar(out=out, in0=in0, scalar1=s1, scalar2=s2, op0=op0, op1=op1)
        self.S.op(eng, fn, r, w)

    def tt(self, eng, out, in0, in1, op, r, w):
        self.S.op(eng, lambda e: e.tensor_tensor(out=out, in0=in0, in1=in1, op=op), r, w)

    def stt(self, eng, out, in0, scalar, in1, op0, op1, r, w):
        self.S.op(eng, lambda e: e.scalar_tensor_tensor(out=out, in0=in0, scalar=scalar, in1=in1, op0=op0, op1=op1), r, w)

    def copy(self, eng, out, in_, r, w):
        if eng == "act":
            self.S.op("act", lambda e: e.activation(out=out, in_=in_, func=AF.Copy), r, w)
        else:
            self.S.op(eng, lambda e: e.tensor_copy(out=out, in_=in_), r, w)

    def memset(self, eng, ap, val, w):
        self.S.op(eng, lambda e: e.memset(ap, val), (), w)

    def mm(self, out, lhsT, rhs, start, stop, r, w, inc=None):
        if inc is None:
            inc = stop
        self.S.op("pe", lambda e: e.matmul(out, lhsT=lhsT, rhs=rhs, start=start, stop=stop), r, w, inc=inc)

    def tr(self, out, in_, ident, r, w, inc):
        self.S.op("pe", lambda e: e.transpose(out, in_, ident), r, w, inc=inc)

    def dma(self, eng, out, in_, dst, src=None):
        self.S.dma(eng, out, in_, dst, src)


def build(only=None, dbg=(), ext_in=()):
    nc = bass.Bass("TRN2", target_bir_lowering=False)
    st = contextlib.ExitStack()
    with st:
        return _build(nc, st, only, dbg, ext_in)


class _LazyIn:
    def __init__(self, nc, name, shape, used):
        self.nc, self.name, self.shape, self.used = nc, name, list(shape), used
        self._ap = None

    def ap(self):
        if self._ap is None:
            self._ap = self.nc.dram_tensor(self.name, self.shape, F32, kind="ExternalInput").ap()
            self.used.append(self.name)
        return self._ap

    def __getitem__(self, k):
        return self.ap()[k]

    def rearrange(self, *a, **kw):
        return self.ap().rearrange(*a, **kw)


def _build(nc, st, only, dbg, ext_in):
    used_inputs = []
    nc._used_inputs = used_inputs

    def din(name, shape, dt=F32):
        return _LazyIn(nc, name, shape, used_inputs)

    def dscr(name, shape, dt):
        kind = "ExternalOutput" if name in dbg else ("ExternalInput" if name in ext_in else "Internal")
        if kind == "ExternalInput":
            used_inputs.append(name)
        return Buf(nc.dram_tensor(name, list(shape), dt, kind=kind).ap(), name, dram=True)

    xin = din("xin", [NTOK, D])
    cvec = din("cvec", [128, 16, 2])
    w_mod = din("w_mod", [2, D, 6 * D])
    b_mod = din("b_mod", [2, 6 * D])
    norm_g = din("norm_g", [2, 4 * D])
    w_ff_in = din("w_ff_in", [2, D, DFF])
    w_ff_out = din("w_ff_out", [2, DFF, D])
    ar_w_in = din("ar_w_in", [D, ARIN])
    ar_w_out = din("ar_w_out", [D, D])
    qkg = din("qkg", [128, 2])
    ropeP = din("ropeP", [128, 128])
    cosT = din("cosT", [128, SEQ])
    sinT = din("sinT", [128, SEQ])
    taps = din("taps", [128, 8, 5])
    convb = din("convb", [128, 8])
    ar_wa = din("ar_wa", [2, 8, 128, 128])
    ar_wx = din("ar_wx", [2, 8, 128, 128])
    rnnp = din("rnnp", [128, 3, 2, 8])
    gm_w_in = din("gm_w_in", [D, 2 * D])
    gm_bu = din("gm_bu", [128, 16])
    gm_rows = din("gm_rows", [3, D])
    gm_w_sp = din("gm_w_sp", [16, 128, 128])
    gm_b_sp = din("gm_b_sp", [1, D])
    gm_w_out = din("gm_w_out", [D, D])
    out = Buf(nc.dram_tensor("out", [HALF, D], F32, kind="ExternalOutput").ap(), "out", dram=True)
    xin_b = Buf(xin, "xin", dram=True)

    modv = dscr("modv", [14, D], F32)
    hT0 = dscr("hT0", [16, 128, HALF], BF16)
    qT = dscr("qT", [8, 128, HALF], BF16)
    kT = dscr("kT", [2, 128, NTOK], BF16)
    Vs = dscr("Vs", [NTOK, 256], BF16)
    xrT = dscr("xrT", [8, 128, NTOK], F32)
    ggT = dscr("ggT", [8, 128, HALF], F32)
    catT = dscr("catT", [16, 128, HALF], BF16)
    x1s = dscr("x1s", [HALF, D], F32)
    x2s = dscr("x2s", [HALF, D], F32)
    hTs = dscr("hTs", [16, 128, HALF], BF16)
    uT = dscr("uT", [8, 128, 64, 256], BF16)
    ffo = dscr("ffo", [HALF, D], F32)

    K = KB(nc, st)
    S = K.S

    PS = [Buf(nc.alloc_psum_tensor("ps%d" % i, [128, 512], F32), "ps%d" % i) for i in range(8)]

    ident = K.sb("ident", [128, 128], BF16)
    ones = K.sb("ones", [128, 128], BF16)
    idf = K.sb("idf", [128, 128], F32)
    K.memset("pool", idf[:], 1.0, [idf])
    S.op("pool", lambda e: e.affine_select(out=idf[:], in_=idf[:], pattern=[[-1, 128]], compare_op=ALU.is_equal,
                                            fill=0.0, base=0, channel_multiplier=1), [idf], [idf])
    K.copy("dve", ident[:], idf[:], [idf], [ident])
    K.memset("pool", ones[:], 1.0, [ones])
    K.base = K.off

    def rstd_from_ssq(ssq_ap, out_ap, n, r, w):
        K.act(out_ap, ssq_ap, AF.Sqrt, r, w, scale=1.0 / n, bias=EPS)
        S.op("dve", lambda e: e.reciprocal(out=out_ap, in_=out_ap), w, w)

    def make_norm_bufs(pairs, n=2):
        return {"tmp": K.sbs("ntmp", [128, D], F32, n), "hb": K.sbs("nhb", [128, D], BF16, n),
                "junk": K.sb("njunk", [128, D], BF16), "ssq": K.sbs("nssq", [128, 2], F32, n),
                "rstd": K.sbs("nrstd", [128, 2], F32, n), "pairs": pairs, "i": 0}

    def norm_mod_transpose(xt, Abc, Bbc, NB, dst_ap, dst_buf):
        i = NB["i"]
        NB["i"] += 1
        n = len(NB["tmp"])
        tmp, hb, ssq, rstd, junk = NB["tmp"][i % n], NB["hb"][i % n], NB["ssq"][i % n], NB["rstd"][i % n], NB["junk"]
        pbanks = NB["pairs"][i % len(NB["pairs"])]
        K.act(junk[:], xt[:], AF.Square, [xt], [junk, ssq], accum=ssq[:, 0:1])
        rstd_from_ssq(ssq[:, 0:1], rstd[:, 0:1], D, [ssq], [rstd])
        K.stt("dve", tmp[:], xt[:], rstd[:, 0:1], Abc[:], ALU.mult, ALU.mult, [xt, rstd, Abc], [tmp])
        K.tt("pool", hb[:], tmp[:], Bbc[:], ALU.add, [tmp, Bbc], [hb])
        for half in range(2):
            pb = pbanks[half]
            pv = pb.t[:].bitcast(BF16)
            for kk in range(8):
                k = half * 8 + kk
                K.tr(pv[:, kk * 128:(kk + 1) * 128], hb[:, k * 128:(k + 1) * 128], ident[:], [hb, ident], [pb], inc=(kk == 7))
            K.copy("act", dst_ap[:, half * 8:(half + 1) * 8, :], pv.rearrange("p (k t) -> p k t", k=8), [pb], [dst_buf])

    def load_bcast(buf, row_ap, src):
        K.dma("sp", buf[:], row_ap.partition_broadcast(128), buf, src)

    def phase_mod():
        K.reset()
        cv = K.sb("cv", [128, 16, 2], F32)
        sv = K.sb("sv", [128, 16, 2], F32)
        wt = K.sbs("wmod", [128, 16, 512], F32, 2)
        raw = K.sb("raw", [2, 6 * D], F32)
        gg = K.sb("gg", [2, 4 * D], F32)
        K.dma("sp", cv[:], cvec.ap(), cv)
        K.act(sv[:], cv[:], AF.Silu, [cv], [sv])
        it = 0
        for l in range(2):
            K.dma("sp", raw[:], b_mod[l:l + 1, :].partition_broadcast(2), raw)
            K.dma("sp", gg[:], norm_g[l:l + 1, :].partition_broadcast(2), gg)
            for n in range(24):
                w = wt[it % 2]
                it += 1
                K.dma("sp", w[:], w_mod[l, :, n * 512:(n + 1) * 512].rearrange("(k p) n -> p k n", p=128), w)
                pb = PS[it % 2]
                for k in range(16):
                    K.mm(pb[0:2, :], sv[:, k, :], w[:, k, :], k == 0, k == 15, [sv, w], [pb])
                K.tt("dve", raw[:, n * 512:(n + 1) * 512], pb[0:2, :], raw[:, n * 512:(n + 1) * 512], ALU.add, [pb, raw], [raw])
            K.stt("dve", raw[:, D:2 * D], raw[:, D:2 * D], 1.0, gg[:, 0:D], ALU.add, ALU.mult, [raw, gg], [raw])
            K.tt("dve", raw[:, 2 * D:3 * D], raw[:, 2 * D:3 * D], gg[:, D:2 * D], ALU.mult, [raw, gg], [raw])
            K.stt("dve", raw[:, 4 * D:5 * D], raw[:, 4 * D:5 * D], 1.0, gg[:, 2 * D:3 * D], ALU.add, ALU.mult, [raw, gg], [raw])
            K.tt("dve", raw[:, 5 * D:6 * D], raw[:, 5 * D:6 * D], gg[:, 3 * D:4 * D], ALU.mult, [raw, gg], [raw])
            for r, src in enumerate((1, 0, 2, 4, 3, 5)):
                K.dma("sp", modv[6 * l + r:6 * l + r + 1, :], raw[0:1, src * D:(src + 1) * D], modv, raw)
            if l == 0:
                K.dma("sp", modv[12:13, :], raw[1:2, D:2 * D], modv, raw)
                K.dma("sp", modv[13:14, :], raw[1:2, 0:D], modv, raw)
        S.barrier()

    def phase_inproj(mode):
        K.reset()
        if mode == "qg":
            col0 = [0, 512, 2560, 3072]
        else:
            col0 = [1024, 1536, 2048]
        npan = len(col0)
        W = K.sb("Win", [128, 16, npan * 512], BF16)
        Wp = [Buf(W.t, "Winp%d" % i) for i in range(npan)]
        for p in range(npan):
            K.dma("pool", W[:, :, p * 512:(p + 1) * 512],
                  ar_w_in[:, col0[p]:col0[p] + 512].rearrange("(k p) n -> p k n", p=128), Wp[p])

        def wloc(ocol):
            for p in range(npan):
                if col0[p] <= ocol < col0[p] + 512:
                    return p * 512 + ocol - col0[p], p
            raise KeyError(ocol)
        Abc = K.sb("Abc", [128, D], F32)
        Bbc = K.sb("Bbc", [128, D], F32)
        xt = K.sbs("xt", [128, D], F32, 2)
        NB = make_norm_bufs([(PS[0], PS[1])])
        hT = K.sbs("hT", [128, 16, 512], BF16, 2)
        qk = K.sb("qkg", [128, 2], F32)
        Pm = K.sb("Pm", [128, 128], BF16)
        Pf = K.sb("Pf", [128, 128], F32)
        cs = K.sbs("cs", [128, 512], F32, 2)
        sn = K.sbs("sn", [128, 512], F32, 2)
        sq = K.sbs("sq", [128, 512], BF16, 2)
        qg = K.sbs("qg", [128, 512], BF16, 2)
        rs = K.sbs("rs", [128, 512], F32, 2)
        t1 = K.sbs("t1", [128, 512], F32, 2)
        t2 = K.sbs("t2", [128, 512], F32, 2)
        ob = K.sbs("ob", [128, 512], BF16, 3)
        of = K.sbs("of", [128, 512], F32, 3)
        vb = K.sbs("vb", [128, 256], BF16, 2)
        K.dma("sp", qk[:], qkg.ap(), qk)
        K.dma("sp", Pf[:], ropeP.ap(), Pf)
        K.copy("dve", Pm[:], Pf[:], [Pf], [Pm])
        load_bcast(Abc, modv[0:1, :], modv)
        load_bcast(Bbc, modv[1:2, :], modv)

        cnt = {"x": 0, "ps": 0, "o": 0, "f": 0, "v": 0, "qk": 0}

        def proj_cols(h, NT, n):
            lc, p = wloc(n * 128)
            pb = PS[2 + cnt["ps"] % 3]
            cnt["ps"] += 1
            for k in range(16):
                K.mm(pb[:, 0:NT], W[:, k, lc:lc + 128], h[:, k, 0:NT], k == 0, k == 15, [Wp[p], h], [pb])
            return pb

        def do_qk(pb, NT, gcol, rope_t0, dst_ap, dst_buf):
            i = cnt["qk"] % 2
            cnt["qk"] += 1
            K.act(sq[i][:, 0:NT], pb[:, 0:NT], AF.Square, [pb], [sq[i]])
            K.act(qg[i][:, 0:NT], pb[:, 0:NT], AF.Identity, [pb, qk], [qg[i]], scale=qk[:, gcol:gcol + 1])
            K.mm(PS[5][:, 0:NT], ones[:], sq[i][:, 0:NT], True, True, [ones, sq[i]], [PS[5]])
            rstd_from_ssq(PS[5][:, 0:NT], rs[i][:, 0:NT], 128, [PS[5]], [rs[i]])
            o = ob[cnt["o"] % 3]
            cnt["o"] += 1
            if rope_t0 is None:
                K.tt("pool", o[:, 0:NT], qg[i][:, 0:NT], rs[i][:, 0:NT], ALU.mult, [qg[i], rs[i]], [o])
            else:
                K.mm(PS[6][:, 0:NT], Pm[:], qg[i][:, 0:NT], True, True, [Pm, qg[i]], [PS[6]])
                K.tt("pool", t1[i][:, 0:NT], qg[i][:, 0:NT], cs[rope_t0][:, 0:NT], ALU.mult, [qg[i], cs[rope_t0]], [t1[i]])
                K.tt("dve", t2[i][:, 0:NT], PS[6][:, 0:NT], sn[rope_t0][:, 0:NT], ALU.mult, [PS[6], sn[rope_t0]], [t2[i]])
                K.tt("pool", t1[i][:, 0:NT], t1[i][:, 0:NT], t2[i][:, 0:NT], ALU.add, [t1[i], t2[i]], [t1[i]])
                K.tt("pool", o[:, 0:NT], t1[i][:, 0:NT], rs[i][:, 0:NT], ALU.mult, [t1[i], rs[i]], [o])
            K.dma("sp", dst_ap, o[:, 0:NT], dst_buf, o)

        for c in range(4 if mode == "qg" else 9):
            own = c < 4
            isctx = c == 8
            NT = 256 if isctx else 512
            t0 = c * 512
            h = hT[c % 2]
            if isctx:
                load_bcast(Abc, modv[12:13, :], modv)
                load_bcast(Bbc, modv[13:14, :], modv)
            if own and mode == "kvx":
                K.dma("sp", h[:], hT0[:, :, t0:t0 + 512].rearrange("k p t -> p k t"), h, hT0)
            else:
                for tt in range(NT // 128):
                    x = xt[cnt["x"] % 2]
                    cnt["x"] += 1
                    K.dma("sp", x[:], xin[t0 + tt * 128:t0 + (tt + 1) * 128, :], x)
                    norm_mod_transpose(x, Abc, Bbc, NB, h[:, :, tt * 128:(tt + 1) * 128], h)
            if not isctx:
                ci = c % 2
                K.dma("sp", cs[ci][:], cosT[:, t0:t0 + 512], cs[ci])
                K.dma("sp", sn[ci][:], sinT[:, t0:t0 + 512], sn[ci])
            ri = None if isctx else c % 2
            if mode == "qg":
                K.dma("sp", hT0[:, :, t0:t0 + 512].rearrange("k p t -> p k t"), h[:], hT0, h)
                for hd in range(8):
                    pb = proj_cols(h, NT, hd)
                    do_qk(pb, NT, 0, ri, qT[hd, :, t0:t0 + NT], qT)
                for n in range(8):
                    pb = proj_cols(h, NT, 20 + n)
                    o = of[cnt["f"] % 3]
                    cnt["f"] += 1
                    K.act(o[:, 0:NT], pb[:, 0:NT], AF.Gelu_apprx_tanh, [pb], [o])
                    K.dma("sp", ggT[n, :, t0:t0 + NT], o[:, 0:NT], ggT, o)
                continue
            for kv in range(2):
                pb = proj_cols(h, NT, 8 + kv)
                do_qk(pb, NT, 1, ri, kT[kv, :, t0:t0 + NT], kT)
            vl, vp = wloc(1280)
            for tt in range(NT // 128):
                pb = PS[2 + cnt["ps"] % 3]
                cnt["ps"] += 1
                for k in range(16):
                    K.mm(pb[:, 0:256], h[:, k, tt * 128:(tt + 1) * 128], W[:, k, vl:vl + 256], k == 0, k == 15, [h, Wp[vp]], [pb])
                v = vb[cnt["v"] % 2]
                cnt["v"] += 1
                K.copy("dve", v[:], pb[:, 0:256], [pb], [v])
                K.dma("sp", Vs[t0 + tt * 128:t0 + (tt + 1) * 128, :], v[:], Vs, v)
            for n in range(8):
                pb = proj_cols(h, NT, 12 + n)
                o = of[cnt["f"] % 3]
                cnt["f"] += 1
                K.copy("act", o[:, 0:NT], pb[:, 0:NT], [pb], [o])
                K.dma("sp", xrT[n, :, t0:t0 + NT], o[:, 0:NT], xrT, o)
        S.barrier()

    def phase_attn():
        K.reset()
        NKT = NTOK // 128
        kt_sb = K.sb("kTs", [128, 2, NTOK], BF16)
        v_sb = K.sb("Vsb", [128, NKT, 256], BF16)
        qs = K.sbs("qs", [128, 512], BF16, 3)
        pbuf = K.sbs("pexp", [128, 512], BF16, 4)
        rl = K.sbs("rl", [128, 512], F32, 2)
        ob = K.sbs("aob", [128, 512], BF16, 2)
        K.dma("sp", kt_sb[:], kT[:].rearrange("k p t -> p k t"), kt_sb, kT)
        v_h = [Buf(v_sb.t, "v_h%d" % i) for i in range(2)]
        for i in range(2):
            K.dma("sp", v_sb[:, i * 17:(i + 1) * 17, :],
                  Vs[i * 17 * 128:(i + 1) * 17 * 128, :].rearrange("(kt p) d -> p kt d", p=128), v_h[i], Vs)
        scale = 128.0 ** -0.5
        it = 0
        sc = 0
        def loadq(i):
            if i < 32:
                K.dma("sp", qs[i % 3][:], qT[i // 4, :, (i % 4) * 512:(i % 4 + 1) * 512], qs[i % 3], qT)
        loadq(0)
        loadq(1)
        for hd in range(8):
            kv = hd // 4
            for qt in range(4):
                q = qs[it % 3]
                loadq(it + 2)
                Ob = PS[4 + 2 * (it % 2)]
                Lb = PS[5 + 2 * (it % 2)]
                sbanks = {}

                def issue_s(kt, q=q, kv=kv):
                    nonlocal sc
                    pb = PS[sc % 4]
                    sc += 1
                    K.mm(pb[:], kt_sb[:, kv, kt * 128:(kt + 1) * 128], q[:], True, True, [kt_sb, q], [pb])
                    sbanks[kt] = pb
                issue_s(0)
                issue_s(1)
                for kt in range(NKT):
                    if kt + 2 < NKT:
                        issue_s(kt + 2)
                    pb = sbanks.pop(kt)
                    p = pbuf[(it * NKT + kt) % 4]
                    K.act(p[:], pb[:], AF.Exp, [pb], [p], scale=scale)
                    K.mm(Ob[:], v_sb[:, kt, kv * 128:(kv + 1) * 128], p[:], kt == 0, kt == NKT - 1, [v_h[kt // 17], p], [Ob], inc=(kt == NKT - 1))
                    K.mm(Lb[:], ones[:], p[:], kt == 0, kt == NKT - 1, [ones, p], [Lb], inc=True)
                r = rl[it % 2]
                o = ob[it % 2]
                S.op("dve", lambda e, r=r, Lb=Lb: e.reciprocal(out=r[:], in_=Lb[:]), [Lb], [r])
                K.tt("dve", o[:], Ob[:], r[:], ALU.mult, [Ob, r], [o])
                K.dma("sp", catT[hd, :, qt * 512:(qt + 1) * 512], o[:], catT, o)
                it += 1
        S.barrier()

    def phase_rnn():
        K.reset()
        tp = K.sb("taps", [128, 8, 5], F32)
        cb = K.sb("convb", [128, 8], F32)
        rp = K.sb("rnnp", [128, 3, 2, 8], F32)
        c1 = K.sb("c1", [128, 2, 8], F32)
        e1 = K.sb("e1", [128, 2, 8], F32)
        hb_ = K.sb("hbias", [128, 2, 2, 8], F32)
        waf = K.sbs("waf", [128, 128], F32, 2)
        wab = K.sbs("wab", [128, 128], BF16, 4)
        wxb = K.sbs("wxb", [128, 128], BF16, 4)
        xl = K.sbs("xl", [128, XL], F32, 1)
        xc = K.sbs("xc", [128, XC], F32, 2)
        accs = K.sbs("acc", [128, NTOK], F32, 2)
        accbs = K.sbs("accb", [128, NTOK], BF16, 2)
        NA = HALF + CTX
        dbuf = []
        for d, n_ in ((0, NA), (1, NTOK)):
            ent = {}
            for nm in ("a", "b", "s"):
                t = K.sb("g%s%d" % (nm, d), [128, n_], F32)
                ent[nm] = t
                ent[nm + "t"] = [Buf(t.t, "g%s%d_%d" % (nm, d, i)) for i in range(9)]
            dbuf.append(ent)
        rr = K.sbs("rr", [128, 512], F32, 2)
        ii = K.sbs("ii", [128, 512], F32, 2)
        hA = K.sb("hA", [128, HALF], F32)
        hB = K.sb("hB", [128, HALF], F32)
        hc = K.sb("hc", [128, CTX], F32)
        fin = K.sb("fin", [128, 2], F32)
        gg = K.sbs("gg", [128, HALF], F32, 1)
        ro = K.sbs("ro", [128, HALF], BF16, 2)
        K.dma("sp", tp[:], taps.ap(), tp)
        K.dma("sp", cb[:], convb.ap(), cb)
        K.dma("sp", rp[:], rnnp.ap(), rp)
        K.act(e1[:], rp[:, 2, :, :], AF.Exp, [rp], [e1], scale=-1.0)
        K.act(e1[:], e1[:], AF.Ln, [e1], [e1], bias=1.0)
        K.ts("dve", c1[:], e1[:], -4.0, None, ALU.mult, None, [e1], [c1])
        K.ts("dve", hb_[:], rp[:, 0:2, :, :], 0.5, None, ALU.mult, None, [rp], [hb_])
        for b_ in xl:
            K.memset("pool", b_[:, 0:2], 0.0, [b_])
            K.memset("pool", b_[:, XL - 2:XL], 0.0, [b_])
        for b_ in xc:
            K.memset("pool", b_[:, 0:2], 0.0, [b_])
            K.memset("pool", b_[:, XC - 2:XC], 0.0, [b_])
        wi = 0
        ti = 0
        gi = 0
        for n in range(8):
            X = xl[0]
            C = xc[n % 2]
            acc = accs[n % 2]
            accb = accbs[n % 2]
            K.dma("sp", X[:, 2:2 + SEQ], xrT[n, :, 0:SEQ], X, xrT)
            K.dma("sp", C[:, 2:2 + CTX], xrT[n, :, SEQ:NTOK], C, xrT)
            for (src, L, o0) in ((X, SEQ, 0), (C, CTX, SEQ)):
                K.act(acc[:, o0:o0 + L], src[:, 0:L], AF.Identity, [src, tp, cb], [acc], scale=tp[:, n, 0:1], bias=cb[:, n:n + 1])
                for j in range(1, 5):
                    K.stt("dve", acc[:, o0:o0 + L], src[:, j:j + L], tp[:, n, j:j + 1], acc[:, o0:o0 + L], ALU.mult, ALU.add, [src, tp, acc], [acc])
            K.copy("act", accb[:], acc[:], [acc], [accb])
            for d in range(2):
                E = dbuf[d]
                wA = wab[wi % 4]
                wX = wxb[wi % 4]
                wi += 1
                for (srcw, dstw) in ((ar_wa, wA), (ar_wx, wX)):
                    f = waf[ti % 2]
                    ti += 1
                    K.dma("sp", f[:], srcw[d, n, :, :], f)
                    K.copy("dve", dstw[:], f[:], [f], [dstw])
                nlat = HALF if d == 0 else SEQ
                tiles = [(SEQ, nlat, CTX)] + [(c0, c0, 512) for c0 in range(0, nlat, 512)]
                for (c0, l0, NT) in tiles:
                    i = gi % 2
                    gi += 1
                    lt = l0 // 512
                    K.mm(PS[0 + i][:, 0:NT], wA[:], accb[:, c0:c0 + NT], True, True, [wA, accb], [PS[0 + i]])
                    K.mm(PS[2 + i][:, 0:NT], wX[:], accb[:, c0:c0 + NT], True, True, [wX, accb], [PS[2 + i]])
                    K.act(rr[i][:, 0:NT], PS[0 + i][:, 0:NT], AF.Tanh, [PS[0 + i], hb_], [rr[i]], scale=0.5, bias=hb_[:, 0, d, n:n + 1])
                    K.act(ii[i][:, 0:NT], PS[2 + i][:, 0:NT], AF.Tanh, [PS[2 + i], hb_], [ii[i]], scale=0.5, bias=hb_[:, 1, d, n:n + 1])
                    K.act(E["a"][:, l0:l0 + NT], rr[i][:, 0:NT], AF.Exp, [rr[i], c1], [E["at"][lt]], scale=c1[:, d, n:n + 1], bias=c1[:, d, n:n + 1])
                    K.tt("pool", E["s"][:, l0:l0 + NT], E["a"][:, l0:l0 + NT], E["a"][:, l0:l0 + NT], ALU.mult, [E["at"][lt]], [E["st"][lt]])
                    K.stt("dve", E["b"][:, l0:l0 + NT], ii[i][:, 0:NT], 1.0, acc[:, c0:c0 + NT], ALU.add, ALU.mult, [ii[i], acc], [E["bt"][lt]])
                nt = nlat // 512
                for (l0, L, tl) in ((0, nlat, list(range(nt))), (nlat, CTX, [nt])):
                    st_ = [E["st"][t] for t in tl]
                    bt_ = [E["bt"][t] for t in tl]
                    K.act(E["s"][:, l0:l0 + L], E["s"][:, l0:l0 + L], AF.Sqrt, st_, st_, scale=-0.25, bias=0.25)
                    K.tt("pool", E["b"][:, l0:l0 + L], E["b"][:, l0:l0 + L], E["s"][:, l0:l0 + L], ALU.mult, st_ + bt_, bt_)
                A_, B_ = E["a"], E["b"]
                at, bt = E["at"], E["bt"]
                if d == 0:
                    S.op("dve", lambda e, A_=A_, B_=B_: e.tensor_tensor_scan(out=hc[:, :], data0=A_[:, HALF:NA], data1=B_[:, HALF:NA],
                                                                               initial=0.0, op0=ALU.mult, op1=ALU.add), [at[4], bt[4]], [hc])
                    S.op("dve", lambda e, A_=A_, B_=B_: e.tensor_tensor_scan(out=hA[:, :], data0=A_[:, 0:HALF], data1=B_[:, 0:HALF],
                                                                               initial=hc[:, CTX - 1:CTX], op0=ALU.mult, op1=ALU.add), at[0:4] + bt[0:4] + [hc], [hA])
                else:
                    S.op("dve", lambda e, A_=A_, B_=B_: e.tensor_tensor_scan(out=hc[:, ::-1], data0=A_[:, SEQ:NTOK][:, ::-1], data1=B_[:, SEQ:NTOK][:, ::-1],
                                                                               initial=0.0, op0=ALU.mult, op1=ALU.add), [at[8], bt[8]], [hc])
                    S.op("dve", lambda e, A_=A_, B_=B_: e.tensor_tensor_scan(out=hB[:, ::-1], data0=A_[:, HALF:SEQ][:, ::-1], data1=B_[:, HALF:SEQ][:, ::-1],
                                                                               initial=hc[:, 0:1], op0=ALU.mult, op1=ALU.add), at[4:8] + bt[4:8] + [hc], [hB])
                    K.copy("dve", fin[:, 0:1], hB[:, 0:1], [hB], [fin])
                    S.op("dve", lambda e, A_=A_, B_=B_: e.tensor_tensor_scan(out=hB[:, ::-1], data0=A_[:, 0:HALF][:, ::-1], data1=B_[:, 0:HALF][:, ::-1],
                                                                               initial=fin[:, 0:1], op0=ALU.mult, op1=ALU.add), at[0:4] + bt[0:4] + [fin], [hB])
            G = gg[0]
            K.dma("sp", G[:], ggT[n, :, :], G, ggT)
            K.tt("pool", hA[:], hA[:], hB[:], ALU.add, [hA, hB], [hA])
            R = ro[n % 2]
            K.tt("pool", R[:], hA[:], G[:], ALU.mult, [hA, G], [R])
            K.dma("sp", catT[8 + n, :, :], R[:], catT, R)
        S.barrier()

    def make_epi_bufs(from_psum):
        B = {}
        B["Gbc"] = K.sb("Gbc", [128, D], F32)
        B["Abc"] = K.sb("Abc", [128, D], F32)
        B["Bbc"] = K.sb("Bbc", [128, D], F32)
        B["xt"] = K.sbs("ext", [128, D], F32, 3)
        if from_psum:
            B["raw"] = K.sbs("eraw", [128, D], F32, 1)
        else:
            B["ft"] = K.sbs("ft", [128, D], F32, 3)
        B["tmp1"] = K.sb("etmp1", [128, D], F32)
        B["tmp2"] = K.sb("etmp2", [128, D], F32)
        B["hb"] = K.sbs("ehb", [128, D], BF16, 2)
        B["junk"] = K.sb("ejunk", [128, D], BF16)
        B["st"] = K.sbs("est", [128, 4], F32, 3)
        B["hst"] = K.sbs("ehst", [128, 16, 512], BF16, 1 if from_psum else 2)
        B["pairs"] = [(PS[4], PS[5]), (PS[6], PS[7])]
        return B

    def epi_load(B, t, xsrc, fsrc=None):
        if t >= 16:
            return
        x = B["xt"][t % 3]
        K.dma("sp", x[:], xsrc[t * 128:(t + 1) * 128, :], x, xsrc)
        if fsrc is not None:
            f = B["ft"][t % 3]
            K.dma("sp", f[:], fsrc[t * 128:(t + 1) * 128, :], f, fsrc)

    def epi_s1(B, t, raw_aps, raw_bufs, from_psum, xdst):
        st, junk, tmp = B["st"][t % 3], B["junk"], B["tmp1"]
        x = B["xt"][t % 3]
        if from_psum:
            raw = B["raw"][0]
            for fb in range(4):
                K.copy("act", raw[:, fb * 512:(fb + 1) * 512], raw_aps[fb], [raw_bufs[fb]], [raw])
            rap = raw[:]
        else:
            raw = raw_bufs[0]
            rap = raw_aps[0]
        K.act(junk[:], rap, AF.Square, [raw], [junk, st], accum=st[:, 0:1])
        rstd_from_ssq(st[:, 0:1], st[:, 1:2], D, [st], [st])
        K.stt("dve", tmp[:], rap, st[:, 1:2], B["Gbc"][:], ALU.mult, ALU.mult, [raw, st, B["Gbc"]], [tmp])
        K.tt("pool", x[:], tmp[:], x[:], ALU.add, [tmp, x], [x])
        K.dma("sp", xdst[t * 128:(t + 1) * 128, :], x[:], xdst, x)

    def epi_s2(B, t):
        st, junk, tmp = B["st"][t % 3], B["junk"], B["tmp2"]
        x = B["xt"][t % 3]
        hb = B["hb"][t % 2]
        K.act(junk[:], x[:], AF.Square, [x], [junk, st], accum=st[:, 2:3])
        rstd_from_ssq(st[:, 2:3], st[:, 3:4], D, [st], [st])
        K.stt("dve", tmp[:], x[:], st[:, 3:4], B["Abc"][:], ALU.mult, ALU.mult, [x, st, B["Abc"]], [tmp])
        K.tt("dve", hb[:], tmp[:], B["Bbc"][:], ALU.add, [tmp, B["Bbc"]], [hb])

    def epi_s3(B, t, hdst):
        hb = B["hb"][t % 2]
        hst = B["hst"][(t // 4) % len(B["hst"])]
        tt = t % 4
        pbanks = B["pairs"][t % 2]
        for half in range(2):
            pb = pbanks[half]
            pv = pb.t[:].bitcast(BF16)
            for kk in range(8):
                k = half * 8 + kk
                K.tr(pv[:, kk * 128:(kk + 1) * 128], hb[:, k * 128:(k + 1) * 128], ident[:], [hb, ident], [pb], inc=(kk == 7))
            K.copy("act", hst[:, half * 8:(half + 1) * 8, tt * 128:(tt + 1) * 128], pv.rearrange("p (k t) -> p k t", k=8), [pb], [hst])
        if tt == 3:
            c0 = (t - 3) * 128
            K.dma("sp", hdst[:, :, c0:c0 + 512].rearrange("k p t -> p k t"), hst[:], hdst, hst)

    def epi_tail(B, t, do_norm, hdst):
        if not do_norm:
            return
        if 0 <= t - 1 < 16:
            epi_s2(B, t - 1)
        if 0 <= t - 2 < 16:
            epi_s3(B, t - 2, hdst)

    def phase_outproj(Wd, mrow, xsrc, xdst, hdst):
        K.reset()
        W = K.sb("Wout", [128, 16, D], BF16)
        Wp = [Buf(W.t, "Woutp%d" % i) for i in range(4)]
        for p in range(4):
            K.dma("pool", W[:, :, p * 512:(p + 1) * 512], Wd[:, p * 512:(p + 1) * 512].rearrange("(k p) n -> p k n", p=128), Wp[p])
        B = make_epi_bufs(True)
        load_bcast(B["Gbc"], modv[mrow + 2:mrow + 3, :], modv)
        load_bcast(B["Abc"], modv[mrow + 3:mrow + 4, :], modv)
        load_bcast(B["Bbc"], modv[mrow + 4:mrow + 5, :], modv)
        cat = K.sbs("cat", [128, 16, 512], BF16, 2)

        def loadcat(c):
            K.dma("sp", cat[c % 2][:], catT[:, :, c * 512:(c + 1) * 512].rearrange("k p t -> p k t"), cat[c % 2], catT)
        loadcat(0)
        epi_load(B, 0, xsrc)
        epi_load(B, 1, xsrc)
        for t in range(18):
            if t < 16:
                c, tt = t // 4, t % 4
                cc = cat[c % 2]
                if tt == 0 and c + 1 < 4:
                    loadcat(c + 1)
                for k in range(16):
                    for fb in range(4):
                        K.mm(PS[fb][:], cc[:, k, tt * 128:(tt + 1) * 128], W[:, k, fb * 512:(fb + 1) * 512],
                             k == 0, k == 15, [cc, Wp[fb]], [PS[fb]])
                epi_s1(B, t, [PS[fb][:] for fb in range(4)], [PS[fb] for fb in range(4)], True, xdst)
            epi_tail(B, t, True, hdst)
            epi_load(B, t + 2, xsrc)
        S.barrier()

    def phase_up(Wd, ncols, hsrc, dst, kind, bias_d=None):
        K.reset()
        h = K.sb("hres", [128, 16, HALF], BF16)
        hp = [Buf(h.t, "hresp%d" % i) for i in range(4)]
        for c in range(4):
            K.dma("sp", h[:, :, c * 512:(c + 1) * 512], hsrc[:, :, c * 512:(c + 1) * 512].rearrange("k p t -> p k t"), hp[c], hsrc)
        Wt = K.sbs("Wup", [128, 16, 512], BF16, 3)
        rl = K.sbs("rl", [128, 512], F32, 3)
        ust = K.sbs("ust", [128, HALF], BF16, 3)
        bias = None
        if bias_d is not None:
            bias = K.sb("bias", [128, ncols // 128], F32)
            K.dma("sp", bias[:], bias_d.ap(), bias)
        npan = ncols // 512

        def loadW(pn):
            K.dma("pool", Wt[pn % 3][:], Wd[:, pn * 512:(pn + 1) * 512].rearrange("(k p) n -> p k n", p=128), Wt[pn % 3])
        loadW(0)
        loadW(1)
        pi = 0
        for pn in range(npan):
            if pn + 2 < npan:
                loadW(pn + 2)
            Wb = Wt[pn % 3]
            for jc in range(4):
                j = pn * 4 + jc
                u = ust[j % 3]
                for c in range(4):
                    pb = PS[pi % 8]
                    pi += 1
                    for k in range(16):
                        K.mm(pb[:], Wb[:, k, jc * 128:(jc + 1) * 128], h[:, k, c * 512:(c + 1) * 512], k == 0, k == 15, [Wb, hp[c]], [pb])
                    if kind == "relu2":
                        r = rl[pi % 3]
                        K.ts("dve", r[:], pb[:], 0.0, None, ALU.max, None, [pb], [r])
                        K.act(u[:, c * 512:(c + 1) * 512], r[:], AF.Square, [r], [u])
                    else:
                        K.act(u[:, c * 512:(c + 1) * 512], pb[:], AF.Gelu_apprx_tanh, [pb, bias], [u], bias=bias[:, j:j + 1])
                if kind == "relu2":
                    K.dma("act", dst[:, :, j, :].rearrange("c p t -> p c t"), u[:].rearrange("p (c t) -> p c t", c=8), dst, u)
                else:
                    K.dma("act", dst[j, :, :], u[:], dst, u)
        S.barrier()

    def phase_down(Wd):
        K.reset()
        Wq = K.sbs("W2q", [128, 64, 512], BF16, 2)
        Wqp = [[Buf(Wq[i].t, "W2q%dp%d" % (i, p)) for p in range(4)] for i in range(2)]
        ut = K.sbs("ut", [128, 64, 256], BF16, 2)
        utp = [[Buf(ut[i].t, "ut%dp%d" % (i, p)) for p in range(4)] for i in range(2)]
        of = K.sbs("dof", [128, 512], F32, 3)

        def loadW(q):
            for p in range(4):
                K.dma("pool", Wq[q % 2][:, p * 16:(p + 1) * 16, :],
                      Wd[p * 2048:(p + 1) * 2048, q * 512:(q + 1) * 512].rearrange("(j p) n -> p j n", p=128), Wqp[q % 2][p])

        def loadU(ui):
            tc = ui % 8
            for p in range(4):
                K.dma("sp", ut[ui % 2][:, p * 16:(p + 1) * 16, :], uT[tc, :, p * 16:(p + 1) * 16, :], utp[ui % 2][p], uT)
        loadW(0)
        loadU(0)
        oi = 0
        for q in range(4):
            Wb = Wq[q % 2]
            for tc in range(8):
                ui = q * 8 + tc
                if ui + 1 < 32:
                    loadU(ui + 1)
                if tc == 1 and q + 1 < 4:
                    loadW(q + 1)
                U = ut[ui % 2]
                Up = utp[ui % 2]
                for tt in range(2):
                    pb = PS[oi % 4]
                    for j in range(64):
                        K.mm(pb[:], U[:, j, tt * 128:(tt + 1) * 128], Wb[:, j, :], j == 0, j == 63, [Up[j // 16], Wqp[q % 2][j // 16]], [pb])
                    o = of[oi % 3]
                    oi += 1
                    K.copy("act", o[:], pb[:], [pb], [o])
                    t0 = tc * 256 + tt * 128
                    K.dma("act", ffo[t0:t0 + 128, q * 512:(q + 1) * 512], o[:], ffo, o)
        S.barrier()

    def phase_ffn_epi(mrow, xsrc, xdst, do_norm, nrow, hdst):
        K.reset()
        B = make_epi_bufs(False)
        load_bcast(B["Gbc"], modv[mrow + 5:mrow + 6, :], modv)
        if do_norm:
            load_bcast(B["Abc"], modv[nrow:nrow + 1, :], modv)
            load_bcast(B["Bbc"], modv[nrow + 1:nrow + 2, :], modv)
        epi_load(B, 0, xsrc, ffo)
        epi_load(B, 1, xsrc, ffo)
        for t in range(18):
            if t < 16:
                f = B["ft"][t % 3]
                epi_s1(B, t, [f[:]], [f], False, xdst)
            epi_tail(B, t, do_norm, hdst)
            epi_load(B, t + 2, xsrc, ffo)
        S.barrier()

    def phase_gm_v():
        K.reset()
        W = K.sb("Wv", [128, 16, D], BF16)
        Wp = [Buf(W.t, "Wvp%d" % i) for i in range(4)]
        for p in range(4):
            K.dma("pool", W[:, :, p * 512:(p + 1) * 512], gm_w_in[:, D + p * 512:D + (p + 1) * 512].rearrange("(k p) n -> p k n", p=128), Wp[p])
        bvb = K.sb("bvb", [128, D], F32)
        vgb = K.sb("vgb", [128, D], F32)
        vbb = K.sb("vbb", [128, D], F32)
        gr = None
        load_bcast(bvb, gm_rows[0:1, :], gr)
        load_bcast(vgb, gm_rows[1:2, :], gr)
        load_bcast(vbb, gm_rows[2:3, :], gr)
        spf = K.sb("spf", [128, 16, 128], F32)
        spb = K.sb("spb", [128, 16, 128], BF16)
        spT = K.sb("spT", [128, 16, 128], BF16)
        bsf = K.sb("bsf", [1, D], F32)
        bsb = K.sb("bsb", [1, D], BF16)
        K.dma("sp", spf[:], gm_w_sp.rearrange("g p q -> p g q"), spf)
        K.copy("dve", spb[:], spf[:], [spf], [spb])
        for half in range(2):
            pb = PS[half]
            pv = pb.t[:].bitcast(BF16)
            for gg_ in range(8):
                g = half * 8 + gg_
                K.tr(pv[:, gg_ * 128:(gg_ + 1) * 128], spb[:, g, :], ident[:], [spb, ident], [pb], inc=(gg_ == 7))
            K.copy("act", spT[:, half * 8:(half + 1) * 8, :], pv.rearrange("p (g t) -> p g t", g=8), [pb], [spT])
        K.dma("sp", bsf[:], gm_b_sp.ap(), bsf)
        K.copy("dve", bsb[:], bsf[:], [bsf], [bsb])
        hT = K.sbs("hTg", [128, 16, 512], BF16, 2)
        gu = K.sbs("gu", [128, 16, 512], BF16, 1)
        pst = K.sbs("pst", [128, 16, 512], BF16, 1)
        v = K.sb("v", [128, D], F32)
        vg = K.sb("vg", [128, D], F32)
        junk = K.sb("junk", [128, D], BF16)
        vln = K.sb("vln", [128, D], BF16)
        st4 = K.sb("st4", [128, 8], F32)
        for c in range(4):
            h = hT[c % 2]
            G = gu[0]
            P = pst[0]
            K.dma("sp", h[:], hTs[:, :, c * 512:(c + 1) * 512].rearrange("k p t -> p k t"), h, hTs)
            K.dma("sp", G[:], ggTu[:, :, c * 512:(c + 1) * 512].rearrange("k p t -> p k t"), G, ggTu)
            for tt in range(4):
                ts_ = slice(tt * 128, (tt + 1) * 128)
                for k in range(16):
                    for fb in range(4):
                        K.mm(PS[fb][:], h[:, k, ts_], W[:, k, fb * 512:(fb + 1) * 512], k == 0, k == 15, [h, Wp[fb]], [PS[fb]])
                for fb in range(4):
                    fs = slice(fb * 512, (fb + 1) * 512)
                    K.tt("dve", v[:, fs], PS[fb][:], bvb[:, fs], ALU.add, [PS[fb], bvb], [v])
                K.act(vg[:], v[:], AF.Gelu_apprx_tanh, [v], [vg, st4], accum=st4[:, 0:1])
                K.act(junk[:], vg[:], AF.Square, [vg], [junk, st4], accum=st4[:, 1:2])
                K.ts("dve", st4[:, 2:3], st4[:, 0:1], 1.0 / D, None, ALU.mult, None, [st4], [st4])
                K.tt("dve", st4[:, 3:4], st4[:, 2:3], st4[:, 2:3], ALU.mult, [st4], [st4])
                K.stt("dve", st4[:, 4:5], st4[:, 1:2], 1.0 / D, st4[:, 3:4], ALU.mult, ALU.subtract, [st4], [st4])
                rstd_from_ssq(st4[:, 4:5], st4[:, 5:6], 1.0, [st4], [st4])
                K.stt("dve", st4[:, 6:7], st4[:, 2:3], -1.0, st4[:, 5:6], ALU.mult, ALU.mult, [st4], [st4])
                K.act(v[:], vg[:], AF.Identity, [vg, st4], [v], scale=st4[:, 5:6], bias=st4[:, 6:7])
                K.tt("dve", vg[:], v[:], vgb[:], ALU.mult, [v, vgb], [vg])
                K.tt("pool", vln[:], vg[:], vbb[:], ALU.add, [vg, vbb], [vln])
                for g in range(16):
                    pb = PS[4 + g // 4]
                    oc = slice((g % 4) * 128, (g % 4 + 1) * 128)
                    K.mm(pb[:, oc], vln[:, g * 128:(g + 1) * 128], spT[:, g, :], True, False, [vln, spT], [pb], inc=False)
                    K.mm(pb[:, oc], ones[0:1, :], bsb[0:1, g * 128:(g + 1) * 128], False, True, [ones, bsb], [pb], inc=(g % 4 == 3))
                for gq in range(4):
                    K.tt("dve", P[:, gq * 4:(gq + 1) * 4, ts_], PS[4 + gq][:].rearrange("p (g t) -> p g t", g=4),
                         G[:, gq * 4:(gq + 1) * 4, ts_], ALU.mult, [PS[4 + gq], G], [P])
            K.dma("sp", catT[:, :, c * 512:(c + 1) * 512].rearrange("k p t -> p k t"), P[:], catT, P)
        S.barrier()

    ggTu = dscr("ggTu", [16, 128, HALF], BF16)

    phases = [
        ("mod", phase_mod),
        ("inproj", lambda: (phase_inproj("qg"), phase_inproj("kvx"))),
        ("attn", phase_attn),
        ("rnn", phase_rnn),
        ("outproj0", lambda: phase_outproj(ar_w_out, 0, xin_b, x1s, hTs)),
        ("up0", lambda: phase_up(w_ff_in[0], DFF, hTs, uT, "relu2")),
        ("down0", lambda: phase_down(w_ff_out[0])),
        ("epi0", lambda: phase_ffn_epi(0, x1s, x2s, True, 6, hTs)),
        ("gmu", lambda: phase_up(gm_w_in, D, hTs, ggTu, "gelu", gm_bu)),
        ("gmv", phase_gm_v),
        ("outproj1", lambda: phase_outproj(gm_w_out, 6, x2s, x1s, hTs)),
        ("up1", lambda: phase_up(w_ff_in[1], DFF, hTs, uT, "relu2")),
        ("down1", lambda: phase_down(w_ff_out[1])),
        ("epi1", lambda: phase_ffn_epi(6, x1s, out, False, 0, None)),
    ]
    for i, (name, fn) in enumerate(phases):
        if only is not None and name not in only:
            continue
        fn()
    S.barrier()
    S.emit()
    return nc


def _rope_tables():
    rows = SEQ // 64
    r_idx, c_idx = np.meshgrid(np.arange(rows), np.arange(64), indexing="ij")
    r_idx = r_idx.reshape(-1).astype(np.float32)
    c_idx = c_idx.reshape(-1).astype(np.float32)
    freqs = (np.float32(10000.0) ** (-np.arange(32, dtype=np.float32) / np.float32(32))).astype(np.float32)
    ang_r = r_idx[:, None] * freqs
    ang_c = c_idx[:, None] * freqs
    ang = np.concatenate([ang_r, ang_r, ang_c, ang_c], axis=1)
    cosT = np.ascontiguousarray(np.cos(ang).T.astype(np.float32))
    sinT = np.ascontiguousarray(np.sin(ang).T.astype(np.float32))
    Pm = np.zeros((128, 128), np.float32)
    for d in range(128):
        blk = d // 32
        if blk % 2 == 0:
            Pm[d, d + 32] = -1.0
        else:
            Pm[d, d - 32] = 1.0
    ropeP = np.ascontiguousarray(Pm.T)
    return cosT, sinT, ropeP


def make_in_maps(inp, cores=range(8)):
    f = lambda a: np.ascontiguousarray(a, dtype=np.float32)
    cosT, sinT, ropeP = _rope_tables()
    shared = {
        "w_mod": f(inp["w_mod"]), "b_mod": f(inp["b_mod"]), "norm_g": f(inp["norm_g"].reshape(2, 4 * D)),
        "w_ff_in": f(inp["w_ff_in"]), "w_ff_out": f(inp["w_ff_out"]),
        "ar_w_in": f(inp["ar_w_in"][0]), "ar_w_out": f(inp["ar_w_out"][0]),
        "qkg": f(np.stack([inp["ar_q_g"][0], inp["ar_k_g"][0]], axis=1)),
        "ropeP": ropeP,
        "convb": f(inp["ar_conv_b"][0].reshape(8, 128).T),
        "gm_w_in": f(inp["gm_w_in"][0]),
        "gm_bu": f(inp["gm_b_in"][0][:D].reshape(16, 128).T),
        "gm_rows": f(np.stack([inp["gm_b_in"][0][D:], inp["gm_v_g"][0], inp["gm_v_b"][0]], axis=0)),
        "gm_w_out": f(inp["gm_w_out"][0]),
    }
    cw = inp["ar_conv_w"][0]
    maps = []
    for core in cores:
        b, half = core // 2, core % 2
        m = dict(shared)
        x = inp["x"][b]
        cx = inp["ctx"][b]
        if half == 0:
            xin = np.concatenate([x[:HALF], x[HALF:], cx], axis=0)
            cs, sn = cosT, sinT
            tp = np.stack([cw[0], cw[1], cw[2], cw[3], np.zeros_like(cw[0])], axis=1)
            dsel = [0, 1]
            wsp = inp["gm_w_sp"][0]
            bsp = inp["gm_b_sp"][0]
        else:
            xr = x[::-1]
            xin = np.concatenate([xr[:HALF], xr[HALF:], cx[::-1]], axis=0)
            cs, sn = cosT[:, ::-1], sinT[:, ::-1]
            tp = np.stack([np.zeros_like(cw[0]), cw[3], cw[2], cw[1], cw[0]], axis=1)
            dsel = [1, 0]
            wsp = inp["gm_w_sp"][0][:, ::-1, ::-1]
            bsp = inp["gm_b_sp"][0][:, ::-1]
        m["xin"] = f(xin)
        m["cosT"] = f(cs)
        m["sinT"] = f(sn)
        m["taps"] = f(tp.reshape(8, 128, 5).transpose(1, 0, 2))
        m["cvec"] = f(np.stack([inp["c"][b].reshape(16, 128).T, inp["c_ctx"].reshape(16, 128).T], axis=2))
        m["ar_wa"] = f(inp["ar_wa"][0][dsel])
        m["ar_wx"] = f(inp["ar_wx"][0][dsel])
        pr = np.stack([inp["ar_ba"][0][dsel], inp["ar_bx"][0][dsel], inp["ar_lambda"][0][dsel]], axis=0)
        m["rnnp"] = f(pr.reshape(3, 2, 8, 128).transpose(3, 0, 1, 2))
        m["gm_w_sp"] = f(wsp)
        m["gm_b_sp"] = f(bsp.reshape(1, D))
        maps.append(m)
    return maps


_NC_CACHE = {}


def kernel(**inputs):
    inp = {k: np.asarray(v) for k, v in inputs.items()}
    if "nc" not in _NC_CACHE:
        _NC_CACHE["nc"] = build()
    nc = _NC_CACHE["nc"]
    maps = make_in_maps(inp)
    res = run_bass_kernel_spmd(nc, maps, core_ids=list(range(8)))
    out = np.empty((4, SEQ, D), np.float32)
    for core in range(8):
        b, half = core // 2, core % 2
        o = np.asarray(res.results[core]["out"], dtype=np.float32)
        if half == 0:
            out[b, :HALF] = o
        else:
            out[b, HALF:] = o[::-1]
    return out
```

```python
import contextlib
import numpy as np
import concourse.bass as bass
import concourse.mybir as mybir
from concourse.bass_utils import run_bass_kernel_spmd

F32 = mybir.dt.float32
BF16 = mybir.dt.bfloat16
AF = mybir.ActivationFunctionType
ALU = mybir.AluOpType
AX = mybir.AxisListType

D = 2048
SEQ = 4096
HALF = 2048
CTX = 256
NTOK = SEQ + CTX
DFF = 8192
ARIN = 3584
EPS = 1e-6
XL = 4 + SEQ
XC = 4 + CTX


class Tile:
    __slots__ = ("name", "w", "r", "dsem", "dcnt")

    def __init__(self, name):
        self.name = name
        self.w = None
        self.r = []
        self.dsem = None
        self.dcnt = 0


class Buf:
    def __init__(self, t, name, dram=False):
        self.t = t
        self.T = Tile(name)
        self.dram = dram

    def __getitem__(self, k):
        return self.t[k]


class Sched:
    ENGS = ("pe", "act", "dve", "pool", "sp")

    def __init__(self, nc, stack):
        self.nc = nc
        self.stack = stack
        self.ops = {e: [] for e in self.ENGS}
        self.sem = {e: stack.enter_context(nc.semaphore("s_" + e)) for e in self.ENGS}
        self.cnt = {e: 0 for e in self.ENGS}
        self.waited = {e: {} for e in self.ENGS}
        self.dtiles = []
        self.free_dsems = []
        self.nsem = 0
        self.pool_fifo = []

    def _dsem(self, t):
        if t.dsem is None:
            if self.free_dsems:
                t.dsem, t.dcnt = self.free_dsems.pop()
            else:
                self.nsem += 1
                t.dsem = self.stack.enter_context(self.nc.semaphore("d%d" % self.nsem))
                t.dcnt = 0
            self.dtiles.append(t)
        return t.dsem

    def _waits(self, eng, deps):
        need = {}
        for (sem, val) in deps:
            if eng == "pe" and sem is self.sem["pe"]:
                continue
            if need.get(sem, 0) < val:
                need[sem] = val
        out = []
        wd = self.waited[eng]
        for sem, val in need.items():
            if wd.get(sem, 0) < val:
                wd[sem] = val
                out.append((sem, val))
        return out

    def op(self, eng, fn, reads=(), writes=(), inc=True):
        deps = []
        for b in reads:
            t = b.T
            if t.w is not None:
                deps.append(t.w)
        for b in writes:
            t = b.T
            if t.w is not None:
                deps.append(t.w)
            deps.extend(t.r)
        waits = self._waits(eng, deps)
        val = self.cnt[eng] + 1
        if inc:
            self.cnt[eng] = val
        ev = (self.sem[eng], val)
        for b in reads:
            b.T.r.append(ev)
        for b in writes:
            b.T.w = ev
            b.T.r = []
        self.ops[eng].append((waits, fn, (self.sem[eng], 1) if inc else None))

    def dma(self, eng, out, in_, dst, src=None):
        deps = []
        if dst.dram:
            own = src.T
            if own.w is not None:
                deps.append(own.w)
        else:
            own = dst.T
            if src is not None and not src.dram and src.T.w is not None:
                deps.append(src.T.w)
            if own.w is not None:
                deps.append(own.w)
            deps.extend(own.r)
        if eng == "pool":
            while len(self.pool_fifo) >= 3:
                deps.append(self.pool_fifo.pop(0))
        waits = self._waits(eng, deps)
        sem = self._dsem(own)
        own.dcnt += 16
        ev = (sem, own.dcnt)
        if eng == "pool":
            self.pool_fifo.append(ev)
        if dst.dram:
            own.r.append(ev)
        else:
            if src is not None and not src.dram:
                src.T.r.append(ev)
            own.w = ev
            own.r = []
        self.ops[eng].append((waits, lambda e: e.dma_start(out=out, in_=in_), (sem, 16)))

    def barrier(self):
        evs = [(self.sem[e], self.cnt[e]) for e in self.ENGS if self.cnt[e] > 0]
        evs += [(t.dsem, t.dcnt) for t in self.dtiles if t.dcnt > 0]
        for e in self.ENGS:
            deps = [ev for ev in evs if ev[0] is not self.sem[e]]
            waits = self._waits(e, deps)
            if waits:
                self.ops[e].append((waits, None, None))
        self.pool_fifo = []
        for t in self.dtiles:
            self.free_dsems.append((t.dsem, t.dcnt))
            t.dsem = None
            t.w = None
            t.r = []
        self.dtiles = []

    def emit(self):
        nc = self.nc
        handles = {"pe": "tensor", "act": "scalar", "dve": "vector", "pool": "gpsimd", "sp": "sync"}
        with nc.Block() as block:
            for e in self.ENGS:
                ops = self.ops[e]

                def body(eng, ops=ops):
                    for waits, fn, inc in ops:
                        for sem, val in waits:
                            eng.wait_ge(sem, val)
                        if fn is not None:
                            ins = fn(eng)
                            if inc is not None:
                                ins.then_inc(inc[0], inc[1])
                getattr(block, handles[e])(body)


class KB:
    def __init__(self, nc, stack):
        self.nc = nc
        self.S = Sched(nc, stack)
        self.off = 18688
        self.uid = 0
        self.base = 18688

    def reset(self):
        self.off = self.base

    def sb(self, name, shape, dtype):
        nb = int(np.prod(shape[1:])) * (4 if dtype == F32 else 2)
        nb = (nb + 63) // 64 * 64
        self.uid += 1
        t = self.nc.alloc_sbuf_tensor_at("%s_%d" % (name, self.uid), list(shape), dtype, offset=self.off)
        self.off += nb
        assert self.off <= 229376, ("SBUF overflow", name, self.off)
        return Buf(t, name)

    def sbs(self, name, shape, dtype, n):
        return [self.sb(name + str(i), shape, dtype) for i in range(n)]

    def act(self, out, in_, func, r, w, scale=1.0, bias=0.0, accum=None):
        if accum is None:
            fn = lambda e: e.activation(out=out, in_=in_, func=func, bias=bias, scale=scale)
        else:
            fn = lambda e: e.activation(out=out, in_=in_, func=func, bias=bias, scale=scale, accum_out=accum)
        self.S.op("act", fn, r, w)

    def ts(self, eng, out, in0, s1, s2, op0, op1, r, w):
        if op1 is None:
            fn = lambda e: e.tensor_scalar(out=out, in0=in0, scalar1=s1, scalar2=None, op0=op0)
        else:
            fn = lambda e: e.tensor_scalar(out=out, in0=in0, scalar1=s1, scalar2=s2, op0=op0, op1=op1)
        self.S.op(eng, fn, r, w)

    def tt(self, eng, out, in0, in1, op, r, w):
        self.S.op(eng, lambda e: e.tensor_tensor(out=out, in0=in0, in1=in1, op=op), r, w)

    def stt(self, eng, out, in0, scalar, in1, op0, op1, r, w):
        self.S.op(eng, lambda e: e.scalar_tensor_tensor(out=out, in0=in0, scalar=scalar, in1=in1, op0=op0, op1=op1), r, w)

    def copy(self, eng, out, in_, r, w):
        if eng == "act":
            self.S.op("act", lambda e: e.activation(out=out, in_=in_, func=AF.Copy), r, w)
        else:
            self.S.op(eng, lambda e: e.tensor_copy(out=out, in_=in_), r, w)

    def memset(self, eng, ap, val, w):
        self.S.op(eng, lambda e: e.memset(ap, val), (), w)

    def mm(self, out, lhsT, rhs, start, stop, r, w, inc=None):
        if inc is None:
            inc = stop
        self.S.op("pe", lambda e: e.matmul(out, lhsT=lhsT, rhs=rhs, start=start, stop=stop), r, w, inc=inc)

    def tr(self, out, in_, ident, r, w, inc):
        self.S.op("pe", lambda e: e.transpose(out, in_, ident), r, w, inc=inc)

    def dma(self, eng, out, in_, dst, src=None):
        self.S.dma(eng, out, in_, dst, src)


def build(only=None, dbg=(), ext_in=()):
    nc = bass.Bass("TRN2", target_bir_lowering=False)
    st = contextlib.ExitStack()
    with st:
        return _build(nc, st, only, dbg, ext_in)


class _LazyIn:
    def __init__(self, nc, name, shape, used):
        self.nc, self.name, self.shape, self.used = nc, name, list(shape), used
        self._ap = None

    def ap(self):
        if self._ap is None:
            self._ap = self.nc.dram_tensor(self.name, self.shape, F32, kind="ExternalInput").ap()
            self.used.append(self.name)
        return self._ap

    def __getitem__(self, k):
        return self.ap()[k]

    def rearrange(self, *a, **kw):
        return self.ap().rearrange(*a, **kw)


def _build(nc, st, only, dbg, ext_in):
    used_inputs = []
    nc._used_inputs = used_inputs

    def din(name, shape, dt=F32):
        return _LazyIn(nc, name, shape, used_inputs)

    def dscr(name, shape, dt):
        kind = "ExternalOutput" if name in dbg else ("ExternalInput" if name in ext_in else "Internal")
        if kind == "ExternalInput":
            used_inputs.append(name)
        return Buf(nc.dram_tensor(name, list(shape), dt, kind=kind).ap(), name, dram=True)

    xin = din("xin", [NTOK, D])
    cvec = din("cvec", [128, 16, 2])
    w_mod = din("w_mod", [2, D, 6 * D])
    b_mod = din("b_mod", [2, 6 * D])
    norm_g = din("norm_g", [2, 4 * D])
    w_ff_in = din("w_ff_in", [2, D, DFF])
    w_ff_out = din("w_ff_out", [2, DFF, D])
    ar_w_in = din("ar_w_in", [D, ARIN])
    ar_w_out = din("ar_w_out", [D, D])
    qkg = din("qkg", [128, 2])
    ropeP = din("ropeP", [128, 128])
    cosT = din("cosT", [128, SEQ])
    sinT = din("sinT", [128, SEQ])
    taps = din("taps", [128, 8, 5])
    convb = din("convb", [128, 8])
    ar_wa = din("ar_wa", [2, 8, 128, 128])
    ar_wx = din("ar_wx", [2, 8, 128, 128])
    rnnp = din("rnnp", [128, 3, 2, 8])
    gm_w_in = din("gm_w_in", [D, 2 * D])
    gm_bu = din("gm_bu", [128, 16])
    gm_rows = din("gm_rows", [3, D])
    gm_w_sp = din("gm_w_sp", [16, 128, 128])
    gm_b_sp = din("gm_b_sp", [1, D])
    gm_w_out = din("gm_w_out", [D, D])
    out = Buf(nc.dram_tensor("out", [HALF, D], F32, kind="ExternalOutput").ap(), "out", dram=True)
    xin_b = Buf(xin, "xin", dram=True)

    modv = dscr("modv", [14, D], F32)
    hT0 = dscr("hT0", [16, 128, HALF], BF16)
    qT = dscr("qT", [8, 128, HALF], BF16)
    kT = dscr("kT", [2, 128, NTOK], BF16)
    Vs = dscr("Vs", [NTOK, 256], BF16)
    xrT = dscr("xrT", [8, 128, NTOK], F32)
    ggT = dscr("ggT", [8, 128, HALF], F32)
    catT = dscr("catT", [16, 128, HALF], BF16)
    x1s = dscr("x1s", [HALF, D], F32)
    x2s = dscr("x2s", [HALF, D], F32)
    hTs = dscr("hTs", [16, 128, HALF], BF16)
    uT = dscr("uT", [8, 128, 64, 256], BF16)
    ffo = dscr("ffo", [HALF, D], F32)

    K = KB(nc, st)
    S = K.S

    PS = [Buf(nc.alloc_psum_tensor("ps%d" % i, [128, 512], F32), "ps%d" % i) for i in range(8)]

    ident = K.sb("ident", [128, 128], BF16)
    ones = K.sb("ones", [128, 128], BF16)
    idf = K.sb("idf", [128, 128], F32)
    K.memset("pool", idf[:], 1.0, [idf])
    S.op("pool", lambda e: e.affine_select(out=idf[:], in_=idf[:], pattern=[[-1, 128]], compare_op=ALU.is_equal,
                                            fill=0.0, base=0, channel_multiplier=1), [idf], [idf])
    K.copy("dve", ident[:], idf[:], [idf], [ident])
    K.memset("pool", ones[:], 1.0, [ones])
    K.base = K.off

    def rstd_from_ssq(ssq_ap, out_ap, n, r, w):
        K.act(out_ap, ssq_ap, AF.Sqrt, r, w, scale=1.0 / n, bias=EPS)
        S.op("dve", lambda e: e.reciprocal(out=out_ap, in_=out_ap), w, w)

    def make_norm_bufs(pairs, n=2):
        return {"tmp": K.sbs("ntmp", [128, D], F32, n), "hb": K.sbs("nhb", [128, D], BF16, n),
                "junk": K.sb("njunk", [128, D], BF16), "ssq": K.sbs("nssq", [128, 2], F32, n),
                "rstd": K.sbs("nrstd", [128, 2], F32, n), "pairs": pairs, "i": 0}

    def norm_mod_transpose(xt, Abc, Bbc, NB, dst_ap, dst_buf):
        i = NB["i"]
        NB["i"] += 1
        n = len(NB["tmp"])
        tmp, hb, ssq, rstd, junk = NB["tmp"][i % n], NB["hb"][i % n], NB["ssq"][i % n], NB["rstd"][i % n], NB["junk"]
        pbanks = NB["pairs"][i % len(NB["pairs"])]
        K.act(junk[:], xt[:], AF.Square, [xt], [junk, ssq], accum=ssq[:, 0:1])
        rstd_from_ssq(ssq[:, 0:1], rstd[:, 0:1], D, [ssq], [rstd])
        K.stt("dve", tmp[:], xt[:], rstd[:, 0:1], Abc[:], ALU.mult, ALU.mult, [xt, rstd, Abc], [tmp])
        K.tt("pool", hb[:], tmp[:], Bbc[:], ALU.add, [tmp, Bbc], [hb])
        for half in range(2):
            pb = pbanks[half]
            pv = pb.t[:].bitcast(BF16)
            for kk in range(8):
                k = half * 8 + kk
                K.tr(pv[:, kk * 128:(kk + 1) * 128], hb[:, k * 128:(k + 1) * 128], ident[:], [hb, ident], [pb], inc=(kk == 7))
            K.copy("act", dst_ap[:, half * 8:(half + 1) * 8, :], pv.rearrange("p (k t) -> p k t", k=8), [pb], [dst_buf])

    def load_bcast(buf, row_ap, src):
        K.dma("sp", buf[:], row_ap.partition_broadcast(128), buf, src)

    def phase_mod():
        K.reset()
        cv = K.sb("cv", [128, 16, 2], F32)
        sv = K.sb("sv", [128, 16, 2], BF16)
        wt = K.sbs("wmod", [128, 16, 512], BF16, 4)
        raw = K.sb("raw", [2, 6 * D], F32)
        gg = K.sb("gg", [2, 4 * D], F32)
        K.dma("sp", cv[:], cvec.ap(), cv)
        K.act(sv[:], cv[:], AF.Silu, [cv], [sv])
        it = 0
        for l in range(2):
            K.dma("sp", raw[:], b_mod[l:l + 1, :].partition_broadcast(2), raw)
            K.dma("sp", gg[:], norm_g[l:l + 1, :].partition_broadcast(2), gg)
            def loadw(n, l=l):
                if n < 24:
                    K.dma("pool", wt[n % 4][:], w_mod[l, :, n * 512:(n + 1) * 512].rearrange("(k p) n -> p k n", p=128), wt[n % 4])
            loadw(0)
            loadw(1)
            for n in range(24):
                loadw(n + 2)
                w = wt[n % 4]
                it += 1
                pb = PS[it % 2]
                for k in range(16):
                    K.mm(pb[0:2, :], sv[:, k, :], w[:, k, :], k == 0, k == 15, [sv, w], [pb])
                K.tt("dve", raw[:, n * 512:(n + 1) * 512], pb[0:2, :], raw[:, n * 512:(n + 1) * 512], ALU.add, [pb, raw], [raw])
            K.stt("dve", raw[:, D:2 * D], raw[:, D:2 * D], 1.0, gg[:, 0:D], ALU.add, ALU.mult, [raw, gg], [raw])
            K.tt("dve", raw[:, 2 * D:3 * D], raw[:, 2 * D:3 * D], gg[:, D:2 * D], ALU.mult, [raw, gg], [raw])
            K.stt("dve", raw[:, 4 * D:5 * D], raw[:, 4 * D:5 * D], 1.0, gg[:, 2 * D:3 * D], ALU.add, ALU.mult, [raw, gg], [raw])
            K.tt("dve", raw[:, 5 * D:6 * D], raw[:, 5 * D:6 * D], gg[:, 3 * D:4 * D], ALU.mult, [raw, gg], [raw])
            for r, src in enumerate((1, 0, 2, 4, 3, 5)):
                K.dma("sp", modv[6 * l + r:6 * l + r + 1, :], raw[0:1, src * D:(src + 1) * D], modv, raw)
            if l == 0:
                K.dma("sp", modv[12:13, :], raw[1:2, D:2 * D], modv, raw)
                K.dma("sp", modv[13:14, :], raw[1:2, 0:D], modv, raw)
        S.barrier()

    def phase_inproj(mode):
        K.reset()
        if mode == "qg":
            col0 = [0, 512, 2560, 3072]
        else:
            col0 = [1024, 1536, 2048]
        npan = len(col0)
        W = K.sb("Win", [128, 16, npan * 512], BF16)
        Wp = [Buf(W.t, "Winp%d" % i) for i in range(npan)]
        for p in range(npan):
            K.dma("pool", W[:, :, p * 512:(p + 1) * 512],
                  ar_w_in[:, col0[p]:col0[p] + 512].rearrange("(k p) n -> p k n", p=128), Wp[p])

        def wloc(ocol):
            for p in range(npan):
                if col0[p] <= ocol < col0[p] + 512:
                    return p * 512 + ocol - col0[p], p
            raise KeyError(ocol)
        Abc = K.sb("Abc", [128, D], F32)
        Bbc = K.sb("Bbc", [128, D], F32)
        xt = K.sbs("xt", [128, D], F32, 2)
        NB = make_norm_bufs([(PS[0], PS[1])])
        hT = K.sbs("hT", [128, 16, 512], BF16, 2)
        qk = K.sb("qkg", [128, 2], F32)
        Pm = K.sb("Pm", [128, 128], BF16)
        Pf = K.sb("Pf", [128, 128], F32)
        cs = K.sbs("cs", [128, 512], F32, 2)
        sn = K.sbs("sn", [128, 512], F32, 2)
        sq = K.sbs("sq", [128, 512], BF16, 2)
        qg = K.sbs("qg", [128, 512], BF16, 2)
        rs = K.sbs("rs", [128, 512], F32, 2)
        t1 = K.sbs("t1", [128, 512], F32, 2)
        t2 = K.sbs("t2", [128, 512], F32, 2)
        ob = K.sbs("ob", [128, 512], BF16, 3)
        of = K.sbs("of", [128, 512], F32, 3)
        vb = K.sbs("vb", [128, 256], BF16, 2)
        K.dma("sp", qk[:], qkg.ap(), qk)
        K.dma("sp", Pf[:], ropeP.ap(), Pf)
        K.copy("dve", Pm[:], Pf[:], [Pf], [Pm])
        load_bcast(Abc, modv[0:1, :], modv)
        load_bcast(Bbc, modv[1:2, :], modv)

        cnt = {"x": 0, "ps": 0, "o": 0, "f": 0, "v": 0, "qk": 0}

        def proj_cols(h, NT, n):
            lc, p = wloc(n * 128)
            pb = PS[2 + cnt["ps"] % 3]
            cnt["ps"] += 1
            for k in range(16):
                K.mm(pb[:, 0:NT], W[:, k, lc:lc + 128], h[:, k, 0:NT], k == 0, k == 15, [Wp[p], h], [pb])
            return pb

        def do_qk(pb, NT, gcol, rope_t0, dst_ap, dst_buf):
            i = cnt["qk"] % 2
            cnt["qk"] += 1
            K.act(sq[i][:, 0:NT], pb[:, 0:NT], AF.Square, [pb], [sq[i]])
            K.act(qg[i][:, 0:NT], pb[:, 0:NT], AF.Identity, [pb, qk], [qg[i]], scale=qk[:, gcol:gcol + 1])
            K.mm(PS[5][:, 0:NT], ones[:], sq[i][:, 0:NT], True, True, [ones, sq[i]], [PS[5]])
            rstd_from_ssq(PS[5][:, 0:NT], rs[i][:, 0:NT], 128, [PS[5]], [rs[i]])
            o = ob[cnt["o"] % 3]
            cnt["o"] += 1
            if rope_t0 is None:
                K.tt("pool", o[:, 0:NT], qg[i][:, 0:NT], rs[i][:, 0:NT], ALU.mult, [qg[i], rs[i]], [o])
            else:
                K.mm(PS[6][:, 0:NT], Pm[:], qg[i][:, 0:NT], True, True, [Pm, qg[i]], [PS[6]])
                K.tt("pool", t1[i][:, 0:NT], qg[i][:, 0:NT], cs[rope_t0][:, 0:NT], ALU.mult, [qg[i], cs[rope_t0]], [t1[i]])
                K.tt("dve", t2[i][:, 0:NT], PS[6][:, 0:NT], sn[rope_t0][:, 0:NT], ALU.mult, [PS[6], sn[rope_t0]], [t2[i]])
                K.tt("pool", t1[i][:, 0:NT], t1[i][:, 0:NT], t2[i][:, 0:NT], ALU.add, [t1[i], t2[i]], [t1[i]])
                K.tt("pool", o[:, 0:NT], t1[i][:, 0:NT], rs[i][:, 0:NT], ALU.mult, [t1[i], rs[i]], [o])
            K.dma("sp", dst_ap, o[:, 0:NT], dst_buf, o)

        nch = 4 if mode == "qg" else 9

        def prep_tasks(c):
            if c >= nch:
                return []
            isctx = c == 8
            NT = 256 if isctx else 512
            t0 = c * 512
            h = hT[c % 2]
            tasks = []
            if isctx:
                def t_bc():
                    load_bcast(Abc, modv[12:13, :], modv)
                    load_bcast(Bbc, modv[13:14, :], modv)
                tasks.append(t_bc)
            if c < 4 and mode == "kvx":
                tasks.append(lambda: K.dma("sp", h[:], hT0[:, :, t0:t0 + 512].rearrange("k p t -> p k t"), h, hT0))
            else:
                for tt in range(NT // 128):
                    def t_norm(tt=tt):
                        x = xt[cnt["x"] % 2]
                        cnt["x"] += 1
                        K.dma("sp", x[:], xin[t0 + tt * 128:t0 + (tt + 1) * 128, :], x)
                        norm_mod_transpose(x, Abc, Bbc, NB, h[:, :, tt * 128:(tt + 1) * 128], h)
                    tasks.append(t_norm)
            if not isctx:
                def t_cs():
                    ci = c % 2
                    K.dma("sp", cs[ci][:], cosT[:, t0:t0 + 512], cs[ci])
                    K.dma("sp", sn[ci][:], sinT[:, t0:t0 + 512], sn[ci])
                tasks.append(t_cs)
            return tasks

        for tk in prep_tasks(0):
            tk()
        for c in range(nch):
            isctx = c == 8
            NT = 256 if isctx else 512
            t0 = c * 512
            h = hT[c % 2]
            ri = None if isctx else c % 2
            pend = prep_tasks(c + 1)
            state = {"n": 0}

            def tick(every):
                state["n"] += 1
                if pend and state["n"] % every == 0:
                    pend.pop(0)()
            if mode == "qg":
                K.dma("sp", hT0[:, :, t0:t0 + 512].rearrange("k p t -> p k t"), h[:], hT0, h)
                for hd in range(8):
                    pb = proj_cols(h, NT, hd)
                    do_qk(pb, NT, 0, ri, qT[hd, :, t0:t0 + NT], qT)
                    tick(2)
                for n in range(8):
                    pb = proj_cols(h, NT, 20 + n)
                    o = of[cnt["f"] % 3]
                    cnt["f"] += 1
                    K.act(o[:, 0:NT], pb[:, 0:NT], AF.Gelu_apprx_tanh, [pb], [o])
                    K.dma("act", ggT[n, :, t0:t0 + NT], o[:, 0:NT], ggT, o)
                    tick(2)
            else:
                for kv in range(2):
                    pb = proj_cols(h, NT, 8 + kv)
                    do_qk(pb, NT, 1, ri, kT[kv, :, t0:t0 + NT], kT)
                    tick(1)
                vl, vp = wloc(1280)
                for tt in range(NT // 128):
                    pb = PS[2 + cnt["ps"] % 3]
                    cnt["ps"] += 1
                    for k in range(16):
                        K.mm(pb[:, 0:256], h[:, k, tt * 128:(tt + 1) * 128], W[:, k, vl:vl + 256], k == 0, k == 15, [h, Wp[vp]], [pb])
                    v = vb[cnt["v"] % 2]
                    cnt["v"] += 1
                    K.copy("act", v[:], pb[:, 0:256], [pb], [v])
                    K.dma("act", Vs[t0 + tt * 128:t0 + (tt + 1) * 128, :], v[:], Vs, v)
                    tick(2)
                for n in range(8):
                    pb = proj_cols(h, NT, 12 + n)
                    o = of[cnt["f"] % 3]
                    cnt["f"] += 1
                    K.copy("act", o[:, 0:NT], pb[:, 0:NT], [pb], [o])
                    K.dma("act", xrT[n, :, t0:t0 + NT], o[:, 0:NT], xrT, o)
                    tick(2)
            while pend:
                pend.pop(0)()
        S.barrier()

    def phase_attn():
        K.reset()
        NKT = NTOK // 128
        kt_sb = K.sb("kTs", [128, 2, NTOK], BF16)
        v_sb = K.sb("Vsb", [128, NKT, 256], BF16)
        qs = K.sbs("qs", [128, 512], BF16, 3)
        pbuf = K.sbs("pexp", [128, 512], BF16, 4)
        rl = K.sbs("rl", [128, 512], F32, 2)
        ob = K.sbs("aob", [128, 512], BF16, 2)
        K.dma("sp", kt_sb[:], kT[:].rearrange("k p t -> p k t"), kt_sb, kT)
        v_h = [Buf(v_sb.t, "v_h%d" % i) for i in range(2)]
        for i in range(2):
            K.dma("sp", v_sb[:, i * 17:(i + 1) * 17, :],
                  Vs[i * 17 * 128:(i + 1) * 17 * 128, :].rearrange("(kt p) d -> p kt d", p=128), v_h[i], Vs)
        scale = 128.0 ** -0.5
        it = 0
        sc = 0
        def loadq(i):
            if i < 32:
                K.dma("sp", qs[i % 3][:], qT[i // 4, :, (i % 4) * 512:(i % 4 + 1) * 512], qs[i % 3], qT)
        loadq(0)
        loadq(1)
        for hd in range(8):
            kv = hd // 4
            for qt in range(4):
                q = qs[it % 3]
                loadq(it + 2)
                Ob = PS[4 + 2 * (it % 2)]
                Lb = PS[5 + 2 * (it % 2)]
                sbanks = {}

                def issue_s(kt, q=q, kv=kv):
                    nonlocal sc
                    pb = PS[sc % 4]
                    sc += 1
                    K.mm(pb[:], kt_sb[:, kv, kt * 128:(kt + 1) * 128], q[:], True, True, [kt_sb, q], [pb])
                    sbanks[kt] = pb
                issue_s(0)
                issue_s(1)
                for kt in range(NKT):
                    if kt + 2 < NKT:
                        issue_s(kt + 2)
                    pb = sbanks.pop(kt)
                    p = pbuf[(it * NKT + kt) % 4]
                    K.act(p[:], pb[:], AF.Exp, [pb], [p], scale=scale)
                    K.mm(Ob[:], v_sb[:, kt, kv * 128:(kv + 1) * 128], p[:], kt == 0, kt == NKT - 1, [v_h[kt // 17], p], [Ob], inc=(kt == NKT - 1))
                    K.mm(Lb[:], ones[:], p[:], kt == 0, kt == NKT - 1, [ones, p], [Lb], inc=True)
                r = rl[it % 2]
                o = ob[it % 2]
                S.op("dve", lambda e, r=r, Lb=Lb: e.reciprocal(out=r[:], in_=Lb[:]), [Lb], [r])
                K.tt("dve", o[:], Ob[:], r[:], ALU.mult, [Ob, r], [o])
                K.dma("sp", catT[hd, :, qt * 512:(qt + 1) * 512], o[:], catT, o)
                it += 1
        S.barrier()

    def phase_rnn():
        K.reset()
        tp = K.sb("taps", [128, 8, 5], F32)
        cb = K.sb("convb", [128, 8], F32)
        rp = K.sb("rnnp", [128, 3, 2, 8], F32)
        c1 = K.sb("c1", [128, 2, 8], F32)
        e1 = K.sb("e1", [128, 2, 8], F32)
        hb_ = K.sb("hbias", [128, 2, 2, 8], F32)
        waf = K.sbs("waf", [128, 128], F32, 2)
        wab = K.sbs("wab", [128, 128], BF16, 4)
        wxb = K.sbs("wxb", [128, 128], BF16, 4)
        xl = K.sbs("xl", [128, XL], F32, 1)
        xc = K.sbs("xc", [128, XC], F32, 2)
        accs = K.sbs("acc", [128, NTOK], F32, 2)
        accbs = K.sbs("accb", [128, NTOK], BF16, 2)
        NA = HALF + CTX
        dbuf = []
        for d, n_ in ((0, NA), (1, NTOK)):
            ent = {}
            for nm in ("a", "b", "s"):
                t = K.sb("g%s%d" % (nm, d), [128, n_], F32)
                ent[nm] = t
                ent[nm + "t"] = [Buf(t.t, "g%s%d_%d" % (nm, d, i)) for i in range(9)]
            dbuf.append(ent)
        rr = K.sbs("rr", [128, 512], F32, 2)
        ii = K.sbs("ii", [128, 512], F32, 2)
        hA = K.sb("hA", [128, HALF], F32)
        hB = K.sb("hB", [128, HALF], F32)
        hc = K.sb("hc", [128, CTX], F32)
        fin = K.sb("fin", [128, 2], F32)
        gg = K.sbs("gg", [128, HALF], F32, 1)
        ro = K.sbs("ro", [128, HALF], BF16, 2)
        K.dma("sp", tp[:], taps.ap(), tp)
        K.dma("sp", cb[:], convb.ap(), cb)
        K.dma("sp", rp[:], rnnp.ap(), rp)
        K.act(e1[:], rp[:, 2, :, :], AF.Exp, [rp], [e1], scale=-1.0)
        K.act(e1[:], e1[:], AF.Ln, [e1], [e1], bias=1.0)
        K.ts("dve", c1[:], e1[:], -4.0, None, ALU.mult, None, [e1], [c1])
        K.ts("dve", hb_[:], rp[:, 0:2, :, :], 0.5, None, ALU.mult, None, [rp], [hb_])
        for b_ in xl:
            K.memset("pool", b_[:, 0:2], 0.0, [b_])
            K.memset("pool", b_[:, XL - 2:XL], 0.0, [b_])
        for b_ in xc:
            K.memset("pool", b_[:, 0:2], 0.0, [b_])
            K.memset("pool", b_[:, XC - 2:XC], 0.0, [b_])
        wi = 0
        ti = 0
        gi = 0
        for n in range(8):
            X = xl[0]
            C = xc[n % 2]
            acc = accs[n % 2]
            accb = accbs[n % 2]
            K.dma("sp", X[:, 2:2 + SEQ], xrT[n, :, 0:SEQ], X, xrT)
            K.dma("sp", C[:, 2:2 + CTX], xrT[n, :, SEQ:NTOK], C, xrT)
            for (src, L, o0) in ((X, SEQ, 0), (C, CTX, SEQ)):
                K.act(acc[:, o0:o0 + L], src[:, 0:L], AF.Identity, [src, tp, cb], [acc], scale=tp[:, n, 0:1], bias=cb[:, n:n + 1])
                for j in range(1, 5):
                    K.stt("dve", acc[:, o0:o0 + L], src[:, j:j + L], tp[:, n, j:j + 1], acc[:, o0:o0 + L], ALU.mult, ALU.add, [src, tp, acc], [acc])
            K.copy("act", accb[:], acc[:], [acc], [accb])
            for d in range(2):
                E = dbuf[d]
                wA = wab[wi % 4]
                wX = wxb[wi % 4]
                wi += 1
                for (srcw, dstw) in ((ar_wa, wA), (ar_wx, wX)):
                    f = waf[ti % 2]
                    ti += 1
                    K.dma("sp", f[:], srcw[d, n, :, :], f)
                    K.copy("dve", dstw[:], f[:], [f], [dstw])
                nlat = HALF if d == 0 else SEQ
                tiles = [(SEQ, nlat, CTX)] + [(c0, c0, 512) for c0 in range(0, nlat, 512)]
                for (c0, l0, NT) in tiles:
                    i = gi % 2
                    gi += 1
                    lt = l0 // 512
                    K.mm(PS[0 + i][:, 0:NT], wA[:], accb[:, c0:c0 + NT], True, True, [wA, accb], [PS[0 + i]])
                    K.mm(PS[2 + i][:, 0:NT], wX[:], accb[:, c0:c0 + NT], True, True, [wX, accb], [PS[2 + i]])
                    K.act(rr[i][:, 0:NT], PS[0 + i][:, 0:NT], AF.Tanh, [PS[0 + i], hb_], [rr[i]], scale=0.5, bias=hb_[:, 0, d, n:n + 1])
                    K.act(ii[i][:, 0:NT], PS[2 + i][:, 0:NT], AF.Tanh, [PS[2 + i], hb_], [ii[i]], scale=0.5, bias=hb_[:, 1, d, n:n + 1])
                    K.act(E["a"][:, l0:l0 + NT], rr[i][:, 0:NT], AF.Exp, [rr[i], c1], [E["at"][lt]], scale=c1[:, d, n:n + 1], bias=c1[:, d, n:n + 1])
                    K.tt("pool", E["s"][:, l0:l0 + NT], E["a"][:, l0:l0 + NT], E["a"][:, l0:l0 + NT], ALU.mult, [E["at"][lt]], [E["st"][lt]])
                    K.stt("dve", E["b"][:, l0:l0 + NT], ii[i][:, 0:NT], 1.0, acc[:, c0:c0 + NT], ALU.add, ALU.mult, [ii[i], acc], [E["bt"][lt]])
                nt = nlat // 512
                for (l0, L, tl) in ((0, nlat, list(range(nt))), (nlat, CTX, [nt])):
                    st_ = [E["st"][t] for t in tl]
                    bt_ = [E["bt"][t] for t in tl]
                    K.act(E["s"][:, l0:l0 + L], E["s"][:, l0:l0 + L], AF.Sqrt, st_, st_, scale=-0.25, bias=0.25)
                    K.tt("pool", E["b"][:, l0:l0 + L], E["b"][:, l0:l0 + L], E["s"][:, l0:l0 + L], ALU.mult, st_ + bt_, bt_)
                A_, B_ = E["a"], E["b"]
                at, bt = E["at"], E["bt"]
                if d == 0:
                    S.op("dve", lambda e, A_=A_, B_=B_: e.tensor_tensor_scan(out=hc[:, :], data0=A_[:, HALF:NA], data1=B_[:, HALF:NA],
                                                                               initial=0.0, op0=ALU.mult, op1=ALU.add), [at[4], bt[4]], [hc])
                    S.op("dve", lambda e, A_=A_, B_=B_: e.tensor_tensor_scan(out=hA[:, :], data0=A_[:, 0:HALF], data1=B_[:, 0:HALF],
                                                                               initial=hc[:, CTX - 1:CTX], op0=ALU.mult, op1=ALU.add), at[0:4] + bt[0:4] + [hc], [hA])
                else:
                    S.op("dve", lambda e, A_=A_, B_=B_: e.tensor_tensor_scan(out=hc[:, ::-1], data0=A_[:, SEQ:NTOK][:, ::-1], data1=B_[:, SEQ:NTOK][:, ::-1],
                                                                               initial=0.0, op0=ALU.mult, op1=ALU.add), [at[8], bt[8]], [hc])
                    S.op("dve", lambda e, A_=A_, B_=B_: e.tensor_tensor_scan(out=hB[:, ::-1], data0=A_[:, HALF:SEQ][:, ::-1], data1=B_[:, HALF:SEQ][:, ::-1],
                                                                               initial=hc[:, 0:1], op0=ALU.mult, op1=ALU.add), at[4:8] + bt[4:8] + [hc], [hB])
                    K.copy("dve", fin[:, 0:1], hB[:, 0:1], [hB], [fin])
                    S.op("dve", lambda e, A_=A_, B_=B_: e.tensor_tensor_scan(out=hB[:, ::-1], data0=A_[:, 0:HALF][:, ::-1], data1=B_[:, 0:HALF][:, ::-1],
                                                                               initial=fin[:, 0:1], op0=ALU.mult, op1=ALU.add), at[0:4] + bt[0:4] + [fin], [hB])
            G = gg[0]
            K.dma("sp", G[:], ggT[n, :, :], G, ggT)
            K.tt("pool", hA[:], hA[:], hB[:], ALU.add, [hA, hB], [hA])
            R = ro[n % 2]
            K.tt("pool", R[:], hA[:], G[:], ALU.mult, [hA, G], [R])
            K.dma("sp", catT[8 + n, :, :], R[:], catT, R)
        S.barrier()

    def make_epi_bufs(from_psum):
        B = {}
        B["Gbc"] = K.sb("Gbc", [128, D], F32)
        B["Abc"] = K.sb("Abc", [128, D], F32)
        B["Bbc"] = K.sb("Bbc", [128, D], F32)
        B["xt"] = K.sbs("ext", [128, D], F32, 3)
        if from_psum:
            B["raw"] = K.sbs("eraw", [128, D], F32, 1)
        else:
            B["ft"] = K.sbs("ft", [128, D], F32, 3)
        B["tmp1"] = K.sb("etmp1", [128, D], F32)
        B["tmp2"] = K.sb("etmp2", [128, D], F32)
        B["hb"] = K.sbs("ehb", [128, D], BF16, 2)
        B["junk"] = K.sb("ejunk", [128, D], BF16)
        B["st"] = K.sbs("est", [128, 4], F32, 3)
        B["hst"] = K.sbs("ehst", [128, 16, 512], BF16, 1 if from_psum else 2)
        B["pairs"] = [(PS[4], PS[5]), (PS[6], PS[7])]
        return B

    def epi_load(B, t, xsrc, fsrc=None):
        if t >= 16:
            return
        x = B["xt"][t % 3]
        K.dma("sp", x[:], xsrc[t * 128:(t + 1) * 128, :], x, xsrc)
        if fsrc is not None:
            f = B["ft"][t % 3]
            K.dma("sp", f[:], fsrc[t * 128:(t + 1) * 128, :], f, fsrc)

    def epi_s1(B, t, raw_aps, raw_bufs, from_psum, xdst):
        st, junk, tmp = B["st"][t % 3], B["junk"], B["tmp1"]
        x = B["xt"][t % 3]
        if from_psum:
            raw = B["raw"][0]
            for fb in range(4):
                K.copy("act", raw[:, fb * 512:(fb + 1) * 512], raw_aps[fb], [raw_bufs[fb]], [raw])
            rap = raw[:]
        else:
            raw = raw_bufs[0]
            rap = raw_aps[0]
        K.act(junk[:], rap, AF.Square, [raw], [junk, st], accum=st[:, 0:1])
        rstd_from_ssq(st[:, 0:1], st[:, 1:2], D, [st], [st])
        K.stt("dve", tmp[:], rap, st[:, 1:2], B["Gbc"][:], ALU.mult, ALU.mult, [raw, st, B["Gbc"]], [tmp])
        K.tt("pool", x[:], tmp[:], x[:], ALU.add, [tmp, x], [x])
        K.dma("sp", xdst[t * 128:(t + 1) * 128, :], x[:], xdst, x)

    def epi_s2(B, t):
        st, junk, tmp = B["st"][t % 3], B["junk"], B["tmp2"]
        x = B["xt"][t % 3]
        hb = B["hb"][t % 2]
        K.act(junk[:], x[:], AF.Square, [x], [junk, st], accum=st[:, 2:3])
        rstd_from_ssq(st[:, 2:3], st[:, 3:4], D, [st], [st])
        K.stt("dve", tmp[:], x[:], st[:, 3:4], B["Abc"][:], ALU.mult, ALU.mult, [x, st, B["Abc"]], [tmp])
        K.tt("dve", hb[:], tmp[:], B["Bbc"][:], ALU.add, [tmp, B["Bbc"]], [hb])

    def epi_s3(B, t, hdst):
        hb = B["hb"][t % 2]
        hst = B["hst"][(t // 4) % len(B["hst"])]
        tt = t % 4
        pbanks = B["pairs"][t % 2]
        for half in range(2):
            pb = pbanks[half]
            pv = pb.t[:].bitcast(BF16)
            for kk in range(8):
                k = half * 8 + kk
                K.tr(pv[:, kk * 128:(kk + 1) * 128], hb[:, k * 128:(k + 1) * 128], ident[:], [hb, ident], [pb], inc=(kk == 7))
            K.copy("act", hst[:, half * 8:(half + 1) * 8, tt * 128:(tt + 1) * 128], pv.rearrange("p (k t) -> p k t", k=8), [pb], [hst])
        if tt == 3:
            c0 = (t - 3) * 128
            K.dma("sp", hdst[:, :, c0:c0 + 512].rearrange("k p t -> p k t"), hst[:], hdst, hst)

    def epi_tail(B, t, do_norm, hdst):
        if not do_norm:
            return
        if 0 <= t - 1 < 16:
            epi_s2(B, t - 1)
        if 0 <= t - 2 < 16:
            epi_s3(B, t - 2, hdst)

    def phase_outproj(Wd, mrow, xsrc, xdst, hdst):
        K.reset()
        W = K.sb("Wout", [128, 16, D], BF16)
        Wp = [Buf(W.t, "Woutp%d" % i) for i in range(4)]
        for p in range(4):
            K.dma("pool", W[:, :, p * 512:(p + 1) * 512], Wd[:, p * 512:(p + 1) * 512].rearrange("(k p) n -> p k n", p=128), Wp[p])
        B = make_epi_bufs(True)
        load_bcast(B["Gbc"], modv[mrow + 2:mrow + 3, :], modv)
        load_bcast(B["Abc"], modv[mrow + 3:mrow + 4, :], modv)
        load_bcast(B["Bbc"], modv[mrow + 4:mrow + 5, :], modv)
        cat = K.sbs("cat", [128, 16, 512], BF16, 2)

        def loadcat(c):
            K.dma("sp", cat[c % 2][:], catT[:, :, c * 512:(c + 1) * 512].rearrange("k p t -> p k t"), cat[c % 2], catT)
        loadcat(0)
        epi_load(B, 0, xsrc)
        epi_load(B, 1, xsrc)
        for t in range(18):
            if t < 16:
                c, tt = t // 4, t % 4
                cc = cat[c % 2]
                if tt == 0 and c + 1 < 4:
                    loadcat(c + 1)
                for k in range(16):
                    for fb in range(4):
                        K.mm(PS[fb][:], cc[:, k, tt * 128:(tt + 1) * 128], W[:, k, fb * 512:(fb + 1) * 512],
                             k == 0, k == 15, [cc, Wp[fb]], [PS[fb]])
                epi_s1(B, t, [PS[fb][:] for fb in range(4)], [PS[fb] for fb in range(4)], True, xdst)
            epi_tail(B, t, True, hdst)
            epi_load(B, t + 2, xsrc)
        S.barrier()

    def phase_up(Wd, ncols, hsrc, dst, kind, bias_d=None):
        K.reset()
        h = K.sb("hres", [128, 16, HALF], BF16)
        hp = [Buf(h.t, "hresp%d" % i) for i in range(4)]
        for c in range(4):
            K.dma("sp", h[:, :, c * 512:(c + 1) * 512], hsrc[:, :, c * 512:(c + 1) * 512].rearrange("k p t -> p k t"), hp[c], hsrc)
        Wt = K.sbs("Wup", [128, 16, 512], BF16, 3)
        rl = K.sbs("rl", [128, 512], F32, 3)
        ust = K.sbs("ust", [128, HALF], BF16, 3)
        bias = None
        if bias_d is not None:
            bias = K.sb("bias", [128, ncols // 128], F32)
            K.dma("sp", bias[:], bias_d.ap(), bias)
        npan = ncols // 512

        def loadW(pn):
            K.dma("pool", Wt[pn % 3][:], Wd[:, pn * 512:(pn + 1) * 512].rearrange("(k p) n -> p k n", p=128), Wt[pn % 3])
        loadW(0)
        loadW(1)
        pi = 0
        for pn in range(npan):
            if pn + 2 < npan:
                loadW(pn + 2)
            Wb = Wt[pn % 3]
            for jc in range(4):
                j = pn * 4 + jc
                u = ust[j % 3]
                for c in range(4):
                    pb = PS[pi % 8]
                    pi += 1
                    for k in range(16):
                        K.mm(pb[:], Wb[:, k, jc * 128:(jc + 1) * 128], h[:, k, c * 512:(c + 1) * 512], k == 0, k == 15, [Wb, hp[c]], [pb])
                    if kind == "relu2":
                        r = rl[pi % 3]
                        K.ts("dve", r[:], pb[:], 0.0, None, ALU.max, None, [pb], [r])
                        K.act(u[:, c * 512:(c + 1) * 512], r[:], AF.Square, [r], [u])
                    else:
                        K.act(u[:, c * 512:(c + 1) * 512], pb[:], AF.Gelu_apprx_tanh, [pb, bias], [u], bias=bias[:, j:j + 1])
                if kind == "relu2":
                    K.dma("act", dst[:, :, j, :].rearrange("c p t -> p c t"), u[:].rearrange("p (c t) -> p c t", c=8), dst, u)
                else:
                    K.dma("act", dst[j, :, :], u[:], dst, u)
        S.barrier()

    def phase_down(Wd):
        K.reset()
        Wq = K.sbs("W2q", [128, 64, 512], BF16, 2)
        Wqp = [[Buf(Wq[i].t, "W2q%dp%d" % (i, p)) for p in range(4)] for i in range(2)]
        ut = K.sbs("ut", [128, 64, 256], BF16, 2)
        utp = [[Buf(ut[i].t, "ut%dp%d" % (i, p)) for p in range(4)] for i in range(2)]
        of = K.sbs("dof", [128, 512], F32, 3)

        def loadW(q):
            for p in range(4):
                K.dma("pool", Wq[q % 2][:, p * 16:(p + 1) * 16, :],
                      Wd[p * 2048:(p + 1) * 2048, q * 512:(q + 1) * 512].rearrange("(j p) n -> p j n", p=128), Wqp[q % 2][p])

        def loadU(ui):
            tc = ui % 8
            for p in range(4):
                K.dma("sp", ut[ui % 2][:, p * 16:(p + 1) * 16, :], uT[tc, :, p * 16:(p + 1) * 16, :], utp[ui % 2][p], uT)
        loadW(0)
        loadU(0)
        oi = 0
        for q in range(4):
            Wb = Wq[q % 2]
            for tc in range(8):
                ui = q * 8 + tc
                if ui + 1 < 32:
                    loadU(ui + 1)
                if tc == 1 and q + 1 < 4:
                    loadW(q + 1)
                U = ut[ui % 2]
                Up = utp[ui % 2]
                for tt in range(2):
                    pb = PS[oi % 4]
                    for j in range(64):
                        K.mm(pb[:], U[:, j, tt * 128:(tt + 1) * 128], Wb[:, j, :], j == 0, j == 63, [Up[j // 16], Wqp[q % 2][j // 16]], [pb])
                    o = of[oi % 3]
                    oi += 1
                    K.copy("act", o[:], pb[:], [pb], [o])
                    t0 = tc * 256 + tt * 128
                    K.dma("act", ffo[t0:t0 + 128, q * 512:(q + 1) * 512], o[:], ffo, o)
        S.barrier()

    def phase_ffn_epi(mrow, xsrc, xdst, do_norm, nrow, hdst):
        K.reset()
        B = make_epi_bufs(False)
        load_bcast(B["Gbc"], modv[mrow + 5:mrow + 6, :], modv)
        if do_norm:
            load_bcast(B["Abc"], modv[nrow:nrow + 1, :], modv)
            load_bcast(B["Bbc"], modv[nrow + 1:nrow + 2, :], modv)
        epi_load(B, 0, xsrc, ffo)
        epi_load(B, 1, xsrc, ffo)
        for t in range(18):
            if t < 16:
                f = B["ft"][t % 3]
                epi_s1(B, t, [f[:]], [f], False, xdst)
            epi_tail(B, t, do_norm, hdst)
            epi_load(B, t + 2, xsrc, ffo)
        S.barrier()

    def phase_gm_v():
        K.reset()
        W = K.sb("Wv", [128, 16, D], BF16)
        Wp = [Buf(W.t, "Wvp%d" % i) for i in range(4)]
        for p in range(4):
            K.dma("pool", W[:, :, p * 512:(p + 1) * 512], gm_w_in[:, D + p * 512:D + (p + 1) * 512].rearrange("(k p) n -> p k n", p=128), Wp[p])
        bvb = K.sb("bvb", [128, D], F32)
        vgb = K.sb("vgb", [128, D], F32)
        vbb = K.sb("vbb", [128, D], F32)
        load_bcast(bvb, gm_rows[0:1, :], None)
        load_bcast(vgb, gm_rows[1:2, :], None)
        load_bcast(vbb, gm_rows[2:3, :], None)
        spT = K.sb("spT", [128, 16, 128], BF16)
        bsb = K.sb("bsb", [1, D], BF16)
        hT = K.sbs("hTg", [128, 16, 512], BF16, 2)
        gu = K.sbs("gu", [128, 16, 512], BF16, 1)
        pst = K.sbs("pst", [128, 16, 512], BF16, 1)
        st = K.sbs("st4", [128, 8], F32, 2)
        mark = K.off
        spf = K.sb("spf", [128, 16, 128], F32)
        spb = K.sb("spb", [128, 16, 128], BF16)
        bsf = K.sb("bsf", [1, D], F32)
        K.dma("sp", spf[:], gm_w_sp.rearrange("g p q -> p g q"), spf)
        K.copy("dve", spb[:], spf[:], [spf], [spb])
        for half in range(2):
            pb = PS[half]
            pv = pb.t[:].bitcast(BF16)
            for gg_ in range(8):
                g = half * 8 + gg_
                K.tr(pv[:, gg_ * 128:(gg_ + 1) * 128], spb[:, g, :], ident[:], [spb, ident], [pb], inc=(gg_ == 7))
            K.copy("act", spT[:, half * 8:(half + 1) * 8, :], pv.rearrange("p (g t) -> p g t", g=8), [pb], [spT])
        K.dma("sp", bsf[:], gm_b_sp.ap(), bsf)
        K.copy("dve", bsb[:], bsf[:], [bsf], [bsb])

        def loadh(c):
            if c < 4:
                K.dma("sp", hT[c % 2][:], hTs[:, :, c * 512:(c + 1) * 512].rearrange("k p t -> p k t"), hT[c % 2], hTs)

        def loadg(c):
            if c < 4:
                K.dma("sp", gu[0][:], ggTu[:, :, c * 512:(c + 1) * 512].rearrange("k p t -> p k t"), gu[0], ggTu)
        loadh(0)
        loadg(0)
        S.barrier()
        K.off = mark
        vs = K.sbs("v", [128, D], F32, 2)
        vgs = K.sbs("vg", [128, D], F32, 2)
        junk = K.sb("junk", [128, D], BF16)
        vlns = K.sbs("vln", [128, D], BF16, 2)

        def s1(t):
            c, tt = t // 4, t % 4
            h = hT[c % 2]
            v = vs[t % 2]
            ts_ = slice(tt * 128, (tt + 1) * 128)
            if tt == 0:
                loadh(c + 1)
            for k in range(16):
                for fb in range(4):
                    K.mm(PS[fb][:], h[:, k, ts_], W[:, k, fb * 512:(fb + 1) * 512], k == 0, k == 15, [h, Wp[fb]], [PS[fb]])
            for fb in range(4):
                fs = slice(fb * 512, (fb + 1) * 512)
                K.tt("dve", v[:, fs], PS[fb][:], bvb[:, fs], ALU.add, [PS[fb], bvb], [v])

        def s2(t):
            v, vg, vln, st4 = vs[t % 2], vgs[t % 2], vlns[t % 2], st[t % 2]
            K.act(vg[:], v[:], AF.Gelu_apprx_tanh, [v], [vg, st4], accum=st4[:, 0:1])
            K.act(junk[:], vg[:], AF.Square, [vg], [junk, st4], accum=st4[:, 1:2])
            K.ts("dve", st4[:, 2:3], st4[:, 0:1], 1.0 / D, None, ALU.mult, None, [st4], [st4])
            K.tt("dve", st4[:, 3:4], st4[:, 2:3], st4[:, 2:3], ALU.mult, [st4], [st4])
            K.stt("dve", st4[:, 4:5], st4[:, 1:2], 1.0 / D, st4[:, 3:4], ALU.mult, ALU.subtract, [st4], [st4])
            rstd_from_ssq(st4[:, 4:5], st4[:, 5:6], 1.0, [st4], [st4])
            K.stt("dve", st4[:, 6:7], st4[:, 2:3], -1.0, st4[:, 5:6], ALU.mult, ALU.mult, [st4], [st4])
            K.act(v[:], vg[:], AF.Identity, [vg, st4], [v], scale=st4[:, 5:6], bias=st4[:, 6:7])
            K.tt("dve", vg[:], v[:], vgb[:], ALU.mult, [v, vgb], [vg])
            K.tt("pool", vln[:], vg[:], vbb[:], ALU.add, [vg, vbb], [vln])

        def s3(t):
            c, tt = t // 4, t % 4
            vln = vlns[t % 2]
            G, P = gu[0], pst[0]
            ts_ = slice(tt * 128, (tt + 1) * 128)
            for g in range(16):
                pb = PS[4 + g // 4]
                oc = slice((g % 4) * 128, (g % 4 + 1) * 128)
                K.mm(pb[:, oc], vln[:, g * 128:(g + 1) * 128], spT[:, g, :], True, False, [vln, spT], [pb], inc=False)
                K.mm(pb[:, oc], ones[0:1, :], bsb[0:1, g * 128:(g + 1) * 128], False, True, [ones, bsb], [pb], inc=(g % 4 == 3))
            for gq in range(4):
                K.tt("dve", P[:, gq * 4:(gq + 1) * 4, ts_], PS[4 + gq][:].rearrange("p (g t) -> p g t", g=4),
                     G[:, gq * 4:(gq + 1) * 4, ts_], ALU.mult, [PS[4 + gq], G], [P])
            if tt == 3:
                K.dma("sp", catT[:, :, c * 512:(c + 1) * 512].rearrange("k p t -> p k t"), P[:], catT, P)
                loadg(c + 1)

        for step in range(18):
            if step < 16:
                s1(step)
            if 0 <= step - 1 < 16:
                s2(step - 1)
            if 0 <= step - 2 < 16:
                s3(step - 2)
        S.barrier()

    ggTu = dscr("ggTu", [16, 128, HALF], BF16)

    phases = [
        ("mod", phase_mod),
        ("inproj", lambda: (phase_inproj("qg"), phase_inproj("kvx"))),
        ("attn", phase_attn),
        ("rnn", phase_rnn),
        ("outproj0", lambda: phase_outproj(ar_w_out, 0, xin_b, x1s, hTs)),
        ("up0", lambda: phase_up(w_ff_in[0], DFF, hTs, uT, "relu2")),
        ("down0", lambda: phase_down(w_ff_out[0])),
        ("epi0", lambda: phase_ffn_epi(0, x1s, x2s, True, 6, hTs)),
        ("gmu", lambda: phase_up(gm_w_in, D, hTs, ggTu, "gelu", gm_bu)),
        ("gmv", phase_gm_v),
        ("outproj1", lambda: phase_outproj(gm_w_out, 6, x2s, x1s, hTs)),
        ("up1", lambda: phase_up(w_ff_in[1], DFF, hTs, uT, "relu2")),
        ("down1", lambda: phase_down(w_ff_out[1])),
        ("epi1", lambda: phase_ffn_epi(6, x1s, out, False, 0, None)),
    ]
    for i, (name, fn) in enumerate(phases):
        if only is not None and name not in only:
            continue
        fn()
    S.barrier()
    S.emit()
    return nc


def _rope_tables():
    rows = SEQ // 64
    r_idx, c_idx = np.meshgrid(np.arange(rows), np.arange(64), indexing="ij")
    r_idx = r_idx.reshape(-1).astype(np.float32)
    c_idx = c_idx.reshape(-1).astype(np.float32)
    freqs = (np.float32(10000.0) ** (-np.arange(32, dtype=np.float32) / np.float32(32))).astype(np.float32)
    ang_r = r_idx[:, None] * freqs
    ang_c = c_idx[:, None] * freqs
    ang = np.concatenate([ang_r, ang_r, ang_c, ang_c], axis=1)
    cosT = np.ascontiguousarray(np.cos(ang).T.astype(np.float32))
    sinT = np.ascontiguousarray(np.sin(ang).T.astype(np.float32))
    Pm = np.zeros((128, 128), np.float32)
    for d in range(128):
        blk = d // 32
        if blk % 2 == 0:
            Pm[d, d + 32] = -1.0
        else:
            Pm[d, d - 32] = 1.0
    ropeP = np.ascontiguousarray(Pm.T)
    return cosT, sinT, ropeP


def make_in_maps(inp, cores=range(8)):
    f = lambda a: np.ascontiguousarray(a, dtype=np.float32)
    cosT, sinT, ropeP = _rope_tables()
    shared = {
        "w_mod": f(inp["w_mod"]), "b_mod": f(inp["b_mod"]), "norm_g": f(inp["norm_g"].reshape(2, 4 * D)),
        "w_ff_in": f(inp["w_ff_in"]), "w_ff_out": f(inp["w_ff_out"]),
        "ar_w_in": f(inp["ar_w_in"][0]), "ar_w_out": f(inp["ar_w_out"][0]),
        "qkg": f(np.stack([inp["ar_q_g"][0], inp["ar_k_g"][0]], axis=1)),
        "ropeP": ropeP,
        "convb": f(inp["ar_conv_b"][0].reshape(8, 128).T),
        "gm_w_in": f(inp["gm_w_in"][0]),
        "gm_bu": f(inp["gm_b_in"][0][:D].reshape(16, 128).T),
        "gm_rows": f(np.stack([inp["gm_b_in"][0][D:], inp["gm_v_g"][0], inp["gm_v_b"][0]], axis=0)),
        "gm_w_out": f(inp["gm_w_out"][0]),
    }
    cw = inp["ar_conv_w"][0]
    maps = []
    for core in cores:
        b, half = core // 2, core % 2
        m = dict(shared)
        x = inp["x"][b]
        cx = inp["ctx"][b]
        if half == 0:
            xin = np.concatenate([x[:HALF], x[HALF:], cx], axis=0)
            cs, sn = cosT, sinT
            tp = np.stack([cw[0], cw[1], cw[2], cw[3], np.zeros_like(cw[0])], axis=1)
            dsel = [0, 1]
            wsp = inp["gm_w_sp"][0]
            bsp = inp["gm_b_sp"][0]
        else:
            xr = x[::-1]
            xin = np.concatenate([xr[:HALF], xr[HALF:], cx[::-1]], axis=0)
            cs, sn = cosT[:, ::-1], sinT[:, ::-1]
            tp = np.stack([np.zeros_like(cw[0]), cw[3], cw[2], cw[1], cw[0]], axis=1)
            dsel = [1, 0]
            wsp = inp["gm_w_sp"][0][:, ::-1, ::-1]
            bsp = inp["gm_b_sp"][0][:, ::-1]
        m["xin"] = f(xin)
        m["cosT"] = f(cs)
        m["sinT"] = f(sn)
        m["taps"] = f(tp.reshape(8, 128, 5).transpose(1, 0, 2))
        m["cvec"] = f(np.stack([inp["c"][b].reshape(16, 128).T, inp["c_ctx"].reshape(16, 128).T], axis=2))
        m["ar_wa"] = f(inp["ar_wa"][0][dsel])
        m["ar_wx"] = f(inp["ar_wx"][0][dsel])
        pr = np.stack([inp["ar_ba"][0][dsel], inp["ar_bx"][0][dsel], inp["ar_lambda"][0][dsel]], axis=0)
        m["rnnp"] = f(pr.reshape(3, 2, 8, 128).transpose(3, 0, 1, 2))
        m["gm_w_sp"] = f(wsp)
        m["gm_b_sp"] = f(bsp.reshape(1, D))
        maps.append(m)
    return maps


_NC_CACHE = {}


def kernel(**inputs):
    inp = {k: np.asarray(v) for k, v in inputs.items()}
    if "nc" not in _NC_CACHE:
        _NC_CACHE["nc"] = build()
    nc = _NC_CACHE["nc"]
    maps = make_in_maps(inp)
    res = run_bass_kernel_spmd(nc, maps, core_ids=list(range(8)))
    out = np.empty((4, SEQ, D), np.float32)
    for core in range(8):
        b, half = core // 2, core % 2
        o = np.asarray(res.results[core]["out"], dtype=np.float32)
        if half == 0:
            out[b, :HALF] = o
        else:
            out[b, HALF:] = o[::-1]
    return out
```

```python
import contextlib
import numpy as np
import concourse.bass as bass
import concourse.mybir as mybir
from concourse.bass_utils import run_bass_kernel_spmd

F32 = mybir.dt.float32
BF16 = mybir.dt.bfloat16
AF = mybir.ActivationFunctionType
ALU = mybir.AluOpType
AX = mybir.AxisListType

D = 2048
SEQ = 4096
HALF = 2048
CTX = 256
NTOK = SEQ + CTX
DFF = 8192
ARIN = 3584
EPS = 1e-6
XL = 4 + SEQ
XC = 4 + CTX


class Tile:
    __slots__ = ("name", "w", "r", "dsem", "dcnt")

    def __init__(self, name):
        self.name = name
        self.w = None
        self.r = []
        self.dsem = None
        self.dcnt = 0


class Buf:
    def __init__(self, t, name, dram=False):
        self.t = t
        self.T = Tile(name)
        self.dram = dram

    def __getitem__(self, k):
        return self.t[k]


class Sched:
    ENGS = ("pe", "act", "dve", "pool", "sp")

    def __init__(self, nc, stack):
        self.nc = nc
        self.stack = stack
        self.ops = {e: [] for e in self.ENGS}
        self.sem = {e: stack.enter_context(nc.semaphore("s_" + e)) for e in self.ENGS}
        self.cnt = {e: 0 for e in self.ENGS}
        self.waited = {e: {} for e in self.ENGS}
        self.dtiles = []
        self.free_dsems = []
        self.nsem = 0
        self.pool_fifo = []

    def _dsem(self, t):
        if t.dsem is None:
            if self.free_dsems:
                t.dsem, t.dcnt = self.free_dsems.pop()
            else:
                self.nsem += 1
                t.dsem = self.stack.enter_context(self.nc.semaphore("d%d" % self.nsem))
                t.dcnt = 0
            self.dtiles.append(t)
        return t.dsem

    def _waits(self, eng, deps):
        need = {}
        for (sem, val) in deps:
            if eng == "pe" and sem is self.sem["pe"]:
                continue
            if need.get(sem, 0) < val:
                need[sem] = val
        out = []
        wd = self.waited[eng]
        for sem, val in need.items():
            if wd.get(sem, 0) < val:
                wd[sem] = val
                out.append((sem, val))
        return out

    def op(self, eng, fn, reads=(), writes=(), inc=True):
        deps = []
        for b in reads:
            t = b.T
            if t.w is not None:
                deps.append(t.w)
        for b in writes:
            t = b.T
            if t.w is not None:
                deps.append(t.w)
            deps.extend(t.r)
        waits = self._waits(eng, deps)
        val = self.cnt[eng] + 1
        if inc:
            self.cnt[eng] = val
        ev = (self.sem[eng], val)
        for b in reads:
            b.T.r.append(ev)
        for b in writes:
            b.T.w = ev
            b.T.r = []
        self.ops[eng].append((waits, fn, (self.sem[eng], 1) if inc else None))

    def dma(self, eng, out, in_, dst, src=None):
        deps = []
        if dst.dram:
            own = src.T
            if own.w is not None:
                deps.append(own.w)
        else:
            own = dst.T
            if src is not None and not src.dram and src.T.w is not None:
                deps.append(src.T.w)
            if own.w is not None:
                deps.append(own.w)
            deps.extend(own.r)
        if eng == "pool":
            while len(self.pool_fifo) >= 3:
                deps.append(self.pool_fifo.pop(0))
        waits = self._waits(eng, deps)
        sem = self._dsem(own)
        own.dcnt += 16
        ev = (sem, own.dcnt)
        if eng == "pool":
            self.pool_fifo.append(ev)
        if dst.dram:
            own.r.append(ev)
        else:
            if src is not None and not src.dram:
                src.T.r.append(ev)
            own.w = ev
            own.r = []
        self.ops[eng].append((waits, lambda e: e.dma_start(out=out, in_=in_), (sem, 16)))

    def barrier(self):
        evs = [(self.sem[e], self.cnt[e]) for e in self.ENGS if self.cnt[e] > 0]
        evs += [(t.dsem, t.dcnt) for t in self.dtiles if t.dcnt > 0]
        for e in self.ENGS:
            deps = [ev for ev in evs if ev[0] is not self.sem[e]]
            waits = self._waits(e, deps)
            if waits:
                self.ops[e].append((waits, None, None))
        self.pool_fifo = []
        for t in self.dtiles:
            self.free_dsems.append((t.dsem, t.dcnt))
            t.dsem = None
            t.w = None
            t.r = []
        self.dtiles = []

    def emit(self):
        nc = self.nc
        handles = {"pe": "tensor", "act": "scalar", "dve": "vector", "pool": "gpsimd", "sp": "sync"}
        with nc.Block() as block:
            for e in self.ENGS:
                ops = self.ops[e]

                def body(eng, ops=ops):
                    for waits, fn, inc in ops:
                        for sem, val in waits:
                            eng.wait_ge(sem, val)
                        if fn is not None:
                            ins = fn(eng)
                            if inc is not None:
                                ins.then_inc(inc[0], inc[1])
                getattr(block, handles[e])(body)


class KB:
    def __init__(self, nc, stack):
        self.nc = nc
        self.S = Sched(nc, stack)
        self.off = 18688
        self.uid = 0
        self.base = 18688

    def reset(self):
        self.off = self.base

    def sb(self, name, shape, dtype):
        nb = int(np.prod(shape[1:])) * (4 if dtype == F32 else 2)
        nb = (nb + 63) // 64 * 64
        self.uid += 1
        t = self.nc.alloc_sbuf_tensor_at("%s_%d" % (name, self.uid), list(shape), dtype, offset=self.off)
        self.off += nb
        assert self.off <= 229376, ("SBUF overflow", name, self.off)
        return Buf(t, name)

    def sbs(self, name, shape, dtype, n):
        return [self.sb(name + str(i), shape, dtype) for i in range(n)]

    def act(self, out, in_, func, r, w, scale=1.0, bias=0.0, accum=None):
        if accum is None:
            fn = lambda e: e.activation(out=out, in_=in_, func=func, bias=bias, scale=scale)
        else:
            fn = lambda e: e.activation(out=out, in_=in_, func=func, bias=bias, scale=scale, accum_out=accum)
        self.S.op("act", fn, r, w)

    def ts(self, eng, out, in0, s1, s2, op0, op1, r, w):
        if op1 is None:
            fn = lambda e: e.tensor_scalar(out=out, in0=in0, scalar1=s1, scalar2=None, op0=op0)
        else:
            fn = lambda e: e.tensor_scalar(out=out, in0=in0, scalar1=s1, scalar2=s2, op0=op0, op1=op1)
        self.S.op(eng, fn, r, w)

    def tt(self, eng, out, in0, in1, op, r, w):
        self.S.op(eng, lambda e: e.tensor_tensor(out=out, in0=in0, in1=in1, op=op), r, w)

    def stt(self, eng, out, in0, scalar, in1, op0, op1, r, w):
        self.S.op(eng, lambda e: e.scalar_tensor_tensor(out=out, in0=in0, scalar=scalar, in1=in1, op0=op0, op1=op1), r, w)

    def copy(self, eng, out, in_, r, w):
        if eng == "act":
            self.S.op("act", lambda e: e.activation(out=out, in_=in_, func=AF.Copy), r, w)
        else:
            self.S.op(eng, lambda e: e.tensor_copy(out=out, in_=in_), r, w)

    def memset(self, eng, ap, val, w):
        self.S.op(eng, lambda e: e.memset(ap, val), (), w)

    def mm(self, out, lhsT, rhs, start, stop, r, w, inc=None):
        if inc is None:
            inc = stop
        self.S.op("pe", lambda e: e.matmul(out, lhsT=lhsT, rhs=rhs, start=start, stop=stop), r, w, inc=inc)

    def tr(self, out, in_, ident, r, w, inc):
        self.S.op("pe", lambda e: e.transpose(out, in_, ident), r, w, inc=inc)

    def dma(self, eng, out, in_, dst, src=None):
        self.S.dma(eng, out, in_, dst, src)


def build(only=None, dbg=(), ext_in=()):
    nc = bass.Bass("TRN2", target_bir_lowering=False)
    st = contextlib.ExitStack()
    with st:
        return _build(nc, st, only, dbg, ext_in)


class _LazyIn:
    def __init__(self, nc, name, shape, used):
        self.nc, self.name, self.shape, self.used = nc, name, list(shape), used
        self._ap = None

    def ap(self):
        if self._ap is None:
            self._ap = self.nc.dram_tensor(self.name, self.shape, F32, kind="ExternalInput").ap()
            self.used.append(self.name)
        return self._ap

    def __getitem__(self, k):
        return self.ap()[k]

    def rearrange(self, *a, **kw):
        return self.ap().rearrange(*a, **kw)


def _build(nc, st, only, dbg, ext_in):
    used_inputs = []
    nc._used_inputs = used_inputs

    def din(name, shape, dt=F32):
        return _LazyIn(nc, name, shape, used_inputs)

    def dscr(name, shape, dt):
        kind = "ExternalOutput" if name in dbg else ("ExternalInput" if name in ext_in else "Internal")
        if kind == "ExternalInput":
            used_inputs.append(name)
        return Buf(nc.dram_tensor(name, list(shape), dt, kind=kind).ap(), name, dram=True)

    xin = din("xin", [NTOK, D])
    cvec = din("cvec", [128, 16, 2])
    w_mod = din("w_mod", [2, D, 6 * D])
    b_mod = din("b_mod", [2, 6 * D])
    norm_g = din("norm_g", [2, 4 * D])
    w_ff_in = din("w_ff_in", [2, D, DFF])
    w_ff_out = din("w_ff_out", [2, DFF, D])
    ar_w_in = din("ar_w_in", [D, ARIN])
    ar_w_out = din("ar_w_out", [D, D])
    qkg = din("qkg", [128, 2])
    ropeP = din("ropeP", [128, 128])
    cosT = din("cosT", [128, SEQ])
    sinT = din("sinT", [128, SEQ])
    taps = din("taps", [128, 8, 5])
    convb = din("convb", [128, 8])
    ar_wa = din("ar_wa", [2, 8, 128, 128])
    ar_wx = din("ar_wx", [2, 8, 128, 128])
    rnnp = din("rnnp", [128, 3, 2, 8])
    gm_w_in = din("gm_w_in", [D, 2 * D])
    gm_bu = din("gm_bu", [128, 16])
    gm_rows = din("gm_rows", [3, D])
    gm_w_sp = din("gm_w_sp", [16, 128, 128])
    gm_b_sp = din("gm_b_sp", [1, D])
    gm_w_out = din("gm_w_out", [D, D])
    out = Buf(nc.dram_tensor("out", [HALF, D], F32, kind="ExternalOutput").ap(), "out", dram=True)
    xin_b = Buf(xin, "xin", dram=True)

    modv = dscr("modv", [14, D], F32)
    hT0 = dscr("hT0", [16, 128, HALF], BF16)
    qT = dscr("qT", [8, 128, HALF], BF16)
    kT = dscr("kT", [2, 128, NTOK], BF16)
    Vs = dscr("Vs", [NTOK, 256], BF16)
    xrT = dscr("xrT", [8, 128, NTOK], F32)
    ggT = dscr("ggT", [8, 128, HALF], F32)
    catT = dscr("catT", [16, 128, HALF], BF16)
    x1s = dscr("x1s", [HALF, D], F32)
    x2s = dscr("x2s", [HALF, D], F32)
    hTs = dscr("hTs", [16, 128, HALF], BF16)
    uT = dscr("uT", [8, 128, 64, 256], BF16)
    ffo = dscr("ffo", [HALF, D], F32)

    K = KB(nc, st)
    S = K.S

    PS = [Buf(nc.alloc_psum_tensor("ps%d" % i, [128, 512], F32), "ps%d" % i) for i in range(8)]

    ident = K.sb("ident", [128, 128], BF16)
    ones = K.sb("ones", [128, 128], BF16)
    idf = K.sb("idf", [128, 128], F32)
    K.memset("pool", idf[:], 1.0, [idf])
    S.op("pool", lambda e: e.affine_select(out=idf[:], in_=idf[:], pattern=[[-1, 128]], compare_op=ALU.is_equal,
                                            fill=0.0, base=0, channel_multiplier=1), [idf], [idf])
    K.copy("dve", ident[:], idf[:], [idf], [ident])
    K.memset("pool", ones[:], 1.0, [ones])
    K.base = K.off

    def rstd_from_ssq(ssq_ap, out_ap, n, r, w):
        K.act(out_ap, ssq_ap, AF.Sqrt, r, w, scale=1.0 / n, bias=EPS)
        S.op("dve", lambda e: e.reciprocal(out=out_ap, in_=out_ap), w, w)

    def make_norm_bufs(pairs, n=2):
        return {"tmp": K.sbs("ntmp", [128, D], F32, n), "hb": K.sbs("nhb", [128, D], BF16, n),
                "junk": K.sb("njunk", [128, D], BF16), "ssq": K.sbs("nssq", [128, 2], F32, n),
                "rstd": K.sbs("nrstd", [128, 2], F32, n), "pairs": pairs, "i": 0}

    def norm_mod_transpose(xt, Abc, Bbc, NB, dst_ap, dst_buf):
        i = NB["i"]
        NB["i"] += 1
        n = len(NB["tmp"])
        tmp, hb, ssq, rstd, junk = NB["tmp"][i % n], NB["hb"][i % n], NB["ssq"][i % n], NB["rstd"][i % n], NB["junk"]
        pbanks = NB["pairs"][i % len(NB["pairs"])]
        K.act(junk[:], xt[:], AF.Square, [xt], [junk, ssq], accum=ssq[:, 0:1])
        rstd_from_ssq(ssq[:, 0:1], rstd[:, 0:1], D, [ssq], [rstd])
        K.stt("dve", tmp[:], xt[:], rstd[:, 0:1], Abc[:], ALU.mult, ALU.mult, [xt, rstd, Abc], [tmp])
        K.tt("pool", hb[:], tmp[:], Bbc[:], ALU.add, [tmp, Bbc], [hb])
        for half in range(2):
            pb = pbanks[half]
            pv = pb.t[:].bitcast(BF16)
            for kk in range(8):
                k = half * 8 + kk
                K.tr(pv[:, kk * 128:(kk + 1) * 128], hb[:, k * 128:(k + 1) * 128], ident[:], [hb, ident], [pb], inc=(kk == 7))
            K.copy("act", dst_ap[:, half * 8:(half + 1) * 8, :], pv.rearrange("p (k t) -> p k t", k=8), [pb], [dst_buf])

    def load_bcast(buf, row_ap, src):
        K.dma("sp", buf[:], row_ap.partition_broadcast(128), buf, src)

    def mod_setup(l, nbuf):
        M = {"l": l, "nbuf": nbuf}
        M["cv"] = K.sb("cv", [128, 16, 2], F32)
        M["sv"] = K.sb("sv", [128, 16, 2], BF16)
        M["wt"] = K.sbs("wmod", [128, 16, 512], BF16, nbuf)
        M["raw"] = K.sb("raw", [2, 6 * D], F32)
        M["gg"] = K.sb("gg", [2, 4 * D], F32)
        K.dma("sp", M["cv"][:], cvec.ap(), M["cv"])
        K.act(M["sv"][:], M["cv"][:], AF.Silu, [M["cv"]], [M["sv"]])
        K.dma("sp", M["raw"][:], b_mod[l:l + 1, :].partition_broadcast(2), M["raw"])
        K.dma("sp", M["gg"][:], norm_g[l:l + 1, :].partition_broadcast(2), M["gg"])
        return M

    def mod_load(M, n):
        if n < 24:
            w = M["wt"][n % M["nbuf"]]
            K.dma("pool", w[:], w_mod[M["l"], :, n * 512:(n + 1) * 512].rearrange("(k p) n -> p k n", p=128), w)

    def mod_tile(M, n, pb):
        w, sv, raw = M["wt"][n % M["nbuf"]], M["sv"], M["raw"]
        for k in range(16):
            K.mm(pb[0:2, :], sv[:, k, :], w[:, k, :], k == 0, k == 15, [sv, w], [pb])
        K.tt("dve", raw[:, n * 512:(n + 1) * 512], pb[0:2, :], raw[:, n * 512:(n + 1) * 512], ALU.add, [pb, raw], [raw])

    def mod_finish(M):
        l, raw, gg = M["l"], M["raw"], M["gg"]
        K.stt("dve", raw[:, D:2 * D], raw[:, D:2 * D], 1.0, gg[:, 0:D], ALU.add, ALU.mult, [raw, gg], [raw])
        K.tt("dve", raw[:, 2 * D:3 * D], raw[:, 2 * D:3 * D], gg[:, D:2 * D], ALU.mult, [raw, gg], [raw])
        K.stt("dve", raw[:, 4 * D:5 * D], raw[:, 4 * D:5 * D], 1.0, gg[:, 2 * D:3 * D], ALU.add, ALU.mult, [raw, gg], [raw])
        K.tt("dve", raw[:, 5 * D:6 * D], raw[:, 5 * D:6 * D], gg[:, 3 * D:4 * D], ALU.mult, [raw, gg], [raw])
        for r, src in enumerate((1, 0, 2, 4, 3, 5)):
            K.dma("sp", modv[6 * l + r:6 * l + r + 1, :], raw[0:1, src * D:(src + 1) * D], modv, raw)
        if l == 0:
            K.dma("sp", modv[12:13, :], raw[1:2, D:2 * D], modv, raw)
            K.dma("sp", modv[13:14, :], raw[1:2, 0:D], modv, raw)

    def phase_mod():
        K.reset()
        M = mod_setup(0, 4)
        mod_load(M, 0)
        mod_load(M, 1)
        for n in range(24):
            mod_load(M, n + 2)
            mod_tile(M, n, PS[n % 2])
        mod_finish(M)
        S.barrier()

    def phase_inproj(mode):
        K.reset()
        if mode == "qg":
            col0 = [0, 512, 2560, 3072]
        else:
            col0 = [1024, 1536, 2048]
        npan = len(col0)
        W = K.sb("Win", [128, 16, npan * 512], BF16)
        Wp = [Buf(W.t, "Winp%d" % i) for i in range(npan)]
        for p in range(npan):
            K.dma("pool", W[:, :, p * 512:(p + 1) * 512],
                  ar_w_in[:, col0[p]:col0[p] + 512].rearrange("(k p) n -> p k n", p=128), Wp[p])

        def wloc(ocol):
            for p in range(npan):
                if col0[p] <= ocol < col0[p] + 512:
                    return p * 512 + ocol - col0[p], p
            raise KeyError(ocol)
        Abc = K.sb("Abc", [128, D], F32)
        Bbc = K.sb("Bbc", [128, D], F32)
        xt = K.sbs("xt", [128, D], F32, 2)
        NB = make_norm_bufs([(PS[0], PS[1])])
        hT = K.sbs("hT", [128, 16, 512], BF16, 2)
        qk = K.sb("qkg", [128, 2], F32)
        Pm = K.sb("Pm", [128, 128], BF16)
        Pf = K.sb("Pf", [128, 128], F32)
        cs = K.sbs("cs", [128, 512], F32, 2)
        sn = K.sbs("sn", [128, 512], F32, 2)
        sq = K.sbs("sq", [128, 512], BF16, 2)
        qg = K.sbs("qg", [128, 512], BF16, 2)
        rs = K.sbs("rs", [128, 512], F32, 2)
        t1 = K.sbs("t1", [128, 512], F32, 2)
        t2 = K.sbs("t2", [128, 512], F32, 2)
        ob = K.sbs("ob", [128, 512], BF16, 3)
        of = K.sbs("of", [128, 512], F32, 3)
        vb = K.sbs("vb", [128, 256], BF16, 2)
        K.dma("sp", qk[:], qkg.ap(), qk)
        K.dma("sp", Pf[:], ropeP.ap(), Pf)
        K.copy("dve", Pm[:], Pf[:], [Pf], [Pm])
        load_bcast(Abc, modv[0:1, :], modv)
        load_bcast(Bbc, modv[1:2, :], modv)

        cnt = {"x": 0, "ps": 0, "o": 0, "f": 0, "v": 0, "qk": 0}

        def proj_cols(h, NT, n):
            lc, p = wloc(n * 128)
            pb = PS[2 + cnt["ps"] % 3]
            cnt["ps"] += 1
            for k in range(16):
                K.mm(pb[:, 0:NT], W[:, k, lc:lc + 128], h[:, k, 0:NT], k == 0, k == 15, [Wp[p], h], [pb])
            return pb

        def do_qk(pb, NT, gcol, rope_t0, dst_ap, dst_buf):
            i = cnt["qk"] % 2
            cnt["qk"] += 1
            K.act(sq[i][:, 0:NT], pb[:, 0:NT], AF.Square, [pb], [sq[i]])
            K.act(qg[i][:, 0:NT], pb[:, 0:NT], AF.Identity, [pb, qk], [qg[i]], scale=qk[:, gcol:gcol + 1])
            K.mm(PS[5][:, 0:NT], ones[:], sq[i][:, 0:NT], True, True, [ones, sq[i]], [PS[5]])
            rstd_from_ssq(PS[5][:, 0:NT], rs[i][:, 0:NT], 128, [PS[5]], [rs[i]])
            o = ob[cnt["o"] % 3]
            cnt["o"] += 1
            if rope_t0 is None:
                K.tt("pool", o[:, 0:NT], qg[i][:, 0:NT], rs[i][:, 0:NT], ALU.mult, [qg[i], rs[i]], [o])
            else:
                K.mm(PS[6][:, 0:NT], Pm[:], qg[i][:, 0:NT], True, True, [Pm, qg[i]], [PS[6]])
                K.tt("pool", t1[i][:, 0:NT], qg[i][:, 0:NT], cs[rope_t0][:, 0:NT], ALU.mult, [qg[i], cs[rope_t0]], [t1[i]])
                K.tt("dve", t2[i][:, 0:NT], PS[6][:, 0:NT], sn[rope_t0][:, 0:NT], ALU.mult, [PS[6], sn[rope_t0]], [t2[i]])
                K.tt("pool", t1[i][:, 0:NT], t1[i][:, 0:NT], t2[i][:, 0:NT], ALU.add, [t1[i], t2[i]], [t1[i]])
                K.tt("pool", o[:, 0:NT], t1[i][:, 0:NT], rs[i][:, 0:NT], ALU.mult, [t1[i], rs[i]], [o])
            K.dma("sp", dst_ap, o[:, 0:NT], dst_buf, o)

        nch = 4 if mode == "qg" else 9

        def prep_tasks(c):
            if c >= nch:
                return []
            isctx = c == 8
            NT = 256 if isctx else 512
            t0 = c * 512
            h = hT[c % 2]
            tasks = []
            if isctx:
                def t_bc():
                    load_bcast(Abc, modv[12:13, :], modv)
                    load_bcast(Bbc, modv[13:14, :], modv)
                tasks.append(t_bc)
            if c < 4 and mode == "kvx":
                tasks.append(lambda: K.dma("sp", h[:], hT0[:, :, t0:t0 + 512].rearrange("k p t -> p k t"), h, hT0))
            else:
                for tt in range(NT // 128):
                    def t_norm(tt=tt):
                        x = xt[cnt["x"] % 2]
                        cnt["x"] += 1
                        K.dma("sp", x[:], xin[t0 + tt * 128:t0 + (tt + 1) * 128, :], x)
                        norm_mod_transpose(x, Abc, Bbc, NB, h[:, :, tt * 128:(tt + 1) * 128], h)
                    tasks.append(t_norm)
            if not isctx:
                def t_cs():
                    ci = c % 2
                    K.dma("sp", cs[ci][:], cosT[:, t0:t0 + 512], cs[ci])
                    K.dma("sp", sn[ci][:], sinT[:, t0:t0 + 512], sn[ci])
                tasks.append(t_cs)
            return tasks

        for tk in prep_tasks(0):
            tk()
        for c in range(nch):
            isctx = c == 8
            NT = 256 if isctx else 512
            t0 = c * 512
            h = hT[c % 2]
            ri = None if isctx else c % 2
            pend = prep_tasks(c + 1)
            state = {"n": 0}

            def tick(every):
                state["n"] += 1
                if pend and state["n"] % every == 0:
                    pend.pop(0)()
            if mode == "qg":
                K.dma("sp", hT0[:, :, t0:t0 + 512].rearrange("k p t -> p k t"), h[:], hT0, h)
                for hd in range(8):
                    pb = proj_cols(h, NT, hd)
                    do_qk(pb, NT, 0, ri, qT[hd, :, t0:t0 + NT], qT)
                    tick(2)
                for n in range(8):
                    pb = proj_cols(h, NT, 20 + n)
                    o = of[cnt["f"] % 3]
                    cnt["f"] += 1
                    K.act(o[:, 0:NT], pb[:, 0:NT], AF.Gelu_apprx_tanh, [pb], [o])
                    K.dma("act", ggT[n, :, t0:t0 + NT], o[:, 0:NT], ggT, o)
                    tick(2)
            else:
                for kv in range(2):
                    pb = proj_cols(h, NT, 8 + kv)
                    do_qk(pb, NT, 1, ri, kT[kv, :, t0:t0 + NT], kT)
                    tick(1)
                vl, vp = wloc(1280)
                for tt in range(NT // 128):
                    pb = PS[2 + cnt["ps"] % 3]
                    cnt["ps"] += 1
                    for k in range(16):
                        K.mm(pb[:, 0:256], h[:, k, tt * 128:(tt + 1) * 128], W[:, k, vl:vl + 256], k == 0, k == 15, [h, Wp[vp]], [pb])
                    v = vb[cnt["v"] % 2]
                    cnt["v"] += 1
                    K.copy("act", v[:], pb[:, 0:256], [pb], [v])
                    K.dma("act", Vs[t0 + tt * 128:t0 + (tt + 1) * 128, :], v[:], Vs, v)
                    tick(2)
                for n in range(8):
                    pb = proj_cols(h, NT, 12 + n)
                    o = of[cnt["f"] % 3]
                    cnt["f"] += 1
                    K.copy("act", o[:, 0:NT], pb[:, 0:NT], [pb], [o])
                    K.dma("act", xrT[n, :, t0:t0 + NT], o[:, 0:NT], xrT, o)
                    tick(2)
            while pend:
                pend.pop(0)()
        S.barrier()

    def phase_attn():
        K.reset()
        NKT = NTOK // 128
        kt_sb = K.sb("kTs", [128, 2, NTOK], BF16)
        v_sb = K.sb("Vsb", [128, NKT, 256], BF16)
        qs = K.sbs("qs", [128, 512], BF16, 3)
        pbuf = K.sbs("pexp", [128, 512], BF16, 4)
        rl = K.sbs("rl", [128, 512], F32, 2)
        ob = K.sbs("aob", [128, 512], BF16, 2)
        K.dma("sp", kt_sb[:], kT[:].rearrange("k p t -> p k t"), kt_sb, kT)
        v_h = [Buf(v_sb.t, "v_h%d" % i) for i in range(2)]
        for i in range(2):
            K.dma("sp", v_sb[:, i * 17:(i + 1) * 17, :],
                  Vs[i * 17 * 128:(i + 1) * 17 * 128, :].rearrange("(kt p) d -> p kt d", p=128), v_h[i], Vs)
        scale = 128.0 ** -0.5
        it = 0
        sc = 0
        M = mod_setup(1, 3)
        mod_load(M, 0)
        mod_load(M, 1)
        def loadq(i):
            if i < 32:
                K.dma("sp", qs[i % 3][:], qT[i // 4, :, (i % 4) * 512:(i % 4 + 1) * 512], qs[i % 3], qT)
        loadq(0)
        loadq(1)
        for hd in range(8):
            kv = hd // 4
            for qt in range(4):
                q = qs[it % 3]
                loadq(it + 2)
                Ob = PS[4 + 2 * (it % 2)]
                Lb = PS[5 + 2 * (it % 2)]
                sbanks = {}

                def issue_s(kt, q=q, kv=kv):
                    nonlocal sc
                    pb = PS[sc % 3]
                    sc += 1
                    K.mm(pb[:], kt_sb[:, kv, kt * 128:(kt + 1) * 128], q[:], True, True, [kt_sb, q], [pb])
                    sbanks[kt] = pb
                issue_s(0)
                issue_s(1)
                if it < 24:
                    mod_load(M, it + 2)
                    mod_tile(M, it, PS[3])
                for kt in range(NKT):
                    if kt + 2 < NKT:
                        issue_s(kt + 2)
                    pb = sbanks.pop(kt)
                    p = pbuf[(it * NKT + kt) % 4]
                    K.act(p[:], pb[:], AF.Exp, [pb], [p], scale=scale)
                    K.mm(Ob[:], v_sb[:, kt, kv * 128:(kv + 1) * 128], p[:], kt == 0, kt == NKT - 1, [v_h[kt // 17], p], [Ob], inc=(kt == NKT - 1))
                    K.mm(Lb[:], ones[:], p[:], kt == 0, kt == NKT - 1, [ones, p], [Lb], inc=True)
                r = rl[it % 2]
                o = ob[it % 2]
                S.op("dve", lambda e, r=r, Lb=Lb: e.reciprocal(out=r[:], in_=Lb[:]), [Lb], [r])
                K.tt("dve", o[:], Ob[:], r[:], ALU.mult, [Ob, r], [o])
                K.dma("sp", catT[hd, :, qt * 512:(qt + 1) * 512], o[:], catT, o)
                it += 1
        mod_finish(M)
        S.barrier()

    def phase_rnn():
        K.reset()
        tp = K.sb("taps", [128, 8, 5], F32)
        cb = K.sb("convb", [128, 8], F32)
        rp = K.sb("rnnp", [128, 3, 2, 8], F32)
        c1 = K.sb("c1", [128, 2, 8], F32)
        e1 = K.sb("e1", [128, 2, 8], F32)
        hb_ = K.sb("hbias", [128, 2, 2, 8], F32)
        waf = K.sbs("waf", [128, 128], F32, 2)
        wab = K.sbs("wab", [128, 128], BF16, 4)
        wxb = K.sbs("wxb", [128, 128], BF16, 4)
        xl = K.sbs("xl", [128, XL], F32, 1)
        xc = K.sbs("xc", [128, XC], F32, 2)
        accs = K.sbs("acc", [128, NTOK], F32, 2)
        accbs = K.sbs("accb", [128, NTOK], BF16, 2)
        NA = HALF + CTX
        dbuf = []
        for d, n_ in ((0, NA), (1, NTOK)):
            ent = {}
            for nm in ("a", "b", "s"):
                t = K.sb("g%s%d" % (nm, d), [128, n_], F32)
                ent[nm] = t
                ent[nm + "t"] = [Buf(t.t, "g%s%d_%d" % (nm, d, i)) for i in range(9)]
            dbuf.append(ent)
        rr = K.sbs("rr", [128, 512], F32, 2)
        ii = K.sbs("ii", [128, 512], F32, 2)
        hA = K.sb("hA", [128, HALF], F32)
        hB = K.sb("hB", [128, HALF], F32)
        hc = K.sb("hc", [128, CTX], F32)
        fin = K.sb("fin", [128, 2], F32)
        gg = K.sbs("gg", [128, HALF], F32, 1)
        ro = K.sbs("ro", [128, HALF], BF16, 2)
        K.dma("sp", tp[:], taps.ap(), tp)
        K.dma("sp", cb[:], convb.ap(), cb)
        K.dma("sp", rp[:], rnnp.ap(), rp)
        K.act(e1[:], rp[:, 2, :, :], AF.Exp, [rp], [e1], scale=-1.0)
        K.act(e1[:], e1[:], AF.Ln, [e1], [e1], bias=1.0)
        K.ts("dve", c1[:], e1[:], -4.0, None, ALU.mult, None, [e1], [c1])
        K.ts("dve", hb_[:], rp[:, 0:2, :, :], 0.5, None, ALU.mult, None, [rp], [hb_])
        for b_ in xl:
            K.memset("pool", b_[:, 0:2], 0.0, [b_])
            K.memset("pool", b_[:, XL - 2:XL], 0.0, [b_])
        for b_ in xc:
            K.memset("pool", b_[:, 0:2], 0.0, [b_])
            K.memset("pool", b_[:, XC - 2:XC], 0.0, [b_])
        cnt = {"wi": 0, "ti": 0, "gi": 0}
        wsel = {}

        def load_conv(n):
            X = xl[0]
            C = xc[n % 2]
            acc = accs[n % 2]
            accb = accbs[n % 2]
            K.dma("sp", X[:, 2:2 + SEQ], xrT[n, :, 0:SEQ], X, xrT)
            K.dma("sp", C[:, 2:2 + CTX], xrT[n, :, SEQ:NTOK], C, xrT)
            for (src, L, o0) in ((X, SEQ, 0), (C, CTX, SEQ)):
                K.act(acc[:, o0:o0 + L], src[:, 0:L], AF.Identity, [src, tp, cb], [acc], scale=tp[:, n, 0:1], bias=cb[:, n:n + 1])
                for j in range(1, 5):
                    K.stt("dve", acc[:, o0:o0 + L], src[:, j:j + L], tp[:, n, j:j + 1], acc[:, o0:o0 + L], ALU.mult, ALU.add, [src, tp, acc], [acc])
            K.copy("act", accb[:], acc[:], [acc], [accb])
            for d in range(2):
                wA = wab[cnt["wi"] % 4]
                wX = wxb[cnt["wi"] % 4]
                cnt["wi"] += 1
                for (srcw, dstw) in ((ar_wa, wA), (ar_wx, wX)):
                    f = waf[cnt["ti"] % 2]
                    cnt["ti"] += 1
                    K.dma("sp", f[:], srcw[d, n, :, :], f)
                    K.copy("dve", dstw[:], f[:], [f], [dstw])
                wsel[(n, d)] = (wA, wX)

        def gate_tiles(n, d):
            E = dbuf[d]
            acc = accs[n % 2]
            accb = accbs[n % 2]
            wA, wX = wsel[(n, d)]
            nlat = HALF if d == 0 else SEQ
            tiles = [(SEQ, nlat, CTX)] + [(c0, c0, 512) for c0 in range(0, nlat, 512)]
            for (c0, l0, NT) in tiles:
                i = cnt["gi"] % 2
                cnt["gi"] += 1
                lt = l0 // 512
                K.mm(PS[0 + i][:, 0:NT], wA[:], accb[:, c0:c0 + NT], True, True, [wA, accb], [PS[0 + i]])
                K.mm(PS[2 + i][:, 0:NT], wX[:], accb[:, c0:c0 + NT], True, True, [wX, accb], [PS[2 + i]])
                K.act(rr[i][:, 0:NT], PS[0 + i][:, 0:NT], AF.Tanh, [PS[0 + i], hb_], [rr[i]], scale=0.5, bias=hb_[:, 0, d, n:n + 1])
                K.act(ii[i][:, 0:NT], PS[2 + i][:, 0:NT], AF.Tanh, [PS[2 + i], hb_], [ii[i]], scale=0.5, bias=hb_[:, 1, d, n:n + 1])
                K.act(E["a"][:, l0:l0 + NT], rr[i][:, 0:NT], AF.Exp, [rr[i], c1], [E["at"][lt]], scale=c1[:, d, n:n + 1], bias=c1[:, d, n:n + 1])
                K.tt("pool", E["s"][:, l0:l0 + NT], E["a"][:, l0:l0 + NT], E["a"][:, l0:l0 + NT], ALU.mult, [E["at"][lt]], [E["st"][lt]])
                K.stt("dve", E["b"][:, l0:l0 + NT], ii[i][:, 0:NT], 1.0, acc[:, c0:c0 + NT], ALU.add, ALU.mult, [ii[i], acc], [E["bt"][lt]])

        def finalize(n, d):
            E = dbuf[d]
            nlat = HALF if d == 0 else SEQ
            nt = nlat // 512
            for (l0, L, tl) in ((0, nlat, list(range(nt))), (nlat, CTX, [nt])):
                st_ = [E["st"][t] for t in tl]
                bt_ = [E["bt"][t] for t in tl]
                K.act(E["s"][:, l0:l0 + L], E["s"][:, l0:l0 + L], AF.Sqrt, st_, st_, scale=-0.25, bias=0.25)
                K.tt("pool", E["b"][:, l0:l0 + L], E["b"][:, l0:l0 + L], E["s"][:, l0:l0 + L], ALU.mult, st_ + bt_, bt_)
            A_, B_ = E["a"], E["b"]
            at, bt = E["at"], E["bt"]
            if d == 0:
                S.op("dve", lambda e: e.tensor_tensor_scan(out=hc[:, :], data0=A_[:, HALF:NA], data1=B_[:, HALF:NA],
                                                           initial=0.0, op0=ALU.mult, op1=ALU.add), [at[4], bt[4]], [hc])
                S.op("dve", lambda e: e.tensor_tensor_scan(out=hA[:, :], data0=A_[:, 0:HALF], data1=B_[:, 0:HALF],
                                                           initial=hc[:, CTX - 1:CTX], op0=ALU.mult, op1=ALU.add), at[0:4] + bt[0:4] + [hc], [hA])
            else:
                S.op("dve", lambda e: e.tensor_tensor_scan(out=hc[:, ::-1], data0=A_[:, SEQ:NTOK][:, ::-1], data1=B_[:, SEQ:NTOK][:, ::-1],
                                                           initial=0.0, op0=ALU.mult, op1=ALU.add), [at[8], bt[8]], [hc])
                S.op("dve", lambda e: e.tensor_tensor_scan(out=hB[:, ::-1], data0=A_[:, HALF:SEQ][:, ::-1], data1=B_[:, HALF:SEQ][:, ::-1],
                                                           initial=hc[:, 0:1], op0=ALU.mult, op1=ALU.add), at[4:8] + bt[4:8] + [hc], [hB])
                K.copy("dve", fin[:, 0:1], hB[:, 0:1], [hB], [fin])
                S.op("dve", lambda e: e.tensor_tensor_scan(out=hB[:, ::-1], data0=A_[:, 0:HALF][:, ::-1], data1=B_[:, 0:HALF][:, ::-1],
                                                           initial=fin[:, 0:1], op0=ALU.mult, op1=ALU.add), at[0:4] + bt[0:4] + [fin], [hB])

        def output(n):
            G = gg[0]
            K.dma("sp", G[:], ggT[n, :, :], G, ggT)
            K.tt("pool", hA[:], hA[:], hB[:], ALU.add, [hA, hB], [hA])
            R = ro[n % 2]
            K.tt("pool", R[:], hA[:], G[:], ALU.mult, [hA, G], [R])
            K.dma("sp", catT[8 + n, :, :], R[:], catT, R)

        load_conv(0)
        for n in range(8):
            gate_tiles(n, 0)
            gate_tiles(n, 1)
            if n + 1 < 8:
                load_conv(n + 1)
            finalize(n, 0)
            finalize(n, 1)
            output(n)
        S.barrier()

    def make_epi_bufs(from_psum):
        B = {}
        B["Gbc"] = K.sb("Gbc", [128, D], F32)
        B["Abc"] = K.sb("Abc", [128, D], F32)
        B["Bbc"] = K.sb("Bbc", [128, D], F32)
        B["xt"] = K.sbs("ext", [128, D], F32, 3)
        if from_psum:
            B["raw"] = K.sbs("eraw", [128, D], F32, 1)
        else:
            B["ft"] = K.sbs("ft", [128, D], F32, 3)
        B["tmp1"] = K.sb("etmp1", [128, D], F32)
        B["tmp2"] = K.sb("etmp2", [128, D], F32)
        B["hb"] = K.sbs("ehb", [128, D], BF16, 2)
        B["junk"] = K.sb("ejunk", [128, D], BF16)
        B["st"] = K.sbs("est", [128, 4], F32, 3)
        B["hst"] = K.sbs("ehst", [128, 16, 512], BF16, 1 if from_psum else 2)
        B["pairs"] = [(PS[4], PS[5]), (PS[6], PS[7])]
        return B

    def epi_load(B, t, xsrc, fsrc=None):
        if t >= 16:
            return
        x = B["xt"][t % 3]
        K.dma("sp", x[:], xsrc[t * 128:(t + 1) * 128, :], x, xsrc)
        if fsrc is not None:
            f = B["ft"][t % 3]
            K.dma("sp", f[:], fsrc[t * 128:(t + 1) * 128, :], f, fsrc)

    def epi_s1(B, t, raw_aps, raw_bufs, from_psum, xdst):
        st, junk, tmp = B["st"][t % 3], B["junk"], B["tmp1"]
        x = B["xt"][t % 3]
        if from_psum:
            raw = B["raw"][0]
            for fb in range(4):
                K.copy("act", raw[:, fb * 512:(fb + 1) * 512], raw_aps[fb], [raw_bufs[fb]], [raw])
            rap = raw[:]
        else:
            raw = raw_bufs[0]
            rap = raw_aps[0]
        K.act(junk[:], rap, AF.Square, [raw], [junk, st], accum=st[:, 0:1])
        rstd_from_ssq(st[:, 0:1], st[:, 1:2], D, [st], [st])
        K.stt("dve", tmp[:], rap, st[:, 1:2], B["Gbc"][:], ALU.mult, ALU.mult, [raw, st, B["Gbc"]], [tmp])
        K.tt("pool", x[:], tmp[:], x[:], ALU.add, [tmp, x], [x])
        K.dma("sp", xdst[t * 128:(t + 1) * 128, :], x[:], xdst, x)

    def epi_s2(B, t):
        st, junk, tmp = B["st"][t % 3], B["junk"], B["tmp2"]
        x = B["xt"][t % 3]
        hb = B["hb"][t % 2]
        K.act(junk[:], x[:], AF.Square, [x], [junk, st], accum=st[:, 2:3])
        rstd_from_ssq(st[:, 2:3], st[:, 3:4], D, [st], [st])
        K.stt("dve", tmp[:], x[:], st[:, 3:4], B["Abc"][:], ALU.mult, ALU.mult, [x, st, B["Abc"]], [tmp])
        K.tt("dve", hb[:], tmp[:], B["Bbc"][:], ALU.add, [tmp, B["Bbc"]], [hb])

    def epi_s3(B, t, hdst):
        hb = B["hb"][t % 2]
        hst = B["hst"][(t // 4) % len(B["hst"])]
        tt = t % 4
        pbanks = B["pairs"][t % 2]
        for half in range(2):
            pb = pbanks[half]
            pv = pb.t[:].bitcast(BF16)
            for kk in range(8):
                k = half * 8 + kk
                K.tr(pv[:, kk * 128:(kk + 1) * 128], hb[:, k * 128:(k + 1) * 128], ident[:], [hb, ident], [pb], inc=(kk == 7))
            K.copy("act", hst[:, half * 8:(half + 1) * 8, tt * 128:(tt + 1) * 128], pv.rearrange("p (k t) -> p k t", k=8), [pb], [hst])
        if tt == 3:
            c0 = (t - 3) * 128
            K.dma("sp", hdst[:, :, c0:c0 + 512].rearrange("k p t -> p k t"), hst[:], hdst, hst)

    def epi_tail(B, t, do_norm, hdst):
        if not do_norm:
            return
        if 0 <= t - 1 < 16:
            epi_s2(B, t - 1)
        if 0 <= t - 2 < 16:
            epi_s3(B, t - 2, hdst)

    def phase_outproj(Wd, mrow, xsrc, xdst, hdst):
        K.reset()
        W = K.sb("Wout", [128, 16, D], BF16)
        Wp = [Buf(W.t, "Woutp%d" % i) for i in range(4)]
        for p in range(4):
            K.dma("pool", W[:, :, p * 512:(p + 1) * 512], Wd[:, p * 512:(p + 1) * 512].rearrange("(k p) n -> p k n", p=128), Wp[p])
        B = make_epi_bufs(True)
        load_bcast(B["Gbc"], modv[mrow + 2:mrow + 3, :], modv)
        load_bcast(B["Abc"], modv[mrow + 3:mrow + 4, :], modv)
        load_bcast(B["Bbc"], modv[mrow + 4:mrow + 5, :], modv)
        cat = K.sbs("cat", [128, 16, 512], BF16, 2)

        def loadcat(c):
            K.dma("sp", cat[c % 2][:], catT[:, :, c * 512:(c + 1) * 512].rearrange("k p t -> p k t"), cat[c % 2], catT)
        loadcat(0)
        epi_load(B, 0, xsrc)
        epi_load(B, 1, xsrc)
        for t in range(18):
            if t < 16:
                c, tt = t // 4, t % 4
                cc = cat[c % 2]
                if tt == 0 and c + 1 < 4:
                    loadcat(c + 1)
                for k in range(16):
                    for fb in range(4):
                        K.mm(PS[fb][:], cc[:, k, tt * 128:(tt + 1) * 128], W[:, k, fb * 512:(fb + 1) * 512],
                             k == 0, k == 15, [cc, Wp[fb]], [PS[fb]])
                epi_s1(B, t, [PS[fb][:] for fb in range(4)], [PS[fb] for fb in range(4)], True, xdst)
            epi_tail(B, t, True, hdst)
            epi_load(B, t + 2, xsrc)
        S.barrier()

    def phase_up(Wd, ncols, hsrc, dst, kind, bias_d=None):
        K.reset()
        h = K.sb("hres", [128, 16, HALF], BF16)
        hp = [Buf(h.t, "hresp%d" % i) for i in range(4)]
        for c in range(4):
            K.dma("sp", h[:, :, c * 512:(c + 1) * 512], hsrc[:, :, c * 512:(c + 1) * 512].rearrange("k p t -> p k t"), hp[c], hsrc)
        Wt = K.sbs("Wup", [128, 16, 512], BF16, 3)
        rl = K.sbs("rl", [128, 512], F32, 3)
        ust = K.sbs("ust", [128, HALF], BF16, 3)
        bias = None
        if bias_d is not None:
            bias = K.sb("bias", [128, ncols // 128], F32)
            K.dma("sp", bias[:], bias_d.ap(), bias)
        npan = ncols // 512

        def loadW(pn):
            K.dma("pool", Wt[pn % 3][:], Wd[:, pn * 512:(pn + 1) * 512].rearrange("(k p) n -> p k n", p=128), Wt[pn % 3])
        loadW(0)
        loadW(1)
        pi = 0
        for pn in range(npan):
            if pn + 2 < npan:
                loadW(pn + 2)
            Wb = Wt[pn % 3]
            for jc in range(4):
                j = pn * 4 + jc
                u = ust[j % 3]
                for c in range(4):
                    pb = PS[pi % 8]
                    pi += 1
                    for k in range(16):
                        K.mm(pb[:], Wb[:, k, jc * 128:(jc + 1) * 128], h[:, k, c * 512:(c + 1) * 512], k == 0, k == 15, [Wb, hp[c]], [pb])
                    if kind == "relu2":
                        r = rl[pi % 3]
                        K.ts("dve", r[:], pb[:], 0.0, None, ALU.max, None, [pb], [r])
                        K.act(u[:, c * 512:(c + 1) * 512], r[:], AF.Square, [r], [u])
                    else:
                        K.act(u[:, c * 512:(c + 1) * 512], pb[:], AF.Gelu_apprx_tanh, [pb, bias], [u], bias=bias[:, j:j + 1])
                if kind == "relu2":
                    K.dma("act", dst[:, :, j, :].rearrange("c p t -> p c t"), u[:].rearrange("p (c t) -> p c t", c=8), dst, u)
                else:
                    K.dma("act", dst[j, :, :], u[:], dst, u)
        S.barrier()

    def phase_down(Wd):
        K.reset()
        Wq = K.sbs("W2q", [128, 64, 512], BF16, 2)
        Wqp = [[Buf(Wq[i].t, "W2q%dp%d" % (i, p)) for p in range(4)] for i in range(2)]
        ut = K.sbs("ut", [128, 64, 256], BF16, 2)
        utp = [[Buf(ut[i].t, "ut%dp%d" % (i, p)) for p in range(4)] for i in range(2)]
        of = K.sbs("dof", [128, 512], F32, 3)

        def loadW(q):
            for p in range(4):
                K.dma("pool", Wq[q % 2][:, p * 16:(p + 1) * 16, :],
                      Wd[p * 2048:(p + 1) * 2048, q * 512:(q + 1) * 512].rearrange("(j p) n -> p j n", p=128), Wqp[q % 2][p])

        def loadU(ui):
            tc = ui % 8
            for p in range(4):
                K.dma("sp", ut[ui % 2][:, p * 16:(p + 1) * 16, :], uT[tc, :, p * 16:(p + 1) * 16, :], utp[ui % 2][p], uT)
        loadW(0)
        loadU(0)
        oi = 0
        for q in range(4):
            Wb = Wq[q % 2]
            for tc in range(8):
                ui = q * 8 + tc
                if ui + 1 < 32:
                    loadU(ui + 1)
                if tc == 1 and q + 1 < 4:
                    loadW(q + 1)
                U = ut[ui % 2]
                Up = utp[ui % 2]
                for tt in range(2):
                    pb = PS[oi % 4]
                    for j in range(64):
                        K.mm(pb[:], U[:, j, tt * 128:(tt + 1) * 128], Wb[:, j, :], j == 0, j == 63, [Up[j // 16], Wqp[q % 2][j // 16]], [pb])
                    o = of[oi % 3]
                    oi += 1
                    K.copy("act", o[:], pb[:], [pb], [o])
                    t0 = tc * 256 + tt * 128
                    K.dma("act", ffo[t0:t0 + 128, q * 512:(q + 1) * 512], o[:], ffo, o)
        S.barrier()

    def phase_ffn_epi(mrow, xsrc, xdst, do_norm, nrow, hdst):
        K.reset()
        B = make_epi_bufs(False)
        load_bcast(B["Gbc"], modv[mrow + 5:mrow + 6, :], modv)
        if do_norm:
            load_bcast(B["Abc"], modv[nrow:nrow + 1, :], modv)
            load_bcast(B["Bbc"], modv[nrow + 1:nrow + 2, :], modv)
        epi_load(B, 0, xsrc, ffo)
        epi_load(B, 1, xsrc, ffo)
        for t in range(18):
            if t < 16:
                f = B["ft"][t % 3]
                epi_s1(B, t, [f[:]], [f], False, xdst)
            epi_tail(B, t, do_norm, hdst)
            epi_load(B, t + 2, xsrc, ffo)
        S.barrier()

    def phase_gm_v():
        K.reset()
        W = K.sb("Wv", [128, 16, D], BF16)
        Wp = [Buf(W.t, "Wvp%d" % i) for i in range(4)]
        for p in range(4):
            K.dma("pool", W[:, :, p * 512:(p + 1) * 512], gm_w_in[:, D + p * 512:D + (p + 1) * 512].rearrange("(k p) n -> p k n", p=128), Wp[p])
        bvb = K.sb("bvb", [128, D], F32)
        vgb = K.sb("vgb", [128, D], F32)
        vbb = K.sb("vbb", [128, D], F32)
        load_bcast(bvb, gm_rows[0:1, :], None)
        load_bcast(vgb, gm_rows[1:2, :], None)
        load_bcast(vbb, gm_rows[2:3, :], None)
        spT = K.sb("spT", [128, 16, 128], BF16)
        bsb = K.sb("bsb", [1, D], BF16)
        hT = K.sbs("hTg", [128, 16, 512], BF16, 2)
        gu = K.sbs("gu", [128, 16, 512], BF16, 1)
        pst = K.sbs("pst", [128, 16, 512], BF16, 1)
        st = K.sbs("st4", [128, 8], F32, 2)
        mark = K.off
        spf = K.sb("spf", [128, 16, 128], F32)
        spb = K.sb("spb", [128, 16, 128], BF16)
        bsf = K.sb("bsf", [1, D], F32)
        K.dma("sp", spf[:], gm_w_sp.rearrange("g p q -> p g q"), spf)
        K.copy("dve", spb[:], spf[:], [spf], [spb])
        for half in range(2):
            pb = PS[half]
            pv = pb.t[:].bitcast(BF16)
            for gg_ in range(8):
                g = half * 8 + gg_
                K.tr(pv[:, gg_ * 128:(gg_ + 1) * 128], spb[:, g, :], ident[:], [spb, ident], [pb], inc=(gg_ == 7))
            K.copy("act", spT[:, half * 8:(half + 1) * 8, :], pv.rearrange("p (g t) -> p g t", g=8), [pb], [spT])
        K.dma("sp", bsf[:], gm_b_sp.ap(), bsf)
        K.copy("dve", bsb[:], bsf[:], [bsf], [bsb])

        def loadh(c):
            if c < 4:
                K.dma("sp", hT[c % 2][:], hTs[:, :, c * 512:(c + 1) * 512].rearrange("k p t -> p k t"), hT[c % 2], hTs)

        def loadg(c):
            if c < 4:
                K.dma("sp", gu[0][:], ggTu[:, :, c * 512:(c + 1) * 512].rearrange("k p t -> p k t"), gu[0], ggTu)
        loadh(0)
        loadg(0)
        S.barrier()
        K.off = mark
        vs = K.sbs("v", [128, D], F32, 2)
        vgs = K.sbs("vg", [128, D], F32, 2)
        junk = K.sb("junk", [128, D], BF16)
        vlns = K.sbs("vln", [128, D], BF16, 2)

        def s1(t):
            c, tt = t // 4, t % 4
            h = hT[c % 2]
            v = vs[t % 2]
            ts_ = slice(tt * 128, (tt + 1) * 128)
            if tt == 0:
                loadh(c + 1)
            for k in range(16):
                for fb in range(4):
                    K.mm(PS[fb][:], h[:, k, ts_], W[:, k, fb * 512:(fb + 1) * 512], k == 0, k == 15, [h, Wp[fb]], [PS[fb]])
            for fb in range(4):
                fs = slice(fb * 512, (fb + 1) * 512)
                K.tt("dve", v[:, fs], PS[fb][:], bvb[:, fs], ALU.add, [PS[fb], bvb], [v])

        def s2(t):
            v, vg, vln, st4 = vs[t % 2], vgs[t % 2], vlns[t % 2], st[t % 2]
            K.act(vg[:], v[:], AF.Gelu_apprx_tanh, [v], [vg, st4], accum=st4[:, 0:1])
            K.act(junk[:], vg[:], AF.Square, [vg], [junk, st4], accum=st4[:, 1:2])
            K.ts("dve", st4[:, 2:3], st4[:, 0:1], 1.0 / D, None, ALU.mult, None, [st4], [st4])
            K.tt("dve", st4[:, 3:4], st4[:, 2:3], st4[:, 2:3], ALU.mult, [st4], [st4])
            K.stt("dve", st4[:, 4:5], st4[:, 1:2], 1.0 / D, st4[:, 3:4], ALU.mult, ALU.subtract, [st4], [st4])
            rstd_from_ssq(st4[:, 4:5], st4[:, 5:6], 1.0, [st4], [st4])
            K.stt("dve", st4[:, 6:7], st4[:, 2:3], -1.0, st4[:, 5:6], ALU.mult, ALU.mult, [st4], [st4])
            K.act(v[:], vg[:], AF.Identity, [vg, st4], [v], scale=st4[:, 5:6], bias=st4[:, 6:7])
            K.tt("dve", vg[:], v[:], vgb[:], ALU.mult, [v, vgb], [vg])
            K.tt("pool", vln[:], vg[:], vbb[:], ALU.add, [vg, vbb], [vln])

        def s3(t):
            c, tt = t // 4, t % 4
            vln = vlns[t % 2]
            G, P = gu[0], pst[0]
            ts_ = slice(tt * 128, (tt + 1) * 128)
            for g in range(16):
                pb = PS[4 + g // 4]
                oc = slice((g % 4) * 128, (g % 4 + 1) * 128)
                K.mm(pb[:, oc], vln[:, g * 128:(g + 1) * 128], spT[:, g, :], True, False, [vln, spT], [pb], inc=False)
                K.mm(pb[:, oc], ones[0:1, :], bsb[0:1, g * 128:(g + 1) * 128], False, True, [ones, bsb], [pb], inc=(g % 4 == 3))
            for gq in range(4):
                K.tt("dve", P[:, gq * 4:(gq + 1) * 4, ts_], PS[4 + gq][:].rearrange("p (g t) -> p g t", g=4),
                     G[:, gq * 4:(gq + 1) * 4, ts_], ALU.mult, [PS[4 + gq], G], [P])
            if tt == 3:
                K.dma("sp", catT[:, :, c * 512:(c + 1) * 512].rearrange("k p t -> p k t"), P[:], catT, P)
                loadg(c + 1)

        for step in range(18):
            if step < 16:
                s1(step)
            if 0 <= step - 1 < 16:
                s2(step - 1)
            if 0 <= step - 2 < 16:
                s3(step - 2)
        S.barrier()

    ggTu = dscr("ggTu", [16, 128, HALF], BF16)

    phases = [
        ("mod", phase_mod),
        ("inproj", lambda: (phase_inproj("qg"), phase_inproj("kvx"))),
        ("attn", phase_attn),
        ("rnn", phase_rnn),
        ("outproj0", lambda: phase_outproj(ar_w_out, 0, xin_b, x1s, hTs)),
        ("up0", lambda: phase_up(w_ff_in[0], DFF, hTs, uT, "relu2")),
        ("down0", lambda: phase_down(w_ff_out[0])),
        ("epi0", lambda: phase_ffn_epi(0, x1s, x2s, True, 6, hTs)),
        ("gmu", lambda: phase_up(gm_w_in, D, hTs, ggTu, "gelu", gm_bu)),
        ("gmv", phase_gm_v),
        ("outproj1", lambda: phase_outproj(gm_w_out, 6, x2s, x1s, hTs)),
        ("up1", lambda: phase_up(w_ff_in[1], DFF, hTs, uT, "relu2")),
        ("down1", lambda: phase_down(w_ff_out[1])),
        ("epi1", lambda: phase_ffn_epi(6, x1s, out, False, 0, None)),
    ]
    for i, (name, fn) in enumerate(phases):
        if only is not None and name not in only:
            continue
        fn()
    S.barrier()
    S.emit()
    return nc


def _rope_tables():
    rows = SEQ // 64
    r_idx, c_idx = np.meshgrid(np.arange(rows), np.arange(64), indexing="ij")
    r_idx = r_idx.reshape(-1).astype(np.float32)
    c_idx = c_idx.reshape(-1).astype(np.float32)
    freqs = (np.float32(10000.0) ** (-np.arange(32, dtype=np.float32) / np.float32(32))).astype(np.float32)
    ang_r = r_idx[:, None] * freqs
    ang_c = c_idx[:, None] * freqs
    ang = np.concatenate([ang_r, ang_r, ang_c, ang_c], axis=1)
    cosT = np.ascontiguousarray(np.cos(ang).T.astype(np.float32))
    sinT = np.ascontiguousarray(np.sin(ang).T.astype(np.float32))
    Pm = np.zeros((128, 128), np.float32)
    for d in range(128):
        blk = d // 32
        if blk % 2 == 0:
            Pm[d, d + 32] = -1.0
        else:
            Pm[d, d - 32] = 1.0
    ropeP = np.ascontiguousarray(Pm.T)
    return cosT, sinT, ropeP


def make_in_maps(inp, cores=range(8)):
    f = lambda a: np.ascontiguousarray(a, dtype=np.float32)
    cosT, sinT, ropeP = _rope_tables()
    shared = {
        "w_mod": f(inp["w_mod"]), "b_mod": f(inp["b_mod"]), "norm_g": f(inp["norm_g"].reshape(2, 4 * D)),
        "w_ff_in": f(inp["w_ff_in"]), "w_ff_out": f(inp["w_ff_out"]),
        "ar_w_in": f(inp["ar_w_in"][0]), "ar_w_out": f(inp["ar_w_out"][0]),
        "qkg": f(np.stack([inp["ar_q_g"][0], inp["ar_k_g"][0]], axis=1)),
        "ropeP": ropeP,
        "convb": f(inp["ar_conv_b"][0].reshape(8, 128).T),
        "gm_w_in": f(inp["gm_w_in"][0]),
        "gm_bu": f(inp["gm_b_in"][0][:D].reshape(16, 128).T),
        "gm_rows": f(np.stack([inp["gm_b_in"][0][D:], inp["gm_v_g"][0], inp["gm_v_b"][0]], axis=0)),
        "gm_w_out": f(inp["gm_w_out"][0]),
    }
    cw = inp["ar_conv_w"][0]
    maps = []
    for core in cores:
        b, half = core // 2, core % 2
        m = dict(shared)
        x = inp["x"][b]
        cx = inp["ctx"][b]
        if half == 0:
            xin = np.concatenate([x[:HALF], x[HALF:], cx], axis=0)
            cs, sn = cosT, sinT
            tp = np.stack([cw[0], cw[1], cw[2], cw[3], np.zeros_like(cw[0])], axis=1)
            dsel = [0, 1]
            wsp = inp["gm_w_sp"][0]
            bsp = inp["gm_b_sp"][0]
        else:
            xr = x[::-1]
            xin = np.concatenate([xr[:HALF], xr[HALF:], cx[::-1]], axis=0)
            cs, sn = cosT[:, ::-1], sinT[:, ::-1]
            tp = np.stack([np.zeros_like(cw[0]), cw[3], cw[2], cw[1], cw[0]], axis=1)
            dsel = [1, 0]
            wsp = inp["gm_w_sp"][0][:, ::-1, ::-1]
            bsp = inp["gm_b_sp"][0][:, ::-1]
        m["xin"] = f(xin)
        m["cosT"] = f(cs)
        m["sinT"] = f(sn)
        m["taps"] = f(tp.reshape(8, 128, 5).transpose(1, 0, 2))
        m["cvec"] = f(np.stack([inp["c"][b].reshape(16, 128).T, inp["c_ctx"].reshape(16, 128).T], axis=2))
        m["ar_wa"] = f(inp["ar_wa"][0][dsel])
        m["ar_wx"] = f(inp["ar_wx"][0][dsel])
        pr = np.stack([inp["ar_ba"][0][dsel], inp["ar_bx"][0][dsel], inp["ar_lambda"][0][dsel]], axis=0)
        m["rnnp"] = f(pr.reshape(3, 2, 8, 128).transpose(3, 0, 1, 2))
        m["gm_w_sp"] = f(wsp)
        m["gm_b_sp"] = f(bsp.reshape(1, D))
        maps.append(m)
    return maps


_NC_CACHE = {}


def kernel(**inputs):
    inp = {k: np.asarray(v) for k, v in inputs.items()}
    if "nc" not in _NC_CACHE:
        _NC_CACHE["nc"] = build()
    nc = _NC_CACHE["nc"]
    maps = make_in_maps(inp)
    res = run_bass_kernel_spmd(nc, maps, core_ids=list(range(8)))
    out = np.empty((4, SEQ, D), np.float32)
    for core in range(8):
        b, half = core // 2, core % 2
        o = np.asarray(res.results[core]["out"], dtype=np.float32)
        if half == 0:
            out[b, :HALF] = o
        else:
            out[b, HALF:] = o[::-1]
    return out
```

```python
import contextlib
import numpy as np
import concourse.bass as bass
import concourse.mybir as mybir
from concourse.bass_utils import run_bass_kernel_spmd

F32 = mybir.dt.float32
BF16 = mybir.dt.bfloat16
AF = mybir.ActivationFunctionType
ALU = mybir.AluOpType
AX = mybir.AxisListType

D = 2048
SEQ = 4096
HALF = 2048
CTX = 256
NTOK = SEQ + CTX
DFF = 8192
ARIN = 3584
EPS = 1e-6
XL = 4 + SEQ
XC = 4 + CTX


class Tile:
    __slots__ = ("name", "w", "r", "dsem", "dcnt")

    def __init__(self, name):
        self.name = name
        self.w = None
        self.r = []
        self.dsem = None
        self.dcnt = 0


class Buf:
    def __init__(self, t, name, dram=False):
        self.t = t
        self.T = Tile(name)
        self.dram = dram

    def __getitem__(self, k):
        return self.t[k]


class Sched:
    ENGS = ("pe", "act", "dve", "pool", "sp")

    def __init__(self, nc, stack):
        self.nc = nc
        self.stack = stack
        self.ops = {e: [] for e in self.ENGS}
        self.sem = {e: stack.enter_context(nc.semaphore("s_" + e)) for e in self.ENGS}
        self.cnt = {e: 0 for e in self.ENGS}
        self.waited = {e: {} for e in self.ENGS}
        self.dtiles = []
        self.free_dsems = []
        self.nsem = 0
        self.pool_fifo = []

    def _dsem(self, t):
        if t.dsem is None:
            if self.free_dsems:
                t.dsem, t.dcnt = self.free_dsems.pop()
            else:
                self.nsem += 1
                t.dsem = self.stack.enter_context(self.nc.semaphore("d%d" % self.nsem))
                t.dcnt = 0
            self.dtiles.append(t)
        return t.dsem

    def _waits(self, eng, deps):
        need = {}
        for (sem, val) in deps:
            if eng == "pe" and sem is self.sem["pe"]:
                continue
            if need.get(sem, 0) < val:
                need[sem] = val
        out = []
        wd = self.waited[eng]
        for sem, val in need.items():
            if wd.get(sem, 0) < val:
                wd[sem] = val
                out.append((sem, val))
        return out

    def op(self, eng, fn, reads=(), writes=(), inc=True):
        deps = []
        for b in reads:
            t = b.T
            if t.w is not None:
                deps.append(t.w)
        for b in writes:
            t = b.T
            if t.w is not None:
                deps.append(t.w)
            deps.extend(t.r)
        waits = self._waits(eng, deps)
        val = self.cnt[eng] + 1
        if inc:
            self.cnt[eng] = val
        ev = (self.sem[eng], val)
        for b in reads:
            b.T.r.append(ev)
        for b in writes:
            b.T.w = ev
            b.T.r = []
        self.ops[eng].append((waits, fn, (self.sem[eng], 1) if inc else None))

    def dma(self, eng, out, in_, dst, src=None):
        deps = []
        if dst.dram:
            own = src.T
            if own.w is not None:
                deps.append(own.w)
        else:
            own = dst.T
            if src is not None and not src.dram and src.T.w is not None:
                deps.append(src.T.w)
            if own.w is not None:
                deps.append(own.w)
            deps.extend(own.r)
        if eng == "pool":
            while len(self.pool_fifo) >= 3:
                deps.append(self.pool_fifo.pop(0))
        waits = self._waits(eng, deps)
        sem = self._dsem(own)
        own.dcnt += 16
        ev = (sem, own.dcnt)
        if eng == "pool":
            self.pool_fifo.append(ev)
        if dst.dram:
            own.r.append(ev)
        else:
            if src is not None and not src.dram:
                src.T.r.append(ev)
            own.w = ev
            own.r = []
        self.ops[eng].append((waits, lambda e: e.dma_start(out=out, in_=in_), (sem, 16)))

    def barrier(self):
        evs = [(self.sem[e], self.cnt[e]) for e in self.ENGS if self.cnt[e] > 0]
        evs += [(t.dsem, t.dcnt) for t in self.dtiles if t.dcnt > 0]
        for e in self.ENGS:
            deps = [ev for ev in evs if ev[0] is not self.sem[e]]
            waits = self._waits(e, deps)
            if waits:
                self.ops[e].append((waits, None, None))
        self.pool_fifo = []
        for t in self.dtiles:
            self.free_dsems.append((t.dsem, t.dcnt))
            t.dsem = None
            t.w = None
            t.r = []
        self.dtiles = []

    def emit(self):
        nc = self.nc
        handles = {"pe": "tensor", "act": "scalar", "dve": "vector", "pool": "gpsimd", "sp": "sync"}
        with nc.Block() as block:
            for e in self.ENGS:
                ops = self.ops[e]

                def body(eng, ops=ops):
                    for waits, fn, inc in ops:
                        for sem, val in waits:
                            eng.wait_ge(sem, val)
                        if fn is not None:
                            ins = fn(eng)
                            if inc is not None:
                                ins.then_inc(inc[0], inc[1])
                getattr(block, handles[e])(body)


class KB:
    def __init__(self, nc, stack):
        self.nc = nc
        self.S = Sched(nc, stack)
        self.off = 18688
        self.uid = 0
        self.base = 18688

    def reset(self):
        self.off = self.base

    def sb(self, name, shape, dtype):
        nb = int(np.prod(shape[1:])) * (4 if dtype == F32 else 2)
        nb = (nb + 63) // 64 * 64
        self.uid += 1
        t = self.nc.alloc_sbuf_tensor_at("%s_%d" % (name, self.uid), list(shape), dtype, offset=self.off)
        self.off += nb
        assert self.off <= 229376, ("SBUF overflow", name, self.off)
        return Buf(t, name)

    def sbs(self, name, shape, dtype, n):
        return [self.sb(name + str(i), shape, dtype) for i in range(n)]

    def act(self, out, in_, func, r, w, scale=1.0, bias=0.0, accum=None):
        if accum is None:
            fn = lambda e: e.activation(out=out, in_=in_, func=func, bias=bias, scale=scale)
        else:
            fn = lambda e: e.activation(out=out, in_=in_, func=func, bias=bias, scale=scale, accum_out=accum)
        self.S.op("act", fn, r, w)

    def ts(self, eng, out, in0, s1, s2, op0, op1, r, w):
        if op1 is None:
            fn = lambda e: e.tensor_scalar(out=out, in0=in0, scalar1=s1, scalar2=None, op0=op0)
        else:
            fn = lambda e: e.tensor_scalar(out=out, in0=in0, scalar1=s1, scalar2=s2, op0=op0, op1=op1)
        self.S.op(eng, fn, r, w)

    def tt(self, eng, out, in0, in1, op, r, w):
        self.S.op(eng, lambda e: e.tensor_tensor(out=out, in0=in0, in1=in1, op=op), r, w)

    def stt(self, eng, out, in0, scalar, in1, op0, op1, r, w):
        self.S.op(eng, lambda e: e.scalar_tensor_tensor(out=out, in0=in0, scalar=scalar, in1=in1, op0=op0, op1=op1), r, w)

    def copy(self, eng, out, in_, r, w):
        if eng == "act":
            self.S.op("act", lambda e: e.activation(out=out, in_=in_, func=AF.Copy), r, w)
        else:
            self.S.op(eng, lambda e: e.tensor_copy(out=out, in_=in_), r, w)

    def memset(self, eng, ap, val, w):
        self.S.op(eng, lambda e: e.memset(ap, val), (), w)

    def mm(self, out, lhsT, rhs, start, stop, r, w, inc=None):
        if inc is None:
            inc = stop
        self.S.op("pe", lambda e: e.matmul(out, lhsT=lhsT, rhs=rhs, start=start, stop=stop), r, w, inc=inc)

    def tr(self, out, in_, ident, r, w, inc):
        self.S.op("pe", lambda e: e.transpose(out, in_, ident), r, w, inc=inc)

    def dma(self, eng, out, in_, dst, src=None):
        self.S.dma(eng, out, in_, dst, src)


def build(only=None, dbg=(), ext_in=()):
    nc = bass.Bass("TRN2", target_bir_lowering=False)
    st = contextlib.ExitStack()
    with st:
        return _build(nc, st, only, dbg, ext_in)


class _LazyIn:
    def __init__(self, nc, name, shape, used):
        self.nc, self.name, self.shape, self.used = nc, name, list(shape), used
        self._ap = None

    def ap(self):
        if self._ap is None:
            self._ap = self.nc.dram_tensor(self.name, self.shape, F32, kind="ExternalInput").ap()
            self.used.append(self.name)
        return self._ap

    def __getitem__(self, k):
        return self.ap()[k]

    def rearrange(self, *a, **kw):
        return self.ap().rearrange(*a, **kw)


def _build(nc, st, only, dbg, ext_in):
    used_inputs = []
    nc._used_inputs = used_inputs

    def din(name, shape, dt=F32):
        return _LazyIn(nc, name, shape, used_inputs)

    def dscr(name, shape, dt):
        kind = "ExternalOutput" if name in dbg else ("ExternalInput" if name in ext_in else "Internal")
        if kind == "ExternalInput":
            used_inputs.append(name)
        return Buf(nc.dram_tensor(name, list(shape), dt, kind=kind).ap(), name, dram=True)

    xin = din("xin", [NTOK, D])
    cvec = din("cvec", [128, 16, 2])
    w_mod = din("w_mod", [2, D, 6 * D])
    b_mod = din("b_mod", [2, 6 * D])
    norm_g = din("norm_g", [2, 4 * D])
    w_ff_in = din("w_ff_in", [2, D, DFF])
    w_ff_out = din("w_ff_out", [2, DFF, D])
    ar_w_in = din("ar_w_in", [D, ARIN])
    ar_w_out = din("ar_w_out", [D, D])
    qkg = din("qkg", [128, 2])
    ropeP = din("ropeP", [128, 128])
    cosT = din("cosT", [128, SEQ])
    sinT = din("sinT", [128, SEQ])
    taps = din("taps", [128, 8, 5])
    convb = din("convb", [128, 8])
    ar_wa = din("ar_wa", [2, 8, 128, 128])
    ar_wx = din("ar_wx", [2, 8, 128, 128])
    rnnp = din("rnnp", [128, 3, 2, 8])
    gm_w_in = din("gm_w_in", [D, 2 * D])
    gm_bu = din("gm_bu", [128, 16])
    gm_rows = din("gm_rows", [3, D])
    gm_w_sp = din("gm_w_sp", [16, 128, 128])
    gm_b_sp = din("gm_b_sp", [1, D])
    gm_w_out = din("gm_w_out", [D, D])
    out = Buf(nc.dram_tensor("out", [HALF, D], F32, kind="ExternalOutput").ap(), "out", dram=True)
    xin_b = Buf(xin, "xin", dram=True)

    modv = dscr("modv", [14, D], F32)
    hT0 = dscr("hT0", [16, 128, HALF], BF16)
    qT = dscr("qT", [8, 128, HALF], BF16)
    kT = dscr("kT", [2, 128, NTOK], BF16)
    Vs = dscr("Vs", [NTOK, 256], BF16)
    xrT = dscr("xrT", [8, 128, NTOK], F32)
    ggT = dscr("ggT", [8, 128, HALF], F32)
    catT = dscr("catT", [16, 128, HALF], BF16)
    x1s = dscr("x1s", [HALF, D], F32)
    x2s = dscr("x2s", [HALF, D], F32)
    hTs = dscr("hTs", [16, 128, HALF], BF16)
    uT = dscr("uT", [8, 128, 64, 256], BF16)
    ffo = dscr("ffo", [HALF, D], F32)

    K = KB(nc, st)
    S = K.S

    PS = [Buf(nc.alloc_psum_tensor("ps%d" % i, [128, 512], F32), "ps%d" % i) for i in range(8)]

    ident = K.sb("ident", [128, 128], BF16)
    ones = K.sb("ones", [128, 128], BF16)
    idf = K.sb("idf", [128, 128], F32)
    K.memset("pool", idf[:], 1.0, [idf])
    S.op("pool", lambda e: e.affine_select(out=idf[:], in_=idf[:], pattern=[[-1, 128]], compare_op=ALU.is_equal,
                                            fill=0.0, base=0, channel_multiplier=1), [idf], [idf])
    K.copy("dve", ident[:], idf[:], [idf], [ident])
    K.memset("pool", ones[:], 1.0, [ones])
    K.base = K.off

    def rstd_from_ssq(ssq_ap, out_ap, n, r, w):
        K.act(out_ap, ssq_ap, AF.Sqrt, r, w, scale=1.0 / n, bias=EPS)
        S.op("dve", lambda e: e.reciprocal(out=out_ap, in_=out_ap), w, w)

    def make_norm_bufs(pairs, n=2):
        return {"tmp": K.sbs("ntmp", [128, D], F32, n), "hb": K.sbs("nhb", [128, D], BF16, n),
                "junk": K.sb("njunk", [128, D], BF16), "ssq": K.sbs("nssq", [128, 2], F32, n),
                "rstd": K.sbs("nrstd", [128, 2], F32, n), "pairs": pairs, "i": 0}

    def norm_mod_transpose(xt, Abc, Bbc, NB, dst_ap, dst_buf):
        i = NB["i"]
        NB["i"] += 1
        n = len(NB["tmp"])
        tmp, hb, ssq, rstd, junk = NB["tmp"][i % n], NB["hb"][i % n], NB["ssq"][i % n], NB["rstd"][i % n], NB["junk"]
        pbanks = NB["pairs"][i % len(NB["pairs"])]
        K.act(junk[:], xt[:], AF.Square, [xt], [junk, ssq], accum=ssq[:, 0:1])
        rstd_from_ssq(ssq[:, 0:1], rstd[:, 0:1], D, [ssq], [rstd])
        K.stt("dve", tmp[:], xt[:], rstd[:, 0:1], Abc[:], ALU.mult, ALU.mult, [xt, rstd, Abc], [tmp])
        K.tt("pool", hb[:], tmp[:], Bbc[:], ALU.add, [tmp, Bbc], [hb])
        for half in range(2):
            pb = pbanks[half]
            pv = pb.t[:].bitcast(BF16)
            for kk in range(8):
                k = half * 8 + kk
                K.tr(pv[:, kk * 128:(kk + 1) * 128], hb[:, k * 128:(k + 1) * 128], ident[:], [hb, ident], [pb], inc=(kk == 7))
            K.copy("act", dst_ap[:, half * 8:(half + 1) * 8, :], pv.rearrange("p (k t) -> p k t", k=8), [pb], [dst_buf])

    def load_bcast(buf, row_ap, src):
        K.dma("sp", buf[:], row_ap.partition_broadcast(128), buf, src)

    def mod_setup(l, nbuf):
        M = {"l": l, "nbuf": nbuf}
        M["cv"] = K.sb("cv", [128, 16, 2], F32)
        M["sv"] = K.sb("sv", [128, 16, 2], BF16)
        M["wt"] = K.sbs("wmod", [128, 16, 512], BF16, nbuf)
        M["raw"] = K.sb("raw", [2, 6 * D], F32)
        M["gg"] = K.sb("gg", [2, 4 * D], F32)
        K.dma("sp", M["cv"][:], cvec.ap(), M["cv"])
        K.act(M["sv"][:], M["cv"][:], AF.Silu, [M["cv"]], [M["sv"]])
        K.dma("sp", M["raw"][:], b_mod[l:l + 1, :].partition_broadcast(2), M["raw"])
        K.dma("sp", M["gg"][:], norm_g[l:l + 1, :].partition_broadcast(2), M["gg"])
        return M

    def mod_load(M, n):
        if n < 24:
            w = M["wt"][n % M["nbuf"]]
            K.dma("pool", w[:], w_mod[M["l"], :, n * 512:(n + 1) * 512].rearrange("(k p) n -> p k n", p=128), w)

    def mod_tile(M, n, pb):
        w, sv, raw = M["wt"][n % M["nbuf"]], M["sv"], M["raw"]
        for k in range(16):
            K.mm(pb[0:2, :], sv[:, k, :], w[:, k, :], k == 0, k == 15, [sv, w], [pb])
        K.tt("dve", raw[:, n * 512:(n + 1) * 512], pb[0:2, :], raw[:, n * 512:(n + 1) * 512], ALU.add, [pb, raw], [raw])

    def mod_finish(M):
        l, raw, gg = M["l"], M["raw"], M["gg"]
        K.stt("dve", raw[:, D:2 * D], raw[:, D:2 * D], 1.0, gg[:, 0:D], ALU.add, ALU.mult, [raw, gg], [raw])
        K.tt("dve", raw[:, 2 * D:3 * D], raw[:, 2 * D:3 * D], gg[:, D:2 * D], ALU.mult, [raw, gg], [raw])
        K.stt("dve", raw[:, 4 * D:5 * D], raw[:, 4 * D:5 * D], 1.0, gg[:, 2 * D:3 * D], ALU.add, ALU.mult, [raw, gg], [raw])
        K.tt("dve", raw[:, 5 * D:6 * D], raw[:, 5 * D:6 * D], gg[:, 3 * D:4 * D], ALU.mult, [raw, gg], [raw])
        for r, src in enumerate((1, 0, 2, 4, 3, 5)):
            K.dma("sp", modv[6 * l + r:6 * l + r + 1, :], raw[0:1, src * D:(src + 1) * D], modv, raw)
        if l == 0:
            K.dma("sp", modv[12:13, :], raw[1:2, D:2 * D], modv, raw)
            K.dma("sp", modv[13:14, :], raw[1:2, 0:D], modv, raw)

    def phase_mod():
        K.reset()
        M = mod_setup(0, 4)
        mod_load(M, 0)
        mod_load(M, 1)
        for n in range(24):
            mod_load(M, n + 2)
            mod_tile(M, n, PS[n % 2])
        mod_finish(M)
        S.barrier()

    def phase_inproj(mode):
        K.reset()
        if mode == "qg":
            col0 = [0, 512, 2560, 3072]
        else:
            col0 = [1024, 1536, 2048]
        npan = len(col0)
        W = K.sb("Win", [128, 16, npan * 512], BF16)
        Wp = [Buf(W.t, "Winp%d" % i) for i in range(npan)]
        for p in range(npan):
            K.dma("pool", W[:, :, p * 512:(p + 1) * 512],
                  ar_w_in[:, col0[p]:col0[p] + 512].rearrange("(k p) n -> p k n", p=128), Wp[p])

        def wloc(ocol):
            for p in range(npan):
                if col0[p] <= ocol < col0[p] + 512:
                    return p * 512 + ocol - col0[p], p
            raise KeyError(ocol)
        Abc = K.sb("Abc", [128, D], F32)
        Bbc = K.sb("Bbc", [128, D], F32)
        xt = K.sbs("xt", [128, D], F32, 2)
        NB = make_norm_bufs([(PS[0], PS[1])])
        hT = K.sbs("hT", [128, 16, 512], BF16, 2)
        qk = K.sb("qkg", [128, 2], F32)
        Pm = K.sb("Pm", [128, 128], BF16)
        Pf = K.sb("Pf", [128, 128], F32)
        cs = K.sbs("cs", [128, 512], F32, 2)
        sn = K.sbs("sn", [128, 512], F32, 2)
        sq = K.sbs("sq", [128, 512], BF16, 2)
        qg = K.sbs("qg", [128, 512], BF16, 2)
        rs = K.sbs("rs", [128, 512], F32, 2)
        t1 = K.sbs("t1", [128, 512], F32, 2)
        t2 = K.sbs("t2", [128, 512], F32, 2)
        ob = K.sbs("ob", [128, 512], BF16, 3)
        of = K.sbs("of", [128, 512], F32, 3)
        vb = K.sbs("vb", [128, 256], BF16, 2)
        K.dma("sp", qk[:], qkg.ap(), qk)
        K.dma("sp", Pf[:], ropeP.ap(), Pf)
        K.copy("dve", Pm[:], Pf[:], [Pf], [Pm])
        load_bcast(Abc, modv[0:1, :], modv)
        load_bcast(Bbc, modv[1:2, :], modv)

        cnt = {"x": 0, "ps": 0, "o": 0, "f": 0, "v": 0, "qk": 0}

        def proj_cols(h, NT, n):
            lc, p = wloc(n * 128)
            pb = PS[2 + cnt["ps"] % 3]
            cnt["ps"] += 1
            for k in range(16):
                K.mm(pb[:, 0:NT], W[:, k, lc:lc + 128], h[:, k, 0:NT], k == 0, k == 15, [Wp[p], h], [pb])
            return pb

        def do_qk(pb, NT, gcol, rope_t0, dst_ap, dst_buf):
            i = cnt["qk"] % 2
            cnt["qk"] += 1
            K.act(sq[i][:, 0:NT], pb[:, 0:NT], AF.Square, [pb], [sq[i]])
            K.act(qg[i][:, 0:NT], pb[:, 0:NT], AF.Identity, [pb, qk], [qg[i]], scale=qk[:, gcol:gcol + 1])
            K.mm(PS[5][:, 0:NT], ones[:], sq[i][:, 0:NT], True, True, [ones, sq[i]], [PS[5]])
            rstd_from_ssq(PS[5][:, 0:NT], rs[i][:, 0:NT], 128, [PS[5]], [rs[i]])
            o = ob[cnt["o"] % 3]
            cnt["o"] += 1
            if rope_t0 is None:
                K.tt("pool", o[:, 0:NT], qg[i][:, 0:NT], rs[i][:, 0:NT], ALU.mult, [qg[i], rs[i]], [o])
            else:
                K.mm(PS[6][:, 0:NT], Pm[:], qg[i][:, 0:NT], True, True, [Pm, qg[i]], [PS[6]])
                K.tt("pool", t1[i][:, 0:NT], qg[i][:, 0:NT], cs[rope_t0][:, 0:NT], ALU.mult, [qg[i], cs[rope_t0]], [t1[i]])
                K.tt("dve", t2[i][:, 0:NT], PS[6][:, 0:NT], sn[rope_t0][:, 0:NT], ALU.mult, [PS[6], sn[rope_t0]], [t2[i]])
                K.tt("pool", t1[i][:, 0:NT], t1[i][:, 0:NT], t2[i][:, 0:NT], ALU.add, [t1[i], t2[i]], [t1[i]])
                K.tt("pool", o[:, 0:NT], t1[i][:, 0:NT], rs[i][:, 0:NT], ALU.mult, [t1[i], rs[i]], [o])
            K.dma("sp", dst_ap, o[:, 0:NT], dst_buf, o)

        nch = 4 if mode == "qg" else 9

        def prep_tasks(c):
            if c >= nch:
                return []
            isctx = c == 8
            NT = 256 if isctx else 512
            t0 = c * 512
            h = hT[c % 2]
            tasks = []
            if isctx:
                def t_bc():
                    load_bcast(Abc, modv[12:13, :], modv)
                    load_bcast(Bbc, modv[13:14, :], modv)
                tasks.append(t_bc)
            if c < 4 and mode == "kvx":
                tasks.append(lambda: K.dma("sp", h[:], hT0[:, :, t0:t0 + 512].rearrange("k p t -> p k t"), h, hT0))
            else:
                for tt in range(NT // 128):
                    def t_norm(tt=tt):
                        x = xt[cnt["x"] % 2]
                        cnt["x"] += 1
                        K.dma("sp", x[:], xin[t0 + tt * 128:t0 + (tt + 1) * 128, :], x)
                        norm_mod_transpose(x, Abc, Bbc, NB, h[:, :, tt * 128:(tt + 1) * 128], h)
                    tasks.append(t_norm)
            if not isctx:
                def t_cs():
                    ci = c % 2
                    K.dma("sp", cs[ci][:], cosT[:, t0:t0 + 512], cs[ci])
                    K.dma("sp", sn[ci][:], sinT[:, t0:t0 + 512], sn[ci])
                tasks.append(t_cs)
            return tasks

        for tk in prep_tasks(0):
            tk()
        for c in range(nch):
            isctx = c == 8
            NT = 256 if isctx else 512
            t0 = c * 512
            h = hT[c % 2]
            ri = None if isctx else c % 2
            pend = prep_tasks(c + 1)
            state = {"n": 0}

            def tick(every):
                state["n"] += 1
                if pend and state["n"] % every == 0:
                    pend.pop(0)()
            if mode == "qg":
                K.dma("sp", hT0[:, :, t0:t0 + 512].rearrange("k p t -> p k t"), h[:], hT0, h)
                prev = None
                for hd in range(8):
                    pb = proj_cols(h, NT, hd)
                    if prev is not None:
                        do_qk(prev[0], NT, 0, ri, qT[prev[1], :, t0:t0 + NT], qT)
                    prev = (pb, hd)
                    tick(2)
                do_qk(prev[0], NT, 0, ri, qT[prev[1], :, t0:t0 + NT], qT)
                for n in range(8):
                    pb = proj_cols(h, NT, 20 + n)
                    o = of[cnt["f"] % 3]
                    cnt["f"] += 1
                    K.act(o[:, 0:NT], pb[:, 0:NT], AF.Gelu_apprx_tanh, [pb], [o])
                    K.dma("act", ggT[n, :, t0:t0 + NT], o[:, 0:NT], ggT, o)
                    tick(2)
            else:
                pb0 = proj_cols(h, NT, 8)
                pb1 = proj_cols(h, NT, 9)
                do_qk(pb0, NT, 1, ri, kT[0, :, t0:t0 + NT], kT)
                tick(1)
                do_qk(pb1, NT, 1, ri, kT[1, :, t0:t0 + NT], kT)
                tick(1)
                vl, vp = wloc(1280)
                for tt in range(NT // 128):
                    pb = PS[2 + cnt["ps"] % 3]
                    cnt["ps"] += 1
                    for k in range(16):
                        K.mm(pb[:, 0:256], h[:, k, tt * 128:(tt + 1) * 128], W[:, k, vl:vl + 256], k == 0, k == 15, [h, Wp[vp]], [pb])
                    v = vb[cnt["v"] % 2]
                    cnt["v"] += 1
                    K.copy("act", v[:], pb[:, 0:256], [pb], [v])
                    K.dma("act", Vs[t0 + tt * 128:t0 + (tt + 1) * 128, :], v[:], Vs, v)
                    tick(2)
                for n in range(8):
                    pb = proj_cols(h, NT, 12 + n)
                    o = of[cnt["f"] % 3]
                    cnt["f"] += 1
                    K.copy("act", o[:, 0:NT], pb[:, 0:NT], [pb], [o])
                    K.dma("act", xrT[n, :, t0:t0 + NT], o[:, 0:NT], xrT, o)
                    tick(2)
            while pend:
                pend.pop(0)()
        S.barrier()

    def phase_attn():
        K.reset()
        NKT = NTOK // 128
        kt_sb = K.sb("kTs", [128, 2, NTOK], BF16)
        v_sb = K.sb("Vsb", [128, NKT, 256], BF16)
        qs = K.sbs("qs", [128, 512], BF16, 3)
        pbuf = K.sbs("pexp", [128, 512], BF16, 4)
        rl = K.sbs("rl", [128, 512], F32, 2)
        ob = K.sbs("aob", [128, 512], BF16, 2)
        K.dma("sp", kt_sb[:], kT[:].rearrange("k p t -> p k t"), kt_sb, kT)
        v_h = [Buf(v_sb.t, "v_h%d" % i) for i in range(2)]
        for i in range(2):
            K.dma("sp", v_sb[:, i * 17:(i + 1) * 17, :],
                  Vs[i * 17 * 128:(i + 1) * 17 * 128, :].rearrange("(kt p) d -> p kt d", p=128), v_h[i], Vs)
        scale = 128.0 ** -0.5
        it = 0
        sc = 0
        M = mod_setup(1, 3)
        mod_load(M, 0)
        mod_load(M, 1)
        def loadq(i):
            if i < 32:
                K.dma("sp", qs[i % 3][:], qT[i // 4, :, (i % 4) * 512:(i % 4 + 1) * 512], qs[i % 3], qT)
        loadq(0)
        loadq(1)
        for hd in range(8):
            kv = hd // 4
            for qt in range(4):
                q = qs[it % 3]
                loadq(it + 2)
                Ob = PS[4 + 2 * (it % 2)]
                Lb = PS[5 + 2 * (it % 2)]
                sbanks = {}

                def issue_s(kt, q=q, kv=kv):
                    nonlocal sc
                    pb = PS[sc % 3]
                    sc += 1
                    K.mm(pb[:], kt_sb[:, kv, kt * 128:(kt + 1) * 128], q[:], True, True, [kt_sb, q], [pb])
                    sbanks[kt] = pb
                issue_s(0)
                issue_s(1)
                if it < 24:
                    mod_load(M, it + 2)
                    mod_tile(M, it, PS[3])
                for kt in range(NKT):
                    if kt + 2 < NKT:
                        issue_s(kt + 2)
                    pb = sbanks.pop(kt)
                    p = pbuf[(it * NKT + kt) % 4]
                    K.act(p[:], pb[:], AF.Exp, [pb], [p], scale=scale)
                    K.mm(Ob[:], v_sb[:, kt, kv * 128:(kv + 1) * 128], p[:], kt == 0, kt == NKT - 1, [v_h[kt // 17], p], [Ob], inc=(kt == NKT - 1))
                    K.mm(Lb[:], ones[:], p[:], kt == 0, kt == NKT - 1, [ones, p], [Lb], inc=True)
                r = rl[it % 2]
                o = ob[it % 2]
                S.op("dve", lambda e, r=r, Lb=Lb: e.reciprocal(out=r[:], in_=Lb[:]), [Lb], [r])
                K.tt("dve", o[:], Ob[:], r[:], ALU.mult, [Ob, r], [o])
                K.dma("sp", catT[hd, :, qt * 512:(qt + 1) * 512], o[:], catT, o)
                it += 1
        mod_finish(M)
        S.barrier()

    def phase_rnn():
        K.reset()
        tp = K.sb("taps", [128, 8, 5], F32)
        cb = K.sb("convb", [128, 8], F32)
        rp = K.sb("rnnp", [128, 3, 2, 8], F32)
        c1 = K.sb("c1", [128, 2, 8], F32)
        e1 = K.sb("e1", [128, 2, 8], F32)
        hb_ = K.sb("hbias", [128, 2, 2, 8], F32)
        waf = K.sbs("waf", [128, 128], F32, 2)
        wab = K.sbs("wab", [128, 128], BF16, 4)
        wxb = K.sbs("wxb", [128, 128], BF16, 4)
        xl = K.sbs("xl", [128, XL], F32, 1)
        xc = K.sbs("xc", [128, XC], F32, 2)
        accs = K.sbs("acc", [128, NTOK], F32, 2)
        accbs = K.sbs("accb", [128, NTOK], BF16, 2)
        NA = HALF + CTX
        dbuf = []
        for d, n_ in ((0, NA), (1, NTOK)):
            ent = {}
            for nm in ("a", "b", "s"):
                t = K.sb("g%s%d" % (nm, d), [128, n_], F32)
                ent[nm] = t
                ent[nm + "t"] = [Buf(t.t, "g%s%d_%d" % (nm, d, i)) for i in range(9)]
            dbuf.append(ent)
        rr = K.sbs("rr", [128, 512], F32, 2)
        ii = K.sbs("ii", [128, 512], F32, 2)
        hA = K.sb("hA", [128, HALF], F32)
        hB = K.sb("hB", [128, HALF], F32)
        hc = K.sb("hc", [128, CTX], F32)
        fin = K.sb("fin", [128, 2], F32)
        gg = K.sbs("gg", [128, HALF], F32, 1)
        ro = K.sbs("ro", [128, HALF], BF16, 2)
        K.dma("sp", tp[:], taps.ap(), tp)
        K.dma("sp", cb[:], convb.ap(), cb)
        K.dma("sp", rp[:], rnnp.ap(), rp)
        K.act(e1[:], rp[:, 2, :, :], AF.Exp, [rp], [e1], scale=-1.0)
        K.act(e1[:], e1[:], AF.Ln, [e1], [e1], bias=1.0)
        K.ts("dve", c1[:], e1[:], -4.0, None, ALU.mult, None, [e1], [c1])
        K.ts("dve", hb_[:], rp[:, 0:2, :, :], 0.5, None, ALU.mult, None, [rp], [hb_])
        for b_ in xl:
            K.memset("pool", b_[:, 0:2], 0.0, [b_])
            K.memset("pool", b_[:, XL - 2:XL], 0.0, [b_])
        for b_ in xc:
            K.memset("pool", b_[:, 0:2], 0.0, [b_])
            K.memset("pool", b_[:, XC - 2:XC], 0.0, [b_])
        cnt = {"wi": 0, "ti": 0, "gi": 0}
        wsel = {}

        def load_conv(n):
            X = xl[0]
            C = xc[n % 2]
            acc = accs[n % 2]
            accb = accbs[n % 2]
            K.dma("sp", X[:, 2:2 + SEQ], xrT[n, :, 0:SEQ], X, xrT)
            K.dma("sp", C[:, 2:2 + CTX], xrT[n, :, SEQ:NTOK], C, xrT)
            for (src, L, o0) in ((X, SEQ, 0), (C, CTX, SEQ)):
                K.act(acc[:, o0:o0 + L], src[:, 0:L], AF.Identity, [src, tp, cb], [acc], scale=tp[:, n, 0:1], bias=cb[:, n:n + 1])
                for j in range(1, 5):
                    K.stt("dve", acc[:, o0:o0 + L], src[:, j:j + L], tp[:, n, j:j + 1], acc[:, o0:o0 + L], ALU.mult, ALU.add, [src, tp, acc], [acc])
            K.copy("act", accb[:], acc[:], [acc], [accb])
            for d in range(2):
                wA = wab[cnt["wi"] % 4]
                wX = wxb[cnt["wi"] % 4]
                cnt["wi"] += 1
                for (srcw, dstw) in ((ar_wa, wA), (ar_wx, wX)):
                    f = waf[cnt["ti"] % 2]
                    cnt["ti"] += 1
                    K.dma("sp", f[:], srcw[d, n, :, :], f)
                    K.copy("dve", dstw[:], f[:], [f], [dstw])
                wsel[(n, d)] = (wA, wX)

        def gate_tiles(n, d):
            E = dbuf[d]
            acc = accs[n % 2]
            accb = accbs[n % 2]
            wA, wX = wsel[(n, d)]
            nlat = HALF if d == 0 else SEQ
            tiles = [(SEQ, nlat, CTX)] + [(c0, c0, 512) for c0 in range(0, nlat, 512)]
            for (c0, l0, NT) in tiles:
                i = cnt["gi"] % 2
                cnt["gi"] += 1
                lt = l0 // 512
                K.mm(PS[0 + i][:, 0:NT], wA[:], accb[:, c0:c0 + NT], True, True, [wA, accb], [PS[0 + i]])
                K.mm(PS[2 + i][:, 0:NT], wX[:], accb[:, c0:c0 + NT], True, True, [wX, accb], [PS[2 + i]])
                K.act(rr[i][:, 0:NT], PS[0 + i][:, 0:NT], AF.Tanh, [PS[0 + i], hb_], [rr[i]], scale=0.5, bias=hb_[:, 0, d, n:n + 1])
                K.act(ii[i][:, 0:NT], PS[2 + i][:, 0:NT], AF.Tanh, [PS[2 + i], hb_], [ii[i]], scale=0.5, bias=hb_[:, 1, d, n:n + 1])
                K.act(E["a"][:, l0:l0 + NT], rr[i][:, 0:NT], AF.Exp, [rr[i], c1], [E["at"][lt]], scale=c1[:, d, n:n + 1], bias=c1[:, d, n:n + 1])
                K.tt("pool", E["s"][:, l0:l0 + NT], E["a"][:, l0:l0 + NT], E["a"][:, l0:l0 + NT], ALU.mult, [E["at"][lt]], [E["st"][lt]])
                K.stt("dve", E["b"][:, l0:l0 + NT], ii[i][:, 0:NT], 1.0, acc[:, c0:c0 + NT], ALU.add, ALU.mult, [ii[i], acc], [E["bt"][lt]])

        def finalize(n, d):
            E = dbuf[d]
            nlat = HALF if d == 0 else SEQ
            nt = nlat // 512
            for (l0, L, tl) in ((0, nlat, list(range(nt))), (nlat, CTX, [nt])):
                st_ = [E["st"][t] for t in tl]
                bt_ = [E["bt"][t] for t in tl]
                K.act(E["s"][:, l0:l0 + L], E["s"][:, l0:l0 + L], AF.Sqrt, st_, st_, scale=-0.25, bias=0.25)
                K.tt("pool", E["b"][:, l0:l0 + L], E["b"][:, l0:l0 + L], E["s"][:, l0:l0 + L], ALU.mult, st_ + bt_, bt_)
            A_, B_ = E["a"], E["b"]
            at, bt = E["at"], E["bt"]
            if d == 0:
                S.op("dve", lambda e: e.tensor_tensor_scan(out=hc[:, :], data0=A_[:, HALF:NA], data1=B_[:, HALF:NA],
                                                           initial=0.0, op0=ALU.mult, op1=ALU.add), [at[4], bt[4]], [hc])
                S.op("dve", lambda e: e.tensor_tensor_scan(out=hA[:, :], data0=A_[:, 0:HALF], data1=B_[:, 0:HALF],
                                                           initial=hc[:, CTX - 1:CTX], op0=ALU.mult, op1=ALU.add), at[0:4] + bt[0:4] + [hc], [hA])
            else:
                S.op("dve", lambda e: e.tensor_tensor_scan(out=hc[:, ::-1], data0=A_[:, SEQ:NTOK][:, ::-1], data1=B_[:, SEQ:NTOK][:, ::-1],
                                                           initial=0.0, op0=ALU.mult, op1=ALU.add), [at[8], bt[8]], [hc])
                S.op("dve", lambda e: e.tensor_tensor_scan(out=hB[:, ::-1], data0=A_[:, HALF:SEQ][:, ::-1], data1=B_[:, HALF:SEQ][:, ::-1],
                                                           initial=hc[:, 0:1], op0=ALU.mult, op1=ALU.add), at[4:8] + bt[4:8] + [hc], [hB])
                K.copy("dve", fin[:, 0:1], hB[:, 0:1], [hB], [fin])
                S.op("dve", lambda e: e.tensor_tensor_scan(out=hB[:, ::-1], data0=A_[:, 0:HALF][:, ::-1], data1=B_[:, 0:HALF][:, ::-1],
                                                           initial=fin[:, 0:1], op0=ALU.mult, op1=ALU.add), at[0:4] + bt[0:4] + [fin], [hB])

        def output(n):
            G = gg[0]
            K.dma("sp", G[:], ggT[n, :, :], G, ggT)
            K.tt("pool", hA[:], hA[:], hB[:], ALU.add, [hA, hB], [hA])
            R = ro[n % 2]
            K.tt("pool", R[:], hA[:], G[:], ALU.mult, [hA, G], [R])
            K.dma("sp", catT[8 + n, :, :], R[:], catT, R)

        load_conv(0)
        for n in range(8):
            gate_tiles(n, 0)
            gate_tiles(n, 1)
            if n + 1 < 8:
                load_conv(n + 1)
            finalize(n, 0)
            finalize(n, 1)
            output(n)
        S.barrier()

    def make_epi_bufs(from_psum):
        B = {}
        B["Gbc"] = K.sb("Gbc", [128, D], F32)
        B["Abc"] = K.sb("Abc", [128, D], F32)
        B["Bbc"] = K.sb("Bbc", [128, D], F32)
        B["xt"] = K.sbs("ext", [128, D], F32, 3)
        if from_psum:
            B["raw"] = K.sbs("eraw", [128, D], F32, 1)
        else:
            B["ft"] = K.sbs("ft", [128, D], F32, 3)
        B["tmp1"] = K.sb("etmp1", [128, D], F32)
        B["tmp2"] = K.sb("etmp2", [128, D], F32)
        B["hb"] = K.sbs("ehb", [128, D], BF16, 2)
        B["junk"] = K.sb("ejunk", [128, D], BF16)
        B["st"] = K.sbs("est", [128, 4], F32, 3)
        B["hst"] = K.sbs("ehst", [128, 16, 512], BF16, 1 if from_psum else 2)
        B["pairs"] = [(PS[4], PS[5]), (PS[6], PS[7])]
        return B

    def epi_load(B, t, xsrc, fsrc=None):
        if t >= 16:
            return
        x = B["xt"][t % 3]
        K.dma("sp", x[:], xsrc[t * 128:(t + 1) * 128, :], x, xsrc)
        if fsrc is not None:
            f = B["ft"][t % 3]
            K.dma("sp", f[:], fsrc[t * 128:(t + 1) * 128, :], f, fsrc)

    def epi_s1(B, t, raw_aps, raw_bufs, from_psum, xdst):
        st, junk, tmp = B["st"][t % 3], B["junk"], B["tmp1"]
        x = B["xt"][t % 3]
        if from_psum:
            raw = B["raw"][0]
            for fb in range(4):
                K.copy("act" if fb % 2 == 0 else "dve", raw[:, fb * 512:(fb + 1) * 512], raw_aps[fb], [raw_bufs[fb]], [raw])
            rap = raw[:]
        else:
            raw = raw_bufs[0]
            rap = raw_aps[0]
        K.act(junk[:], rap, AF.Square, [raw], [junk, st], accum=st[:, 0:1])
        rstd_from_ssq(st[:, 0:1], st[:, 1:2], D, [st], [st])
        K.stt("dve", tmp[:], rap, st[:, 1:2], B["Gbc"][:], ALU.mult, ALU.mult, [raw, st, B["Gbc"]], [tmp])
        K.tt("pool", x[:], tmp[:], x[:], ALU.add, [tmp, x], [x])
        K.dma("sp", xdst[t * 128:(t + 1) * 128, :], x[:], xdst, x)

    def epi_s2(B, t):
        st, junk, tmp = B["st"][t % 3], B["junk"], B["tmp2"]
        x = B["xt"][t % 3]
        hb = B["hb"][t % 2]
        K.act(junk[:], x[:], AF.Square, [x], [junk, st], accum=st[:, 2:3])
        rstd_from_ssq(st[:, 2:3], st[:, 3:4], D, [st], [st])
        K.stt("dve", tmp[:], x[:], st[:, 3:4], B["Abc"][:], ALU.mult, ALU.mult, [x, st, B["Abc"]], [tmp])
        K.tt("dve", hb[:], tmp[:], B["Bbc"][:], ALU.add, [tmp, B["Bbc"]], [hb])

    def epi_s3(B, t, hdst):
        hb = B["hb"][t % 2]
        hst = B["hst"][(t // 4) % len(B["hst"])]
        tt = t % 4
        pbanks = B["pairs"][t % 2]
        for half in range(2):
            pb = pbanks[half]
            pv = pb.t[:].bitcast(BF16)
            for kk in range(8):
                k = half * 8 + kk
                K.tr(pv[:, kk * 128:(kk + 1) * 128], hb[:, k * 128:(k + 1) * 128], ident[:], [hb, ident], [pb], inc=(kk == 7))
            K.copy("act", hst[:, half * 8:(half + 1) * 8, tt * 128:(tt + 1) * 128], pv.rearrange("p (k t) -> p k t", k=8), [pb], [hst])
        if tt == 3:
            c0 = (t - 3) * 128
            K.dma("sp", hdst[:, :, c0:c0 + 512].rearrange("k p t -> p k t"), hst[:], hdst, hst)

    def epi_tail(B, t, do_norm, hdst):
        if not do_norm:
            return
        if 0 <= t - 1 < 16:
            epi_s2(B, t - 1)
        if 0 <= t - 2 < 16:
            epi_s3(B, t - 2, hdst)

    def phase_outproj(Wd, mrow, xsrc, xdst, hdst):
        K.reset()
        W = K.sb("Wout", [128, 16, D], BF16)
        Wp = [Buf(W.t, "Woutp%d" % i) for i in range(4)]
        for p in range(4):
            K.dma("pool", W[:, :, p * 512:(p + 1) * 512], Wd[:, p * 512:(p + 1) * 512].rearrange("(k p) n -> p k n", p=128), Wp[p])
        B = make_epi_bufs(True)
        load_bcast(B["Gbc"], modv[mrow + 2:mrow + 3, :], modv)
        load_bcast(B["Abc"], modv[mrow + 3:mrow + 4, :], modv)
        load_bcast(B["Bbc"], modv[mrow + 4:mrow + 5, :], modv)
        cat = K.sbs("cat", [128, 16, 512], BF16, 2)

        def loadcat(c):
            K.dma("sp", cat[c % 2][:], catT[:, :, c * 512:(c + 1) * 512].rearrange("k p t -> p k t"), cat[c % 2], catT)
        loadcat(0)
        epi_load(B, 0, xsrc)
        epi_load(B, 1, xsrc)
        for t in range(18):
            if t < 16:
                c, tt = t // 4, t % 4
                cc = cat[c % 2]
                if tt == 0 and c + 1 < 4:
                    loadcat(c + 1)
                for k in range(16):
                    for fb in range(4):
                        K.mm(PS[fb][:], cc[:, k, tt * 128:(tt + 1) * 128], W[:, k, fb * 512:(fb + 1) * 512],
                             k == 0, k == 15, [cc, Wp[fb]], [PS[fb]])
                epi_s1(B, t, [PS[fb][:] for fb in range(4)], [PS[fb] for fb in range(4)], True, xdst)
            epi_tail(B, t, True, hdst)
            epi_load(B, t + 2, xsrc)
        S.barrier()

    def phase_up(Wd, ncols, hsrc, dst, kind, bias_d=None):
        K.reset()
        h = K.sb("hres", [128, 16, HALF], BF16)
        hp = [Buf(h.t, "hresp%d" % i) for i in range(4)]
        for c in range(4):
            K.dma("sp", h[:, :, c * 512:(c + 1) * 512], hsrc[:, :, c * 512:(c + 1) * 512].rearrange("k p t -> p k t"), hp[c], hsrc)
        Wt = K.sbs("Wup", [128, 16, 512], BF16, 3)
        rl = K.sbs("rl", [128, 512], F32, 3)
        ust = K.sbs("ust", [128, HALF], BF16, 3)
        bias = None
        if bias_d is not None:
            bias = K.sb("bias", [128, ncols // 128], F32)
            K.dma("sp", bias[:], bias_d.ap(), bias)
        npan = ncols // 512

        def loadW(pn):
            K.dma("pool", Wt[pn % 3][:], Wd[:, pn * 512:(pn + 1) * 512].rearrange("(k p) n -> p k n", p=128), Wt[pn % 3])
        loadW(0)
        loadW(1)
        pi = 0
        for pn in range(npan):
            if pn + 2 < npan:
                loadW(pn + 2)
            Wb = Wt[pn % 3]
            for jc in range(4):
                j = pn * 4 + jc
                u = ust[j % 3]
                for c in range(4):
                    pb = PS[pi % 8]
                    pi += 1
                    for k in range(16):
                        K.mm(pb[:], Wb[:, k, jc * 128:(jc + 1) * 128], h[:, k, c * 512:(c + 1) * 512], k == 0, k == 15, [Wb, hp[c]], [pb])
                    if kind == "relu2":
                        r = rl[pi % 3]
                        K.ts("dve", r[:], pb[:], 0.0, None, ALU.max, None, [pb], [r])
                        K.act(u[:, c * 512:(c + 1) * 512], r[:], AF.Square, [r], [u])
                    else:
                        K.act(u[:, c * 512:(c + 1) * 512], pb[:], AF.Gelu_apprx_tanh, [pb, bias], [u], bias=bias[:, j:j + 1])
                if kind == "relu2":
                    K.dma("act", dst[:, :, j, :].rearrange("c p t -> p c t"), u[:].rearrange("p (c t) -> p c t", c=8), dst, u)
                else:
                    K.dma("act", dst[j, :, :], u[:], dst, u)
        S.barrier()

    def phase_down(Wd):
        K.reset()
        Wq = K.sbs("W2q", [128, 64, 512], BF16, 2)
        Wqp = [[Buf(Wq[i].t, "W2q%dp%d" % (i, p)) for p in range(4)] for i in range(2)]
        ut = K.sbs("ut", [128, 64, 256], BF16, 2)
        utp = [[Buf(ut[i].t, "ut%dp%d" % (i, p)) for p in range(4)] for i in range(2)]
        of = K.sbs("dof", [128, 512], F32, 3)

        def loadW(q):
            for p in range(4):
                K.dma("pool", Wq[q % 2][:, p * 16:(p + 1) * 16, :],
                      Wd[p * 2048:(p + 1) * 2048, q * 512:(q + 1) * 512].rearrange("(j p) n -> p j n", p=128), Wqp[q % 2][p])

        def loadU(ui):
            tc = ui % 8
            for p in range(4):
                K.dma("sp", ut[ui % 2][:, p * 16:(p + 1) * 16, :], uT[tc, :, p * 16:(p + 1) * 16, :], utp[ui % 2][p], uT)
        loadW(0)
        loadU(0)
        oi = 0
        for q in range(4):
            Wb = Wq[q % 2]
            for tc in range(8):
                ui = q * 8 + tc
                if ui + 1 < 32:
                    loadU(ui + 1)
                if tc == 1 and q + 1 < 4:
                    loadW(q + 1)
                U = ut[ui % 2]
                Up = utp[ui % 2]
                for tt in range(2):
                    pb = PS[oi % 4]
                    for j in range(64):
                        K.mm(pb[:], U[:, j, tt * 128:(tt + 1) * 128], Wb[:, j, :], j == 0, j == 63, [Up[j // 16], Wqp[q % 2][j // 16]], [pb])
                    o = of[oi % 3]
                    oi += 1
                    K.copy("act", o[:], pb[:], [pb], [o])
                    t0 = tc * 256 + tt * 128
                    K.dma("act", ffo[t0:t0 + 128, q * 512:(q + 1) * 512], o[:], ffo, o)
        S.barrier()

    def phase_ffn_epi(mrow, xsrc, xdst, do_norm, nrow, hdst):
        K.reset()
        B = make_epi_bufs(False)
        load_bcast(B["Gbc"], modv[mrow + 5:mrow + 6, :], modv)
        if do_norm:
            load_bcast(B["Abc"], modv[nrow:nrow + 1, :], modv)
            load_bcast(B["Bbc"], modv[nrow + 1:nrow + 2, :], modv)
        epi_load(B, 0, xsrc, ffo)
        epi_load(B, 1, xsrc, ffo)
        for t in range(18):
            if t < 16:
                f = B["ft"][t % 3]
                epi_s1(B, t, [f[:]], [f], False, xdst)
            epi_tail(B, t, do_norm, hdst)
            epi_load(B, t + 2, xsrc, ffo)
        S.barrier()

    def phase_gm_v():
        K.reset()
        W = K.sb("Wv", [128, 16, D], BF16)
        Wp = [Buf(W.t, "Wvp%d" % i) for i in range(4)]
        for p in range(4):
            K.dma("pool", W[:, :, p * 512:(p + 1) * 512], gm_w_in[:, D + p * 512:D + (p + 1) * 512].rearrange("(k p) n -> p k n", p=128), Wp[p])
        bvb = K.sb("bvb", [128, D], F32)
        vgb = K.sb("vgb", [128, D], F32)
        vbb = K.sb("vbb", [128, D], F32)
        load_bcast(bvb, gm_rows[0:1, :], None)
        load_bcast(vgb, gm_rows[1:2, :], None)
        load_bcast(vbb, gm_rows[2:3, :], None)
        spT = K.sb("spT", [128, 16, 128], BF16)
        bsb = K.sb("bsb", [1, D], BF16)
        hT = K.sbs("hTg", [128, 16, 512], BF16, 2)
        gu = K.sbs("gu", [128, 16, 512], BF16, 1)
        pst = K.sbs("pst", [128, 16, 512], BF16, 1)
        st = K.sbs("st4", [128, 8], F32, 2)
        mark = K.off
        spf = K.sb("spf", [128, 16, 128], F32)
        spb = K.sb("spb", [128, 16, 128], BF16)
        bsf = K.sb("bsf", [1, D], F32)
        K.dma("sp", spf[:], gm_w_sp.rearrange("g p q -> p g q"), spf)
        K.copy("dve", spb[:], spf[:], [spf], [spb])
        for half in range(2):
            pb = PS[half]
            pv = pb.t[:].bitcast(BF16)
            for gg_ in range(8):
                g = half * 8 + gg_
                K.tr(pv[:, gg_ * 128:(gg_ + 1) * 128], spb[:, g, :], ident[:], [spb, ident], [pb], inc=(gg_ == 7))
            K.copy("act", spT[:, half * 8:(half + 1) * 8, :], pv.rearrange("p (g t) -> p g t", g=8), [pb], [spT])
        K.dma("sp", bsf[:], gm_b_sp.ap(), bsf)
        K.copy("dve", bsb[:], bsf[:], [bsf], [bsb])

        def loadh(c):
            if c < 4:
                K.dma("sp", hT[c % 2][:], hTs[:, :, c * 512:(c + 1) * 512].rearrange("k p t -> p k t"), hT[c % 2], hTs)

        def loadg(c):
            if c < 4:
                K.dma("sp", gu[0][:], ggTu[:, :, c * 512:(c + 1) * 512].rearrange("k p t -> p k t"), gu[0], ggTu)
        loadh(0)
        loadg(0)
        S.barrier()
        K.off = mark
        vs = K.sbs("v", [128, D], F32, 2)
        vgs = K.sbs("vg", [128, D], F32, 2)
        junk = K.sb("junk", [128, D], BF16)
        vlns = K.sbs("vln", [128, D], BF16, 2)

        def s1(t):
            c, tt = t // 4, t % 4
            h = hT[c % 2]
            v = vs[t % 2]
            ts_ = slice(tt * 128, (tt + 1) * 128)
            if tt == 0:
                loadh(c + 1)
            for k in range(16):
                for fb in range(4):
                    K.mm(PS[fb][:], h[:, k, ts_], W[:, k, fb * 512:(fb + 1) * 512], k == 0, k == 15, [h, Wp[fb]], [PS[fb]])
            for fb in range(4):
                fs = slice(fb * 512, (fb + 1) * 512)
                K.tt("dve", v[:, fs], PS[fb][:], bvb[:, fs], ALU.add, [PS[fb], bvb], [v])

        def s2(t):
            v, vg, vln, st4 = vs[t % 2], vgs[t % 2], vlns[t % 2], st[t % 2]
            K.act(vg[:], v[:], AF.Gelu_apprx_tanh, [v], [vg, st4], accum=st4[:, 0:1])
            K.act(junk[:], vg[:], AF.Square, [vg], [junk, st4], accum=st4[:, 1:2])
            K.ts("dve", st4[:, 2:3], st4[:, 0:1], 1.0 / D, None, ALU.mult, None, [st4], [st4])
            K.tt("dve", st4[:, 3:4], st4[:, 2:3], st4[:, 2:3], ALU.mult, [st4], [st4])
            K.stt("dve", st4[:, 4:5], st4[:, 1:2], 1.0 / D, st4[:, 3:4], ALU.mult, ALU.subtract, [st4], [st4])
            rstd_from_ssq(st4[:, 4:5], st4[:, 5:6], 1.0, [st4], [st4])
            K.stt("dve", st4[:, 6:7], st4[:, 2:3], -1.0, st4[:, 5:6], ALU.mult, ALU.mult, [st4], [st4])
            K.act(v[:], vg[:], AF.Identity, [vg, st4], [v], scale=st4[:, 5:6], bias=st4[:, 6:7])
            K.tt("dve", vg[:], v[:], vgb[:], ALU.mult, [v, vgb], [vg])
            K.tt("pool", vln[:], vg[:], vbb[:], ALU.add, [vg, vbb], [vln])

        def s3(t):
            c, tt = t // 4, t % 4
            vln = vlns[t % 2]
            G, P = gu[0], pst[0]
            ts_ = slice(tt * 128, (tt + 1) * 128)
            for g in range(16):
                pb = PS[4 + g // 4]
                oc = slice((g % 4) * 128, (g % 4 + 1) * 128)
                K.mm(pb[:, oc], vln[:, g * 128:(g + 1) * 128], spT[:, g, :], True, False, [vln, spT], [pb], inc=False)
                K.mm(pb[:, oc], ones[0:1, :], bsb[0:1, g * 128:(g + 1) * 128], False, True, [ones, bsb], [pb], inc=(g % 4 == 3))
            for gq in range(4):
                K.tt("dve", P[:, gq * 4:(gq + 1) * 4, ts_], PS[4 + gq][:].rearrange("p (g t) -> p g t", g=4),
                     G[:, gq * 4:(gq + 1) * 4, ts_], ALU.mult, [PS[4 + gq], G], [P])
            if tt == 3:
                K.dma("sp", catT[:, :, c * 512:(c + 1) * 512].rearrange("k p t -> p k t"), P[:], catT, P)
                loadg(c + 1)

        for step in range(18):
            if step < 16:
                s1(step)
            if 0 <= step - 1 < 16:
                s2(step - 1)
            if 0 <= step - 2 < 16:
                s3(step - 2)
        S.barrier()

    ggTu = dscr("ggTu", [16, 128, HALF], BF16)

    phases = [
        ("mod", phase_mod),
        ("inproj", lambda: (phase_inproj("qg"), phase_inproj("kvx"))),
        ("attn", phase_attn),
        ("rnn", phase_rnn),
        ("outproj0", lambda: phase_outproj(ar_w_out, 0, xin_b, x1s, hTs)),
        ("up0", lambda: phase_up(w_ff_in[0], DFF, hTs, uT, "relu2")),
        ("down0", lambda: phase_down(w_ff_out[0])),
        ("epi0", lambda: phase_ffn_epi(0, x1s, x2s, True, 6, hTs)),
        ("gmu", lambda: phase_up(gm_w_in, D, hTs, ggTu, "gelu", gm_bu)),
        ("gmv", phase_gm_v),
        ("outproj1", lambda: phase_outproj(gm_w_out, 6, x2s, x1s, hTs)),
        ("up1", lambda: phase_up(w_ff_in[1], DFF, hTs, uT, "relu2")),
        ("down1", lambda: phase_down(w_ff_out[1])),
        ("epi1", lambda: phase_ffn_epi(6, x1s, out, False, 0, None)),
    ]
    for i, (name, fn) in enumerate(phases):
        if only is not None and name not in only:
            continue
        fn()
    S.barrier()
    S.emit()
    return nc


def _rope_tables():
    rows = SEQ // 64
    r_idx, c_idx = np.meshgrid(np.arange(rows), np.arange(64), indexing="ij")
    r_idx = r_idx.reshape(-1).astype(np.float32)
    c_idx = c_idx.reshape(-1).astype(np.float32)
    freqs = (np.float32(10000.0) ** (-np.arange(32, dtype=np.float32) / np.float32(32))).astype(np.float32)
    ang_r = r_idx[:, None] * freqs
    ang_c = c_idx[:, None] * freqs
    ang = np.concatenate([ang_r, ang_r, ang_c, ang_c], axis=1)
    cosT = np.ascontiguousarray(np.cos(ang).T.astype(np.float32))
    sinT = np.ascontiguousarray(np.sin(ang).T.astype(np.float32))
    Pm = np.zeros((128, 128), np.float32)
    for d in range(128):
        blk = d // 32
        if blk % 2 == 0:
            Pm[d, d + 32] = -1.0
        else:
            Pm[d, d - 32] = 1.0
    ropeP = np.ascontiguousarray(Pm.T)
    return cosT, sinT, ropeP


def make_in_maps(inp, cores=range(8)):
    f = lambda a: np.ascontiguousarray(a, dtype=np.float32)
    cosT, sinT, ropeP = _rope_tables()
    shared = {
        "w_mod": f(inp["w_mod"]), "b_mod": f(inp["b_mod"]), "norm_g": f(inp["norm_g"].reshape(2, 4 * D)),
        "w_ff_in": f(inp["w_ff_in"]), "w_ff_out": f(inp["w_ff_out"]),
        "ar_w_in": f(inp["ar_w_in"][0]), "ar_w_out": f(inp["ar_w_out"][0]),
        "qkg": f(np.stack([inp["ar_q_g"][0], inp["ar_k_g"][0]], axis=1)),
        "ropeP": ropeP,
        "convb": f(inp["ar_conv_b"][0].reshape(8, 128).T),
        "gm_w_in": f(inp["gm_w_in"][0]),
        "gm_bu": f(inp["gm_b_in"][0][:D].reshape(16, 128).T),
        "gm_rows": f(np.stack([inp["gm_b_in"][0][D:], inp["gm_v_g"][0], inp["gm_v_b"][0]], axis=0)),
        "gm_w_out": f(inp["gm_w_out"][0]),
    }
    cw = inp["ar_conv_w"][0]
    maps = []
    for core in cores:
        b, half = core // 2, core % 2
        m = dict(shared)
        x = inp["x"][b]
        cx = inp["ctx"][b]
        if half == 0:
            xin = np.concatenate([x[:HALF], x[HALF:], cx], axis=0)
            cs, sn = cosT, sinT
            tp = np.stack([cw[0], cw[1], cw[2], cw[3], np.zeros_like(cw[0])], axis=1)
            dsel = [0, 1]
            wsp = inp["gm_w_sp"][0]
            bsp = inp["gm_b_sp"][0]
        else:
            xr = x[::-1]
            xin = np.concatenate([xr[:HALF], xr[HALF:], cx[::-1]], axis=0)
            cs, sn = cosT[:, ::-1], sinT[:, ::-1]
            tp = np.stack([np.zeros_like(cw[0]), cw[3], cw[2], cw[1], cw[0]], axis=1)
            dsel = [1, 0]
            wsp = inp["gm_w_sp"][0][:, ::-1, ::-1]
            bsp = inp["gm_b_sp"][0][:, ::-1]
        m["xin"] = f(xin)
        m["cosT"] = f(cs)
        m["sinT"] = f(sn)
        m["taps"] = f(tp.reshape(8, 128, 5).transpose(1, 0, 2))
        m["cvec"] = f(np.stack([inp["c"][b].reshape(16, 128).T, inp["c_ctx"].reshape(16, 128).T], axis=2))
        m["ar_wa"] = f(inp["ar_wa"][0][dsel])
        m["ar_wx"] = f(inp["ar_wx"][0][dsel])
        pr = np.stack([inp["ar_ba"][0][dsel], inp["ar_bx"][0][dsel], inp["ar_lambda"][0][dsel]], axis=0)
        m["rnnp"] = f(pr.reshape(3, 2, 8, 128).transpose(3, 0, 1, 2))
        m["gm_w_sp"] = f(wsp)
        m["gm_b_sp"] = f(bsp.reshape(1, D))
        maps.append(m)
    return maps


_NC_CACHE = {}


def kernel(**inputs):
    inp = {k: np.asarray(v) for k, v in inputs.items()}
    if "nc" not in _NC_CACHE:
        _NC_CACHE["nc"] = build()
    nc = _NC_CACHE["nc"]
    maps = make_in_maps(inp)
    res = run_bass_kernel_spmd(nc, maps, core_ids=list(range(8)))
    out = np.empty((4, SEQ, D), np.float32)
    for core in range(8):
        b, half = core // 2, core % 2
        o = np.asarray(res.results[core]["out"], dtype=np.float32)
        if half == 0:
            out[b, :HALF] = o
        else:
            out[b, HALF:] = o[::-1]
    return out
```
